# Optimizing a Trainium2 kernel written in Bass

```python
import jax, jax.numpy as jnp
from jax import lax
import numpy as np

D_MODEL = 1024
BATCH = 16
SEQ = 2048
DEPTH = 2

N_MIXERS = 2
FOURIER_GROUPS = 4
RWKV_HEAD_SIZE = 64
RWKV_HEADS = D_MODEL // RWKV_HEAD_SIZE
DECAY_LORA = 64
AAA_LORA = 64
GATE_LORA = 128
N_DIRS = 2
N_SHIFT_MIX = 6
D_FF = ((8 * D_MODEL + 3 * 256 - 1) // (3 * 256)) * 256
N_FOURIER_LAYERS = (DEPTH + 1) // 2
N_RWKV_LAYERS = DEPTH // 2
RMS_EPS = 1e-6
GN_EPS = RWKV_HEAD_SIZE * 1e-5

kernel_name = "fnet_rwkv7_hybrid_encoder"


def rmsnorm(x, g):
    xf = x.astype(jnp.float32)
    y = xf * lax.rsqrt(jnp.mean(xf * xf, axis=-1, keepdims=True) + RMS_EPS)
    return (y * g.astype(jnp.float32)).astype(x.dtype)


def fourier_mix(h, w_out):
    b, s, d = h.shape
    hg = h.reshape(b, s, FOURIER_GROUPS, d // FOURIER_GROUPS).astype(jnp.float32)
    f = jnp.fft.fftn(hg, axes=(1, 3), norm="ortho").real
    return f.reshape(b, s, d).astype(h.dtype) @ w_out


def split_heads(t):
    return t.reshape(t.shape[:-1] + (t.shape[-1] // RWKV_HEAD_SIZE, RWKV_HEAD_SIZE))


def wkv7_scan(r, w, k, v, kk, a, reverse):
    b, s, hh, n = r.shape
    xs = tuple(jnp.moveaxis(t, 1, 0) for t in (r, w, k, v, kk, a))

    def step(state, inp):
        r_t, w_t, k_t, v_t, kk_t, a_t = inp
        sk = jnp.einsum('bhvk,bhk->bhv', state, kk_t)
        state = (state * w_t[:, :, None, :]
                 - sk[..., None] * (kk_t * a_t)[:, :, None, :]
                 + v_t[..., None] * k_t[:, :, None, :])
        y_t = jnp.einsum('bhvk,bhk->bhv', state, r_t)
        return state, y_t

    state0 = jnp.zeros((b, hh, n, n), jnp.float32)
    _, ys = lax.scan(step, state0, xs, reverse=reverse)
    return jnp.moveaxis(ys, 0, 1)


def rwkv7_mix(h, mu, w_rkv, w_o, w0, w1, w2, a0, a1, a2, g1, g2, k_k, k_a, r_k, ln_w, ln_b):
    f32 = jnp.float32
    b, s, d = h.shape
    hp = jnp.pad(h, ((0, 0), (1, 1), (0, 0)))
    xx = 0.5 * (hp[:, :-2] + hp[:, 2:]) - h
    xs = h[None] + xx[None] * mu[:, None, None, :]
    rkv = jnp.einsum('cbsd,cde->cbse', xs[:3], w_rkv)
    r, k, v = rkv[0], rkv[1], rkv[2]
    xw, xa, xg = xs[3], xs[4], xs[5]
    w_lora = jnp.einsum('jbsr,jre->jbse', jnp.tanh(jnp.einsum('bsd,jdr->jbsr', xw, w1)), w2)
    w_log = -jax.nn.softplus(-(w0[:, None, None, :] + w_lora).astype(f32)) - 0.5
    decay = jnp.exp(-jnp.exp(w_log))
    a_lora = jnp.einsum('jbsr,jre->jbse', jnp.einsum('bsd,jdr->jbsr', xa, a1), a2)
    a = jax.nn.sigmoid((a0[:, None, None, :] + a_lora).astype(f32))
    g = jax.nn.sigmoid(xg @ g1) @ g2
    rf = split_heads(r.astype(f32))
    vf = split_heads(v.astype(f32))
    kf = k.astype(f32)
    kk = split_heads(kf * k_k.astype(f32))
    kk = kk * lax.rsqrt(jnp.maximum(jnp.sum(kk * kk, axis=-1, keepdims=True), 1e-24))
    k_dir = split_heads(kf[None] * (1.0 + (a - 1.0) * k_a.astype(f32)))
    a_h = split_heads(a)
    decay_h = split_heads(decay)
    y = (wkv7_scan(rf, decay_h[0], k_dir[0], vf, kk, a_h[0], False)
         + wkv7_scan(rf, decay_h[1], k_dir[1], vf, kk, a_h[1], True))
    mean = jnp.mean(y, axis=-1, keepdims=True)
    var = jnp.mean(jnp.square(y - mean), axis=-1, keepdims=True)
    yn = ((y - mean) * lax.rsqrt(var + GN_EPS)).reshape(b, s, d) * ln_w.astype(f32) + ln_b.astype(f32)
    bonus = jnp.sum(jnp.sum(rf[None] * k_dir * r_k.astype(f32), axis=-1, keepdims=True), axis=0) * vf
    out = (yn + bonus.reshape(b, s, d)).astype(h.dtype) * g
    return out @ w_o


def swiglu(h, w_gate, w_up, w_down):
    return (jax.nn.silu(h @ w_gate) * (h @ w_up)) @ w_down


def setup_inputs(seed: int = 0) -> dict:
    key = jax.random.key(seed)
    ks = jax.random.split(key, 26)
    f32 = jnp.float32
    D, F, NF, NR = D_MODEL, D_FF, N_FOURIER_LAYERS, N_RWKV_LAYERS

    def nrm(k, shape, scale):
        return jax.random.normal(k, shape, f32) * scale

    return {
        "x": nrm(ks[0], (BATCH, SEQ, D), 1.0),
        "norm_mix_g": 1.0 + nrm(ks[1], (DEPTH, D), 0.05),
        "norm_ffn_g": 1.0 + nrm(ks[2], (DEPTH, D), 0.05),
        "norm_final_g": 1.0 + nrm(ks[3], (D,), 0.05),
        "fno_w_out": nrm(ks[4], (NF, D, D), D ** -0.5),
        "rwkv_mu": jax.random.uniform(ks[5], (NR, N_SHIFT_MIX, D), f32),
        "rwkv_w_rkv": nrm(ks[6], (NR, 3, D, D), D ** -0.5),
        "rwkv_w_o": nrm(ks[7], (NR, D, D), D ** -0.5),
        "rwkv_w0": jax.random.uniform(ks[8], (NR, N_DIRS, D), f32, -4.0, 1.0),
        "rwkv_w1": nrm(ks[9], (NR, N_DIRS, D, DECAY_LORA), D ** -0.5),
        "rwkv_w2": nrm(ks[10], (NR, N_DIRS, DECAY_LORA, D), 0.3 * DECAY_LORA ** -0.5),
        "rwkv_a0": nrm(ks[11], (NR, N_DIRS, D), 0.3),
        "rwkv_a1": nrm(ks[12], (NR, N_DIRS, D, AAA_LORA), D ** -0.5),
        "rwkv_a2": nrm(ks[13], (NR, N_DIRS, AAA_LORA, D), 0.3 * AAA_LORA ** -0.5),
        "rwkv_g1": nrm(ks[14], (NR, D, GATE_LORA), D ** -0.5),
        "rwkv_g2": nrm(ks[15], (NR, GATE_LORA, D), GATE_LORA ** -0.5),
        "rwkv_k_k": 0.85 + nrm(ks[16], (NR, D), 0.05),
        "rwkv_k_a": 1.0 + nrm(ks[17], (NR, D), 0.05),
        "rwkv_r_k": nrm(ks[18], (NR, RWKV_HEADS, RWKV_HEAD_SIZE), 0.1),
        "rwkv_ln_w": 1.0 + nrm(ks[19], (NR, D), 0.05),
        "rwkv_ln_b": nrm(ks[20], (NR, D), 0.01),
        "ffn_w_gate": nrm(ks[21], (DEPTH, D, F), D ** -0.5),
        "ffn_w_up": nrm(ks[22], (DEPTH, D, F), D ** -0.5),
        "ffn_w_down": nrm(ks[23], (DEPTH, F, D), F ** -0.5),
    }


def reference(x, norm_mix_g, norm_ffn_g, norm_final_g, fno_w_out, rwkv_mu, rwkv_w_rkv, rwkv_w_o,
              rwkv_w0, rwkv_w1, rwkv_w2, rwkv_a0, rwkv_a1, rwkv_a2, rwkv_g1, rwkv_g2,
              rwkv_k_k, rwkv_k_a, rwkv_r_k, rwkv_ln_w, rwkv_ln_b, ffn_w_gate, ffn_w_up, ffn_w_down):
    for i in range(DEPTH):
        h = rmsnorm(x, norm_mix_g[i])
        j = i // N_MIXERS
        if i % N_MIXERS == 0:
            x = x + fourier_mix(h, fno_w_out[j])
        else:
            x = x + rwkv7_mix(h, rwkv_mu[j], rwkv_w_rkv[j], rwkv_w_o[j],
                              rwkv_w0[j], rwkv_w1[j], rwkv_w2[j],
                              rwkv_a0[j], rwkv_a1[j], rwkv_a2[j],
                              rwkv_g1[j], rwkv_g2[j], rwkv_k_k[j], rwkv_k_a[j],
                              rwkv_r_k[j], rwkv_ln_w[j], rwkv_ln_b[j])
        h = rmsnorm(x, norm_ffn_g[i])
        x = x + swiglu(h, ffn_w_gate[i], ffn_w_up[i], ffn_w_down[i])
    return rmsnorm(x, norm_final_g)
```

```python
import numpy as np
import ml_dtypes
from contextlib import ExitStack
import concourse.bass as bass
import concourse.mybir as mybir
from concourse.bass_utils import run_bass_kernel_spmd

ENG_EPOCH = 20000


class Buf:
    __slots__ = ("name", "w", "r")

    def __init__(self, name):
        self.name = name
        self.w = []
        self.r = []


class Op:
    __slots__ = ("eng", "emit", "deps", "sig", "is_dma", "semkey", "ndma", "tok", "idx")


class Prog:
    def __init__(self, nc, stack):
        self.nc = nc
        self.stack = stack
        self.ops = []
        self.engs = {"pe": nc.tensor, "dve": nc.vector, "act": nc.scalar, "pool": nc.gpsimd, "sp": nc.sync}
        self.last_dma = {}

    def _record(self, op, reads, writes):
        deps = []
        raw = set()
        for b in reads:
            deps.extend(b.w)
            for d in b.w:
                raw.add(id(d))
        for b in writes:
            deps.extend(b.w)
            deps.extend(b.r)
        out = []
        seen = set()
        for d in deps:
            if id(d) in seen or d is op:
                continue
            seen.add(id(d))
            if (not d.is_dma) and d.eng == op.eng and not op.is_dma:
                if op.eng == "pe":
                    continue
            out.append(d)
        op.deps = out
        for d in out:
            d.sig = True
        for b in reads:
            b.r.append(op)
        for b in writes:
            if b.r:
                b.w = [op]
            else:
                b.w = b.w + [op]
            b.r = []
        op.idx = len(self.ops)
        self.ops.append(op)
        return op

    def op(self, eng, emit, reads=(), writes=()):
        o = Op()
        o.eng = eng; o.emit = emit; o.sig = False; o.is_dma = False
        o.semkey = None; o.ndma = 0; o.tok = None
        return self._record(o, list(reads), list(writes))

    def dma(self, eng, emits, reads=(), writes=(), semkey=None):
        o = Op()
        o.eng = eng; o.emit = emits; o.sig = True; o.is_dma = True
        o.semkey = semkey; o.ndma = len(emits); o.tok = None
        prev = self.last_dma.get(semkey)
        self._record(o, list(reads), list(writes))
        if prev is not None and prev not in o.deps:
            o.deps.append(prev)
        self.last_dma[semkey] = o
        return o

    def barrier(self, bufs=()):
        last = {}
        for o in self.ops:
            if o.emit is None:
                continue
            key = ("dma", o.semkey) if o.is_dma else ("eng", o.eng)
            last[key] = o
        deps = list(last.values())
        for e in ("pe", "dve", "act", "pool", "sp"):
            o = Op()
            o.eng = e; o.emit = None; o.sig = False; o.is_dma = False
            o.semkey = None; o.ndma = 0; o.tok = None
            o.deps = [d for d in deps]
            for d in o.deps:
                d.sig = True
            o.idx = len(self.ops)
            self.ops.append(o)

    def finalize(self):
        nc = self.nc
        eng_cnt = {}
        eng_sems = {}
        dma_sems = {}
        dma_cnt = {}

        def eng_sem(e, epoch):
            k = (e, epoch)
            if k not in eng_sems:
                eng_sems[k] = self.stack.enter_context(nc.semaphore("s_%s_%d" % (e, epoch)))
            return eng_sems[k]

        for o in self.ops:
            if o.is_dma:
                if o.semkey not in dma_sems:
                    dma_sems[o.semkey] = self.stack.enter_context(nc.semaphore("d_%s" % (o.semkey,)))
                    dma_cnt[o.semkey] = 0
                dma_cnt[o.semkey] += 16 * o.ndma
                o.tok = (dma_sems[o.semkey], dma_cnt[o.semkey], ("d", o.semkey))
            elif o.sig:
                c = eng_cnt.get(o.eng, 0) + 1
                eng_cnt[o.eng] = c
                epoch = (c - 1) // ENG_EPOCH
                o.tok = (eng_sem(o.eng, epoch), c - epoch * ENG_EPOCH, ("e", o.eng, epoch))
        known = {e: {} for e in self.engs}
        nwait = 0
        for o in self.ops:
            E = self.engs[o.eng]
            kn = known[o.eng]
            need = {}
            for d in o.deps:
                sem, val, key = d.tok
                if kn.get(key, 0) >= val:
                    continue
                if key not in need or need[key][1] < val:
                    need[key] = (sem, val)
            for key, (sem, val) in need.items():
                E.wait_ge(sem, val)
                kn[key] = val
                nwait += 1
            if o.emit is None:
                continue
            if o.is_dma:
                sem = o.tok[0]
                for f in o.emit:
                    f(E).then_inc(sem, 16)
            else:
                ins = o.emit(E)
                if o.sig:
                    ins.then_inc(o.tok[0], 1)
        self.stats = dict(nops=len(self.ops), nwait=nwait, nsem=len(eng_sems) + len(dma_sems))
        return self.stats


class Arena:
    def __init__(self, base_ap, words):
        self.base = base_ap
        self.words = words
        self.top = 0

    def mark(self):
        return self.top

    def release(self, m):
        self.top = m

    def alloc(self, nelem, dtype, parts=128):
        bpe = 2 if dtype == mybir.dt.bfloat16 else 4
        nw = (nelem * bpe + 3) // 4
        nw = (nw + 7) // 8 * 8
        assert self.top + nw <= self.words, "SBUF arena overflow: %d + %d > %d" % (self.top, nw, self.words)
        ap = self.base[0:parts, self.top:self.top + nw]
        self.top += nw
        if dtype != mybir.dt.float32:
            ap = ap.bitcast(dtype)
        return ap[:, 0:nelem]


F32 = mybir.dt.float32
BF16 = mybir.dt.bfloat16
ALU = mybir.AluOpType
AF = mybir.ActivationFunctionType
AX = mybir.AxisListType

D = 1024; KD = 8; T = 4096; S = 2048
TP = 128
NBLK = T // 128
CDEC = float(np.exp(-0.5))
DBG_LIMIT = None
DBG_ITERS = None
ROWS = ["k_k", "k_a", "w0_0", "w0_1", "a0_0", "a0_1", "r_k", "ln_w", "ln_b"]
RI = {n: i for i, n in enumerate(ROWS)}
SCR_F32 = ["s_bonus", "s_g", "s_y0", "s_y1", "s_r32", "s_k32", "s_v32"]
SCR_TOK = ["s_v", "s_ka0", "s_ka1", "s_kh0", "s_kh1", "s_bh0", "s_bh1"]
SCR_CH = ["c_kt0", "c_kt1", "c_rt0", "c_rt1", "c_ktl0", "c_ktl1", "c_bt0", "c_bt1"]


def declare(ctx_dr, drh, nc, din, debug):
    din("w_rkv", [3, D, D]); din("w_o", [D, D])
    din("w1", [2, D, 64]); din("w2", [2, 64, D]); din("a1", [2, D, 64]); din("a2", [2, 64, D])
    din("g1", [D, 128]); din("g2", [128, D]); din("rows", [len(ROWS), D]); din("ident", [128, 128], BF16)
    din("cmask", [128, 2, 5, 128]); din("ctri", [128, 2, 3, 128], BF16); din("cind", [128, 2], BF16); din("cid2", [128, 64])
    for n in SCR_F32:
        h = nc.dram_tensor(n, [T, D], F32, kind=("ExternalOutput" if debug else "Internal")); drh[n] = h; ctx_dr[n] = h.ap()
    for n in SCR_TOK:
        h = nc.dram_tensor(n, [T, D], BF16, kind="Internal"); drh[n] = h; ctx_dr[n] = h.ap()
    for n in SCR_CH:
        h = nc.dram_tensor(n, [D, T], BF16, kind="Internal"); drh[n] = h; ctx_dr[n] = h.ap()
    h = nc.dram_tensor("c_lt", [3, 128, T], BF16, kind="Internal"); drh["c_lt"] = h; ctx_dr["c_lt"] = h.ap()


def rwkv_consts():
    import ml_dtypes
    s = np.arange(128)[:, None]; t = np.arange(128)[None, :]
    same = (s // 64) == (t // 64)
    cmask = np.zeros((128, 2, 5, 128), np.float32)
    ctri = np.zeros((128, 2, 3, 128), np.float32)
    for d in range(2):
        rs = (s < t) if d == 0 else (s > t)
        ri = (s <= t) if d == 0 else (s >= t)
        ro = (s > t) if d == 0 else (s < t)
        cmask[:, d, 0, :] = -1.0 * (rs & same); cmask[:, d, 1, :] = (ri & same)
        cmask[:, d, 2, :] = (rs & same); cmask[:, d, 3, :] = (ri & same)
        cmask[:, d, 4, :] = -1.0 * ((rs & same).T)
        ctri[:, d, 0, :] = (ri & same); ctri[:, d, 1, :] = (rs & same); ctri[:, d, 2, :] = (ro & same)
    cind = np.zeros((128, 2), np.float32); cind[:64, 0] = 1; cind[64:, 1] = 1
    cid2 = np.zeros((128, 64), np.float32); cid2[np.arange(128), np.arange(128) % 64] = 1
    return dict(cmask=cmask, ctri=ctri.astype(ml_dtypes.bfloat16), cind=cind.astype(ml_dtypes.bfloat16), cid2=cid2)


def phase_rwkv(c, src, dst, sub=("prep", "scan", "post")):
    P = c.P; A = c.A; dr = c.dr; dbuf = c.dbuf; getbank = c.getbank; vcol = c.vcol
    vecs_b = c.vecs_b; ones_b = c.ones_b; onesD = c.onesD; drh = c.drh
    for n in SCR_F32 + SCR_TOK + SCR_CH + ["c_lt"]:
        if n not in dbuf:
            dbuf[n] = Buf("dram_" + n)
    sv = dr[src].rearrange("(k p) t -> p k t", p=128)
    dv = dr[dst].rearrange("(k p) t -> p k t", p=128)

    def alloc3(k, n, dt):
        return A.alloc(k * n, dt).rearrange("p (k n) -> p k n", k=k)

    def load_rows(names):
        out = {}
        for n in names:
            ap = A.alloc(D, F32); b = Buf("row_" + n)
            P.dma("sp", [lambda e, ap=ap, n=n: e.dma_start(out=ap, in_=dr["rows"][RI[n]].partition_broadcast(128))],
                  writes=[b], semkey="row_" + n)
            out[n] = (ap, b)
        return out

    def h3(ap):
        return ap.rearrange("p (h n) -> p h n", h=16)

    def cload(name, nelem, dt, shape_str=None, **kw):
        ap = A.alloc(nelem, dt); b = Buf("c_" + name)
        src_ap = dr[name]
        P.dma("sp", [lambda e: e.dma_start(out=ap, in_=src_ap.rearrange(shape_str, **kw) if shape_str else src_ap)], writes=[b], semkey="c_" + name)
        return ap, b

    PC = [A.alloc(8 * 64, F32).rearrange("p (j c) -> p j c", j=8) for _ in range(2)]
    PC_b = [Buf("pc0"), Buf("pc1")]
    ident = A.alloc(128, BF16); ident_b = Buf("ident")
    P.dma("sp", [lambda e: e.dma_start(out=ident, in_=dr["ident"])], writes=[ident_b], semkey="ident")

    def prep():
        m = A.mark()
        stages = [(A.alloc(D, F32), Buf("wst%d" % i)) for i in range(2)]
        srr = [0]

        def stage_cast(src_ap, dst_ap, n_, dst_buf, view=None):
            i = srr[0] % 2; srr[0] += 1
            stg, stg_b = stages[i]
            sview = stg[:, 0:n_] if view is None else view(stg[:, 0:n_])
            wb = dst_buf if isinstance(dst_buf, list) else [dst_buf]
            P.dma("sp", [lambda e: e.dma_start(out=sview, in_=src_ap)], writes=[stg_b], semkey="wst%d" % i)
            if (srr[0] // 2) % 2 == 0:
                P.op("dve", lambda e: e.tensor_copy(dst_ap, sview), reads=[stg_b], writes=wb)
            else:
                P.op("act", lambda e: e.copy(dst_ap, sview), reads=[stg_b], writes=wb)

        def load_w(w2d, K, N, tag, npart=128):
            dstw = A.alloc(K * N, BF16).rearrange("p (k n) -> p k n", k=K)
            bufs = [Buf("%s_%d" % (tag, k)) for k in range(K)]
            wv = w2d.rearrange("(k p) n -> p k n", p=npart)
            for k in range(K):
                stage_cast(wv[:, k, :], dstw[:, k, :], N, bufs[k])
            return dstw, bufs
        Wr, Wr_b = load_w(dr["w_rkv"][0], KD, D, "wr")
        Wk, Wk_b = load_w(dr["w_rkv"][1], KD, D, "wk")
        Wv, Wv_b = load_w(dr["w_rkv"][2], KD, D, "wv")
        w1c = A.alloc(KD * 128, BF16).rearrange("p (k n) -> p k n", k=KD); w1c_b = [Buf("w1c%d" % k) for k in range(KD)]
        a1c = A.alloc(KD * 128, BF16).rearrange("p (k n) -> p k n", k=KD); a1c_b = [Buf("a1c%d" % k) for k in range(KD)]
        for (dstw, bufs, name) in ((w1c, w1c_b, "w1"), (a1c, a1c_b, "a1")):
            for j in range(2):
                wv = dr[name][j].rearrange("(k p) n -> p k n", p=128)
                stage_cast(wv, dstw[:, :, j * 64:(j + 1) * 64], 512, bufs, view=lambda ap: ap.rearrange("p (k n) -> p k n", k=KD))
        G1 = A.alloc(KD * 128, BF16).rearrange("p (k n) -> p k n", k=KD); G1_b = [Buf("g1_%d" % k) for k in range(KD)]
        stage_cast(dr["g1"].rearrange("(k p) n -> p k n", p=128), G1, 1024, G1_b, view=lambda ap: ap.rearrange("p (k n) -> p k n", k=KD))
        NH = TP + 2
        FS = []
        for i in range(3):
            FS.append({"xt": (alloc3(KD, NH, F32), Buf("xt%d" % i)), "hf": (alloc3(KD, NH, F32), Buf("hf%d" % i)),
                       "sq": (alloc3(KD, NH, BF16), Buf("sq%d" % i)), "rstd": (A.alloc(NH, F32), Buf("rstd%d" % i)),
                       "xx": (alloc3(KD, TP, F32), Buf("xx%d" % i)), "tmp": (alloc3(KD, TP, F32), Buf("tmp%d" % i)),
                       "xs": [alloc3(KD, TP, BF16) for _ in range(6)], "xs_b": [Buf("xs%d_%d" % (i, c_)) for c_ in range(6)]})
        W2 = [{n: (A.alloc(D, F32), Buf("%s_%d" % (n, i))) for n in ("tr", "tk", "tv")} for i in range(3)]
        LT = [{n: (A.alloc(TP, BF16), Buf("%s_%d" % (n, i))) for n in ("twT", "taT", "sgT")} for i in range(3)]

        def fe(ti):
            t0 = ti * TP
            ss_ = ti % 3
            F_ = FS[ss_]
            xt, xt_b = F_["xt"]; hf, hf_b = F_["hf"]; sq, sq_b = F_["sq"]; rstd, rstd_b = F_["rstd"]
            xx, xx_b = F_["xx"]; tmp, tmp_b = F_["tmp"]; xs = F_["xs"]; xs_b = F_["xs_b"]
            first = (t0 % S == 0); last = ((t0 + TP) % S == 0)
            lo_ = 1 if first else 0; hi_ = NH - 1 if last else NH
            P.dma("sp", [lambda e, t0=t0, lo_=lo_, hi_=hi_: e.dma_start(out=xt[:, :, lo_:hi_], in_=sv[:, :, t0 - 1 + lo_:t0 - 1 + hi_])],
                  reads=[dbuf[src]], writes=[xt_b], semkey="xt%d" % ss_)
            if first:
                P.op("pool", lambda e: e.memset(xt[:, :, 0:1], 0.0), writes=[xt_b])
            if last:
                P.op("pool", lambda e: e.memset(xt[:, :, NH - 1:NH], 0.0), writes=[xt_b])
            P.op("act", lambda e: e.activation(out=sq, in_=xt, func=AF.Square), reads=[xt_b], writes=[sq_b])
            bk, bk_b = getbank()

            def mm(e, bk=bk):
                ins = None
                for k in range(KD):
                    ins = e.matmul(bk[:, 0:NH], onesD[:, :], sq[:, k, :], start=(k == 0), stop=(k == KD - 1))
                return ins
            P.op("pe", mm, reads=[sq_b, ones_b], writes=[bk_b])
            P.op("act", lambda e, bk=bk: e.activation(out=rstd, in_=bk[:, 0:NH], func=AF.Sqrt, bias=1e-6, scale=1.0),
                 reads=[bk_b], writes=[rstd_b])
            P.op("dve", lambda e: e.reciprocal(rstd, rstd), reads=[rstd_b], writes=[rstd_b])
            yield
            for k in range(KD):
                P.op("dve", lambda e, k=k: e.scalar_tensor_tensor(out=hf[:, k, :], in0=xt[:, k, :], scalar=vcol("nm1", k),
                                                                  in1=rstd, op0=ALU.mult, op1=ALU.mult),
                     reads=[xt_b, rstd_b, vecs_b], writes=[hf_b])
            P.op("pool", lambda e: e.tensor_tensor(out=tmp, in0=hf[:, :, 0:TP], in1=hf[:, :, 2:TP + 2], op=ALU.add),
                 reads=[hf_b], writes=[tmp_b])
            P.op("dve", lambda e: e.scalar_tensor_tensor(out=xx, in0=tmp, scalar=0.5, in1=hf[:, :, 1:TP + 1],
                                                         op0=ALU.mult, op1=ALU.subtract),
                 reads=[tmp_b, hf_b], writes=[xx_b])
            yield
            for ci in (3, 4, 5, 0, 1, 2):
                for k in range(KD):
                    P.op("dve", lambda e, ci=ci, k=k: e.scalar_tensor_tensor(
                        out=xs[ci][:, k, :], in0=xx[:, k, :], scalar=vcol("mu%d" % ci, k), in1=hf[:, k, 1:TP + 1],
                        op0=ALU.mult, op1=ALU.add), reads=[xx_b, hf_b, vecs_b], writes=[xs_b[ci]])
                yield
            for (wc, wc_b, xi, oname, func) in ((w1c, w1c_b, 3, "twT", AF.Tanh), (a1c, a1c_b, 4, "taT", None), (G1, G1_b, 5, "sgT", AF.Sigmoid)):
                outT, outT_b = LT[ss_][oname]
                bk, bk_b = getbank()

                def mm(e, bk=bk, wc=wc, xi=xi):
                    ins = None
                    for k in range(KD):
                        ins = e.matmul(bk[:, 0:TP], wc[:, k, :], xs[xi][:, k, :], start=(k == 0), stop=(k == KD - 1))
                    return ins
                P.op("pe", mm, reads=[xs_b[xi]] + wc_b, writes=[bk_b])
                if func is None:
                    P.op("act", lambda e, bk=bk, outT=outT: e.copy(outT, bk[:, 0:TP]), reads=[bk_b], writes=[outT_b])
                else:
                    P.op("act", lambda e, bk=bk, outT=outT, func=func: e.activation(out=outT, in_=bk[:, 0:TP], func=func),
                         reads=[bk_b], writes=[outT_b])
            yield
            for (xi, Wm, Wm_b, tile) in ((0, Wr, Wr_b, "tr"), (1, Wk, Wk_b, "tk"), (2, Wv, Wv_b, "tv")):
                ap, b = W2[ss_][tile]
                for half in range(2):
                    bk, bk_b = getbank()

                    def mm(e, bk=bk, half=half, xi=xi, Wm=Wm):
                        ins = None
                        for k in range(KD):
                            ins = e.matmul(bk[:, :], xs[xi][:, k, :], Wm[:, k, half * 512:(half + 1) * 512],
                                           start=(k == 0), stop=(k == KD - 1))
                        return ins
                    P.op("pe", mm, reads=[xs_b[xi]] + Wm_b, writes=[bk_b])
                    P.op("act", lambda e, bk=bk, half=half, ap=ap: e.copy(ap[:, half * 512:(half + 1) * 512], bk[:, :]),
                         reads=[bk_b], writes=[b])
                yield
            for (tile, scr) in (("tr", "s_r32"), ("tk", "s_k32"), ("tv", "s_v32")):
                ap, b = W2[ss_][tile]
                P.dma("act", [lambda e, ap=ap, scr=scr, t0=t0: e.dma_start(out=dr[scr][t0:t0 + 128, :], in_=ap)],
                      reads=[b], writes=[dbuf[scr]], semkey="fst_%s_%d" % (tile, ss_))
            for li, oname in enumerate(("twT", "taT", "sgT")):
                ap, b = LT[ss_][oname]
                P.dma("act", [lambda e, ap=ap, li=li, t0=t0: e.dma_start(out=dr["c_lt"][li, :, t0:t0 + 128], in_=ap)],
                      reads=[b], writes=[dbuf["c_lt"]], semkey="fst_%s_%d" % (oname, ss_))
            yield


        def drive_window(make_gen, n, width):
            active = []; nxt = 0
            while nxt < n or active:
                while len(active) < width and nxt < n:
                    active.append(make_gen(nxt)); nxt += 1
                for g in list(active):
                    try:
                        next(g)
                    except StopIteration:
                        active.remove(g)

        nb_ = NBLK if DBG_LIMIT is None else DBG_LIMIT
        drive_window(fe, nb_, 3)
        P.barrier()
        A.release(m)

        m = A.mark()
        stages = [(A.alloc(D, F32), Buf("wsu%d" % i)) for i in range(2)]
        srr[0] = 0
        w2c = A.alloc(D, BF16); w2c_b = Buf("w2c")
        a2c = A.alloc(D, BF16); a2c_b = Buf("a2c")
        G2 = A.alloc(D, BF16); G2_b = Buf("g2")
        for (dstw, b, srcap) in ((w2c, w2c_b, dr["w2"].rearrange("j r n -> (j r) n")),
                                 (a2c, a2c_b, dr["a2"].rearrange("j r n -> (j r) n")), (G2, G2_b, dr["g2"])):
            stage_cast(srcap, dstw, D, b)
        rows = load_rows(["k_k", "k_a", "w0_0", "w0_1", "a0_0", "a0_1", "r_k"])
        ctri_ap, ctri_b = cload("ctri", 2 * 3 * 128, BF16, "p d m t -> p (d m t)")
        ctri = ctri_ap.rearrange("p (d m t) -> p d m t", d=2, m=3)
        cind, cind_b = cload("cind", 2, BF16)
        NCH = 2
        CS = []
        for i in range(NCH):
            names = ["tr", "tk", "tv", "tkap", "ta", "tw", "tx", "ty", "te0", "te1"]
            CS.append({"W": {n: (A.alloc(D, F32), Buf("%s_c%d" % (n, i))) for n in names},
                       "twT": (A.alloc(TP, BF16), Buf("twT_c%d" % i)), "taT": (A.alloc(TP, BF16), Buf("taT_c%d" % i)),
                       "sgT": (A.alloc(TP, BF16), Buf("sgT_c%d" % i)),
                       "hi": (A.alloc(D, BF16), Buf("hi_c%d" % i)), "lo": (A.alloc(D, BF16), Buf("lo_c%d" % i)),
                       "ss": (A.alloc(16, F32), Buf("ss_c%d" % i)),
                       "rk": [A.alloc(16, F32) for _ in range(2)], "rk_b": [Buf("rk0_c%d" % i), Buf("rk1_c%d" % i)], "terr": [0]})
        NOB = 8; NOC = 6
        OB = [(A.alloc(D, BF16), Buf("ob%d" % i)) for i in range(NOB)]
        OC = [(A.alloc(D, BF16), Buf("oc%d" % i)) for i in range(NOC)]
        rrc = {"ob": 0, "oc": 0}

        def be(ti):
            tb0 = ti * TP
            ss_ = ti % NCH
            C_ = CS[ss_]
            W = C_["W"]
            tr, tr_b = W["tr"]; tk, tk_b = W["tk"]; tv, tv_b = W["tv"]
            twT, twT_b = C_["twT"]; taT, taT_b = C_["taT"]; sgT, sgT_b = C_["sgT"]
            hi, hi_b = C_["hi"]; lo, lo_b = C_["lo"]; ss, ss_b = C_["ss"]; rk = C_["rk"]; rk_b = C_["rk_b"]
            Wl = W
            for (tile, scr) in (("tr", "s_r32"), ("tk", "s_k32"), ("tv", "s_v32")):
                ap, b = W[tile]
                P.dma("sp", [lambda e, ap=ap, scr=scr, tb0=tb0: e.dma_start(out=ap, in_=dr[scr][tb0:tb0 + 128, :])],
                      reads=[dbuf[scr]], writes=[b], semkey="bld_%s_%d" % (tile, ss_))
            for li, (ap, b) in enumerate(((twT, twT_b), (taT, taT_b), (sgT, sgT_b))):
                P.dma("sp", [lambda e, ap=ap, li=li, tb0=tb0: e.dma_start(out=ap, in_=dr["c_lt"][li, :, tb0:tb0 + 128])],
                      reads=[dbuf["c_lt"]], writes=[b], semkey="bld_lt%d_%d" % (li, ss_))
            yield

            def st(ap, b, scr, key):
                P.dma("act", [lambda e, ap=ap, scr=scr, tb0=tb0: e.dma_start(out=dr[scr][tb0:tb0 + 128, :], in_=ap)],
                      reads=[b], writes=[dbuf[scr]], semkey="st_" + key)

            def getob():
                i = rrc["ob"] % NOB; rrc["ob"] += 1
                return OB[i] + ("ob%d" % i,)

            def emit_tok(srcname, te, te_b, scr):
                ap, b, key = getob()
                s_ap, s_b = Wl[srcname]
                eng = "pool" if rrc["ob"] % 4 == 0 else "dve"
                P.op(eng, lambda e, ap=ap, s_ap=s_ap, te=te: e.tensor_tensor(out=ap, in0=s_ap, in1=te, op=ALU.mult),
                     reads=[s_b, te_b], writes=[b])
                st(ap, b, scr, key)

            def emit_ch(srcname, te, te_b, scr):
                ap, b, key = getob()
                s_ap, s_b = Wl[srcname]
                eng = "pool" if rrc["ob"] % 4 == 0 else "dve"
                P.op(eng, lambda e, ap=ap, s_ap=s_ap, te=te: e.tensor_tensor(out=ap, in0=s_ap, in1=te, op=ALU.mult),
                     reads=[s_b, te_b], writes=[b])
                bk, bk_b = getbank()
                bkb = bk[:, :].bitcast(BF16)

                def trp(e, bkb=bkb, ap=ap):
                    ins = None
                    for q in range(8):
                        ins = e.transpose(bkb[:, q * 128:(q + 1) * 128], ap[:, q * 128:(q + 1) * 128], ident[:, :])
                    return ins
                P.op("pe", trp, reads=[b, ident_b], writes=[bk_b])
                i = rrc["oc"] % NOC; rrc["oc"] += 1
                oc, oc_b = OC[i]
                P.op("act", lambda e, oc=oc, bkb=bkb: e.copy(oc, bkb), reads=[bk_b], writes=[oc_b])
                P.dma("act", [lambda e, oc=oc, scr=scr, tb0=tb0: e.dma_start(
                    out=dr[scr].rearrange("(j p) t -> p j t", p=128)[:, :, tb0:tb0 + 128], in_=oc.rearrange("p (j t) -> p j t", j=8))],
                    reads=[oc_b], writes=[dbuf[scr]], semkey="stc%d" % i)

            tkap, tkap_b = W["tkap"]
            tx, tx_b = W["tx"]; ty, ty_b = W["ty"]; ta, ta_b = W["ta"]; tw, tw_b = W["tw"]
            vb, vb_b, vkey = getob()
            P.op("act", lambda e, vb=vb: e.copy(vb, tv), reads=[tv_b], writes=[vb_b])
            st(vb, vb_b, "s_v", vkey)
            P.op("dve", lambda e: e.tensor_tensor(out=tx, in0=tk, in1=rows["k_k"][0], op=ALU.mult),
                 reads=[tk_b, rows["k_k"][1]], writes=[tx_b])
            P.op("act", lambda e: e.activation(out=ty, in_=tx, func=AF.Square), reads=[tx_b], writes=[ty_b])
            P.op("dve", lambda e: e.tensor_reduce(out=ss, in_=h3(ty), axis=AX.X, op=ALU.add), reads=[ty_b], writes=[ss_b])
            P.op("dve", lambda e: e.tensor_scalar(ss, ss, 1e-24, None, ALU.max), reads=[ss_b], writes=[ss_b])
            P.op("act", lambda e: e.activation(out=ss, in_=ss, func=AF.Sqrt), reads=[ss_b], writes=[ss_b])
            P.op("dve", lambda e: e.reciprocal(ss, ss), reads=[ss_b], writes=[ss_b])
            P.op("dve", lambda e: e.tensor_tensor(out=h3(tkap), in0=h3(tx), in1=ss.unsqueeze(2).to_broadcast([128, 16, 64]), op=ALU.mult),
                 reads=[tx_b, ss_b], writes=[tkap_b])
            yield
            for j in range(2):
                for (cT, cT_b, c2, c2_b, dstt, dstt_b, rown) in ((twT, twT_b, w2c, w2c_b, tw, tw_b, "w0_%d" % j),
                                                                (taT, taT_b, a2c, a2c_b, ta, ta_b, "a0_%d" % j)):
                    for half in range(2):
                        bk, bk_b = getbank()
                        P.op("pe", lambda e, bk=bk, cT=cT, c2=c2, half=half, j=j: e.matmul(
                            bk[:, :], cT[j * 64:(j + 1) * 64, :], c2[j * 64:(j + 1) * 64, half * 512:(half + 1) * 512],
                            start=True, stop=True), reads=[cT_b, c2_b], writes=[bk_b])
                        P.op("dve", lambda e, bk=bk, dstt=dstt, half=half, rown=rown: e.tensor_tensor(
                            out=dstt[:, half * 512:(half + 1) * 512], in0=bk[:, :], in1=rows[rown][0][:, half * 512:(half + 1) * 512],
                            op=ALU.add), reads=[bk_b, rows[rown][1]], writes=[dstt_b])
                P.op("act", lambda e: e.activation(out=tw, in_=tw, func=AF.Sigmoid), reads=[tw_b], writes=[tw_b])
                P.op("act", lambda e: e.activation(out=ta, in_=ta, func=AF.Sigmoid), reads=[ta_b], writes=[ta_b])
                yield
                P.op("pool", lambda e: e.tensor_tensor(out=tx, in0=tkap, in1=ta, op=ALU.mult), reads=[tkap_b, ta_b], writes=[tx_b])
                P.op("dve", lambda e: e.scalar_tensor_tensor(out=ty, in0=ta, scalar=-1.0, in1=rows["k_a"][0], op0=ALU.add, op1=ALU.mult),
                     reads=[ta_b, rows["k_a"][1]], writes=[ty_b])
                P.op("dve", lambda e: e.scalar_tensor_tensor(out=ta, in0=ty, scalar=1.0, in1=tk, op0=ALU.add, op1=ALU.mult),
                     reads=[ty_b, tk_b], writes=[ta_b])
                P.op("pool", lambda e: e.tensor_tensor(out=ty, in0=tr, in1=rows["r_k"][0], op=ALU.mult),
                     reads=[tr_b, rows["r_k"][1]], writes=[ty_b])
                P.op("pool", lambda e: e.tensor_tensor(out=ty, in0=ty, in1=ta, op=ALU.mult), reads=[ty_b, ta_b], writes=[ty_b])
                P.op("dve", lambda e, j=j: e.tensor_reduce(out=rk[j], in_=h3(ty), axis=AX.X, op=ALU.add), reads=[ty_b], writes=[rk_b[j]])
                P.op("act", lambda e: e.copy(hi, tw), reads=[tw_b], writes=[hi_b])
                P.op("dve", lambda e: e.tensor_tensor(out=lo, in0=tw, in1=hi, op=ALU.subtract), reads=[tw_b, hi_b], writes=[lo_b])
                yield
                bk, bk_b = getbank()

                def mmp(e, bk=bk):
                    ins = None
                    for pj in range(8):
                        for (src_, fl) in ((hi, 0), (lo, 1)):
                            ins = e.matmul(bk[:, pj * 2:pj * 2 + 2], src_[:, pj * 128:(pj + 1) * 128], cind[:, :], start=(fl == 0), stop=(fl == 1))
                    return ins
                P.op("pe", mmp, reads=[hi_b, lo_b, cind_b], writes=[bk_b])
                P.op("act", lambda e, bk=bk, j=j, ti=ti: e.activation(out=PC[j][:, :, 2 * ti:2 * ti + 2],
                                                                     in_=bk[:, 0:16].rearrange("p (j c) -> p j c", j=8), func=AF.Exp, scale=-CDEC),
                     reads=[bk_b], writes=[PC_b[j]])
                for mi in range(3):
                    cb = {}
                    for half in range(2):
                        bk, bk_b = getbank()

                        def mmc(e, bk=bk, mi=mi, half=half, j=j):
                            e.matmul(bk[:, :], ctri[:, j, mi, :], hi[:, half * 512:(half + 1) * 512], start=True, stop=False)
                            return e.matmul(bk[:, :], ctri[:, j, mi, :], lo[:, half * 512:(half + 1) * 512], start=False, stop=True)
                        P.op("pe", mmc, reads=[hi_b, lo_b, ctri_b], writes=[bk_b])
                        cb[half] = (bk, bk_b)

                    def expo(scale, cb=cb):
                        i = C_["terr"][0] % 2; C_["terr"][0] += 1
                        te, te_b = W["te%d" % i]
                        for half in range(2):
                            bk, bk_b = cb[half]
                            P.op("act", lambda e, te=te, bk=bk, half=half, scale=scale: e.activation(
                                out=te[:, half * 512:(half + 1) * 512], in_=bk[:, :], func=AF.Exp, scale=scale), reads=[bk_b], writes=[te_b])
                        return te, te_b
                    if mi == 0:
                        te, te_b = expo(-CDEC)
                        te2, te2_b = expo(CDEC)
                        emit_ch("tr", te, te_b, "c_rt%d" % j)
                        yield
                        emit_ch("ta", te2, te2_b, "c_ktl%d" % j)
                        emit_ch("tx", te2, te2_b, "c_bt%d" % j)
                    elif mi == 1:
                        te, te_b = expo(-CDEC)
                        emit_tok("tkap", te, te_b, "s_ka%d" % j)
                        emit_ch("tkap", te, te_b, "c_kt%d" % j)
                    else:
                        te, te_b = expo(-CDEC)
                        emit_tok("ta", te, te_b, "s_kh%d" % j)
                        emit_tok("tx", te, te_b, "s_bh%d" % j)
                    yield
            P.op("dve", lambda e: e.tensor_tensor(out=rk[0], in0=rk[0], in1=rk[1], op=ALU.add), reads=[rk_b[0], rk_b[1]], writes=[rk_b[0]])
            P.op("dve", lambda e: e.tensor_tensor(out=h3(ty), in0=h3(tv), in1=rk[0].unsqueeze(2).to_broadcast([128, 16, 64]), op=ALU.mult),
                 reads=[tv_b, rk_b[0]], writes=[ty_b])
            st(ty, ty_b, "s_bonus", "ty")
            for half in range(2):
                bk, bk_b = getbank()
                P.op("pe", lambda e, bk=bk, half=half: e.matmul(bk[:, :], sgT[:, :], G2[:, half * 512:(half + 1) * 512],
                                                                start=True, stop=True), reads=[sgT_b, G2_b], writes=[bk_b])
                P.op("act", lambda e, bk=bk, half=half: e.copy(tw[:, half * 512:(half + 1) * 512], bk[:, :]),
                     reads=[bk_b], writes=[tw_b])
            st(tw, tw_b, "s_g", "tw")
            yield


        drive_window(be, nb_, NCH)
        P.barrier()
        A.release(m)

    def scan():
        m = A.mark()
        cm_ap, cm_b = cload("cmask", 2 * 5 * 128, F32, "p d m t -> p (d m t)")
        cmask = cm_ap.rearrange("p (d m t) -> p d m t", d=2, m=5)
        id2, id2_b = cload("cid2", 64, F32)

        def a4(n_, dt=BF16):
            return A.alloc(16 * n_, dt).rearrange("p (h n) -> p h n", h=16)
        NSLOT = 2
        IN = []
        for s_ in range(NSLOT):
            d_ = {}
            d_["KR"] = (A.alloc(8 * 2 * 128, BF16).rearrange("p (j c t) -> p j c t", j=8, c=2), Buf("KR%d" % s_))
            d_["KT"] = (alloc3(8, 128, BF16), Buf("KT%d" % s_))
            d_["BT"] = (alloc3(8, 128, BF16), Buf("BT%d" % s_))
            for n in ("V", "KA", "KH", "BH"):
                d_[n] = (A.alloc(D, BF16), Buf("%s%d" % (n, s_)))
            IN.append(d_)
        NR = A.alloc(16 * 2 * 128, BF16).rearrange("p (h c t) -> p h c t", h=16, c=2); NR_b = [Buf("NR%d" % i) for i in range(8)]
        BK = A.alloc(16 * 2 * 128, BF16).rearrange("p (h c t) -> p h c t", h=16, c=2); BK_b = [Buf("BK%d" % i) for i in range(8)]
        Nn = a4(128); Nn_b = [Buf("Nn%d" % i) for i in range(4)]
        SS = [A.alloc(16 * 2 * 128, BF16).rearrange("p (h c t) -> p h c t", h=16, c=2) for _ in range(2)]
        SS_b = [[Buf("SS%d_%d" % (s_, i)) for i in range(8)] for s_ in range(2)]
        QT = [a4(128) for _ in range(2)]; QT_b = [[Buf("QT%d_%d" % (s_, i)) for i in range(4)] for s_ in range(2)]
        Wt = A.alloc(D, BF16); Wt_b = [Buf("Wt0"), Buf("Wt1")]
        BVt = A.alloc(D, BF16); BVt_b = [Buf("BV0"), Buf("BV1")]
        nUt = A.alloc(D, BF16); nUt_b = [Buf("nU0"), Buf("nU1")]
        diagP = [A.alloc(512, F32).rearrange("p (j k) -> p j k", j=8) for _ in range(2)]; diagP_b = [Buf("dP0"), Buf("dP1")]
        OUT = []
        for s_ in range(2):
            d_ = {}
            d_["GT"] = (A.alloc(2 * 512, BF16).rearrange("p (c j k) -> p c j k", c=2, j=8), [Buf("GT%d_0" % s_), Buf("GT%d_1" % s_)])
            d_["H"] = (A.alloc(2 * 512, F32).rearrange("p (c n) -> p c n", c=2), [Buf("H%d_0" % s_), Buf("H%d_1" % s_)])
            d_["RhT"] = (alloc3(8, 128, BF16), [Buf("Rh%d_0" % s_), Buf("Rh%d_1" % s_)])
            d_["Yl"] = (A.alloc(D, F32), [Buf("Yl%d_0" % s_), Buf("Yl%d_1" % s_)])
            OUT.append(d_)
        U = [(A.alloc(512, BF16).rearrange("p (j v) -> p j v", j=8), Buf("U%d" % i)) for i in range(4)]
        yo = [(A.alloc(D, F32), [Buf("yo%d_0" % i), Buf("yo%d_1" % i)]) for i in range(2)]
        urr = [0]

        def nextU():
            u = U[urr[0] % 4]; urr[0] += 1
            return u

        def hp(ix):
            return ix % 8, ix // 8

        def hcol(ix):
            hh_ = 2 * (ix % 8) + ix // 8
            return slice(hh_ * 64, (hh_ + 1) * 64)

        def icol(ix):
            return slice(ix * 64, (ix + 1) * 64)

        def qs(q):
            return slice(q * 64, (q + 1) * 64)

        iters = []
        for b in range(2):
            for dr_ in range(2):
                blocks = list(range(16)) if dr_ == 0 else list(range(15, -1, -1))
                for bidx, bi in enumerate(blocks):
                    iters.append((b, dr_, bi, bidx == 0))
        if DBG_ITERS is not None:
            iters = [iters[i_] for i_ in DBG_ITERS]
        chv = lambda n: dr[n].rearrange("(j p) t -> p j t", p=128)

        def issue_loads(n_):
            b, dr_, bi, _ = iters[n_]
            sl = n_ % NSLOT
            I = IN[sl]
            tb0 = b * S + bi * 128
            KR, KR_b = I["KR"]; KT, KT_b = I["KT"]; BT, BT_b = I["BT"]
            P.dma("sp", [lambda e, KR=KR, tb0=tb0, dr_=dr_: e.dma_start(out=KR[:, :, 0, :], in_=chv("c_kt%d" % dr_)[:, :, tb0:tb0 + 128]),
                         lambda e, KR=KR, tb0=tb0, dr_=dr_: e.dma_start(out=KR[:, :, 1, :], in_=chv("c_rt%d" % dr_)[:, :, tb0:tb0 + 128])],
                  reads=[dbuf["c_kt%d" % dr_], dbuf["c_rt%d" % dr_]], writes=[KR_b], semkey="lKR%d" % sl)
            P.dma("sp", [lambda e, KT=KT, tb0=tb0, dr_=dr_: e.dma_start(out=KT, in_=chv("c_ktl%d" % dr_)[:, :, tb0:tb0 + 128])],
                  reads=[dbuf["c_ktl%d" % dr_]], writes=[KT_b], semkey="lKT%d" % sl)
            P.dma("sp", [lambda e, BT=BT, tb0=tb0, dr_=dr_: e.dma_start(out=BT, in_=chv("c_bt%d" % dr_)[:, :, tb0:tb0 + 128])],
                  reads=[dbuf["c_bt%d" % dr_]], writes=[BT_b], semkey="lBT%d" % sl)
            for (n, scr) in (("V", "s_v"), ("KA", "s_ka%d" % dr_), ("KH", "s_kh%d" % dr_), ("BH", "s_bh%d" % dr_)):
                ap, b_ = I[n]
                P.dma("sp", [lambda e, ap=ap, scr=scr, tb0=tb0: e.dma_start(out=ap, in_=dr[scr][tb0:tb0 + 128, :])],
                      reads=[dbuf[scr]], writes=[b_], semkey="l%s%d" % (n, sl))

        issue_loads(0)
        u_cur = u_cur_b = None
        if True:
            if True:
                def stageA(it_):
                    (b, dr_, bi, isfirst) = iters[it_]
                    if it_ + 1 < len(iters):
                        issue_loads(it_ + 1)
                    corder = (0, 1) if dr_ == 0 else (1, 0)
                    sl = it_ % NSLOT; osl = it_ % 2; it = it_ + 1
                    I = IN[sl]; O = OUT[osl]
                    tb0 = b * S + bi * 128
                    gchunk = (b * 16 + bi) * 2
                    KR, KR_b = I["KR"]; KT, KT_b = I["KT"]; BT, BT_b = I["BT"]
                    Vt, Vt_b = I["V"]; KAt, KAt_b = I["KA"]; KHt, KHt_b = I["KH"]; BHt, BHt_b = I["BH"]
                    MK1 = cmask[:, dr_, 0:2, :]; MK2 = cmask[:, dr_, 2:4, :]; MK3 = cmask[:, dr_, 4, :]

                    def qs(q):
                        return slice(q * 64, (q + 1) * 64)

                    for (lhs, lhs_b, dstt, dstt_b, MK) in ((BT, BT_b, NR, NR_b, MK1), (KT, KT_b, BK, BK_b, MK2)):
                        for g in range(8):
                            bk, bk_b = getbank()

                            def mm(e, bk=bk, g=g, lhs=lhs, KR=KR):
                                ins = None
                                for hh in range(2):
                                    h = 2 * g + hh; j, q = hp(h)
                                    ins = e.matmul(bk[:, hh * 256:(hh + 1) * 256], lhs[qs(q), j, :],
                                                   KR[qs(q), j, :, :].rearrange("p c t -> p (c t)"), start=True, stop=True)
                                return ins
                            P.op("pe", mm, reads=[lhs_b, KR_b], writes=[bk_b])
                            P.op("dve", lambda e, bk=bk, g=g, dstt=dstt, MK=MK: e.tensor_tensor(
                                out=dstt[:, 2 * g:2 * g + 2, :, :], in0=bk[:, :].rearrange("p (h c t) -> p h c t", h=2, c=2),
                                in1=MK.unsqueeze(1).to_broadcast([128, 2, 2, 128]), op=ALU.mult),
                                reads=[bk_b, cm_b], writes=[dstt_b[g]])
                    yield
                    for g in range(4):
                        bk, bk_b = getbank()

                        def mm(e, bk=bk, g=g, KR=KR, BT=BT):
                            ins = None
                            for hh in range(4):
                                h = 4 * g + hh; j, q = hp(h)
                                ins = e.matmul(bk[:, hh * 128:(hh + 1) * 128], KR[qs(q), j, 0, :], BT[qs(q), j, :], start=True, stop=True)
                            return ins
                        P.op("pe", mm, reads=[KR_b, BT_b], writes=[bk_b])
                        P.op("dve", lambda e, bk=bk, g=g, MK3=MK3: e.tensor_tensor(
                            out=Nn[:, 4 * g:4 * g + 4, :], in0=bk[:, :].rearrange("p (h t) -> p h t", h=4),
                            in1=MK3.unsqueeze(1).to_broadcast([128, 4, 128]), op=ALU.mult), reads=[bk_b, cm_b], writes=[Nn_b[g]])
                    yield
                    q0 = 0
                    for g in range(4):
                        P.op("pool", lambda e, g=g: e.tensor_tensor(out=QT[0][:, 4 * g:4 * g + 4, :], in0=NR[:, 4 * g:4 * g + 4, 0, :],
                                                                     in1=ident.unsqueeze(1).to_broadcast([128, 4, 128]), op=ALU.add),
                             reads=[NR_b[2 * g], NR_b[2 * g + 1], ident_b], writes=[QT_b[0][g]])
                    for lev in range(5):
                        sidx = lev % 2
                        SSn = SS[sidx]; SSn_b = SS_b[sidx]
                        if lev == 0:
                            Np = lambda h: Nn[:, h, :]; NTp = lambda h: NR[:, h, 0, :]
                            Np_b = lambda h: [Nn_b[h // 4]]; NTp_b = lambda h: [NR_b[h // 2]]
                        else:
                            SSp = SS[1 - sidx]; SSp_b = SS_b[1 - sidx]
                            Np = lambda h, SSp=SSp: SSp[:, h, 0, :]; NTp = lambda h, SSp=SSp: SSp[:, h, 1, :]
                            Np_b = lambda h, SSp_b=SSp_b: [SSp_b[h // 2]]; NTp_b = Np_b
                        for g in range(8):
                            bk, bk_b = getbank()

                            def mm(e, bk=bk, g=g, Np=Np, NTp=NTp):
                                ins = None
                                for hh in range(2):
                                    h = 2 * g + hh
                                    e.matmul(bk[:, hh * 256:hh * 256 + 128], NTp(h), Np(h), start=True, stop=True)
                                    ins = e.matmul(bk[:, hh * 256 + 128:hh * 256 + 256], Np(h), NTp(h), start=True, stop=True)
                                return ins
                            P.op("pe", mm, reads=Np_b(2 * g) + NTp_b(2 * g) + Np_b(2 * g + 1) + NTp_b(2 * g + 1), writes=[bk_b])
                            P.op("act", lambda e, bk=bk, g=g, SSn=SSn: e.copy(SSn[:, 2 * g:2 * g + 2, :, :],
                                                                            bk[:, :].rearrange("p (h c t) -> p h c t", h=2, c=2)),
                                 reads=[bk_b], writes=[SSn_b[g]])
                        yield
                        Qp = QT[q0]; Qn = QT[1 - q0]; Qp_b = QT_b[q0]; Qn_b = QT_b[1 - q0]
                        for g in range(4):
                            bk, bk_b = getbank()

                            def mm(e, bk=bk, g=g, SSn=SSn, Qp=Qp):
                                ins = None
                                for hh in range(4):
                                    h = 4 * g + hh
                                    ins = e.matmul(bk[:, hh * 128:(hh + 1) * 128], SSn[:, h, 0, :], Qp[:, h, :], start=True, stop=True)
                                return ins
                            P.op("pe", mm, reads=[SSn_b[2 * g], SSn_b[2 * g + 1], Qp_b[g]], writes=[bk_b])
                            P.op("dve", lambda e, bk=bk, g=g, Qp=Qp, Qn=Qn: e.tensor_tensor(
                                out=Qn[:, 4 * g:4 * g + 4, :], in0=bk[:, :].rearrange("p (h t) -> p h t", h=4), in1=Qp[:, 4 * g:4 * g + 4, :], op=ALU.add),
                                reads=[bk_b, Qp_b[g]], writes=[Qn_b[g]])
                        q0 = 1 - q0
                    Qf = QT[q0]; Qf_b = QT_b[q0]
                    yield
                    for (kind, dstt, dstt_b) in (("W", Wt, Wt_b), ("BV", BVt, BVt_b), ("nU", nUt, nUt_b)):
                        for g in range(2):
                            bk, bk_b = getbank()

                            def mm(e, bk=bk, g=g, kind=kind, Qf=Qf, KAt=KAt, Vt=Vt):
                                ins = None
                                for hh in range(8):
                                    h = 8 * g + hh
                                    if kind == "W":
                                        ins = e.matmul(bk[:, hh * 64:(hh + 1) * 64], Qf[:, h, :], KAt[:, hcol(h)], start=True, stop=True)
                                    elif kind == "BV":
                                        ins = e.matmul(bk[:, hh * 64:(hh + 1) * 64], BK[:, h, 0, :], Vt[:, hcol(h)], start=True, stop=True)
                                    else:
                                        ins = e.matmul(bk[:, hh * 64:(hh + 1) * 64], Qf[:, h, :], BVt[:, icol(h)], start=True, stop=True)
                                return ins
                            if kind == "W":
                                rd = [Qf_b[2 * g], Qf_b[2 * g + 1], KAt_b]
                            elif kind == "BV":
                                rd = BK_b[4 * g:4 * g + 4] + [Vt_b]
                            else:
                                rd = [Qf_b[2 * g], Qf_b[2 * g + 1], BVt_b[g]]
                            P.op("pe", mm, reads=rd, writes=[bk_b])
                            if kind == "nU":
                                P.op("act", lambda e, bk=bk, g=g, dstt=dstt: e.mul(dstt[:, g * 512:(g + 1) * 512], bk[:, :], -1.0),
                                     reads=[bk_b], writes=[dstt_b[g]])
                            else:
                                P.op("act", lambda e, bk=bk, g=g, dstt=dstt: e.copy(dstt[:, g * 512:(g + 1) * 512], bk[:, :]),
                                     reads=[bk_b], writes=[dstt_b[g]])
                    GT, GT_b = O["GT"]; H, H_b = O["H"]; RhT, RhT_b = O["RhT"]; Yl, Yl_b = O["Yl"]
                    yield
                    for cch in range(2):
                        csl = slice(cch * 64, (cch + 1) * 64)
                        dP = diagP[cch]; dP_b = diagP_b[cch]
                        P.op("pool", lambda e, dP=dP, dr_=dr_, gc=gchunk + cch: e.tensor_tensor(
                            out=dP, in0=id2.unsqueeze(1).to_broadcast([128, 8, 64]),
                            in1=PC[dr_][:, :, gc:gc + 1].to_broadcast([128, 8, 64]), op=ALU.mult),
                            reads=[id2_b, PC_b[dr_]], writes=[dP_b])
                        bk, bk_b = getbank()

                        def mm(e, bk=bk, csl=csl, cch=cch, BHt=BHt):
                            ins = None
                            for h in range(16):
                                j, q = hp(h)
                                ins = e.matmul(bk[qs(q), j * 64:(j + 1) * 64], Wt[csl, icol(h)], BHt[csl, hcol(h)], start=True, stop=True,
                                               tile_position=(cch * 64, q * 64))
                            return ins
                        P.op("pe", mm, reads=Wt_b + [BHt_b], writes=[bk_b])
                        P.op("dve", lambda e, bk=bk, cch=cch, GT=GT, dP=dP: e.tensor_tensor(
                            out=GT[:, cch, :, :], in0=dP, in1=bk[:, :].rearrange("p (j k) -> p j k", j=8), op=ALU.subtract),
                            reads=[bk_b, dP_b], writes=[GT_b[cch]])
                        bk, bk_b = getbank()

                        def mm(e, bk=bk, csl=csl, cch=cch, KHt=KHt, BHt=BHt, Vt=Vt):
                            ins = None
                            for h in range(16):
                                j, q = hp(h)
                                e.matmul(bk[qs(q), j * 64:(j + 1) * 64], KHt[csl, hcol(h)], Vt[csl, hcol(h)], start=True, stop=False,
                                         tile_position=(cch * 64, q * 64))
                                ins = e.matmul(bk[qs(q), j * 64:(j + 1) * 64], BHt[csl, hcol(h)], nUt[csl, icol(h)], start=False, stop=True,
                                               tile_position=(cch * 64, q * 64))
                            return ins
                        P.op("pe", mm, reads=[KHt_b, BHt_b, Vt_b] + nUt_b, writes=[bk_b])
                        P.op("act", lambda e, bk=bk, cch=cch, H=H: e.copy(H[:, cch, :], bk[:, :]), reads=[bk_b], writes=[H_b[cch]])
                    yield
                    for g in range(2):
                        bk, bk_b = getbank()

                        def mm(e, bk=bk, g=g):
                            ins = None
                            for jj in range(4):
                                for q in range(2):
                                    j = 4 * g + jj; h = q * 8 + j
                                    ins = e.matmul(bk[qs(q), jj * 128:(jj + 1) * 128], Wt[:, icol(h)], NR[:, h, 1, :],
                                                   start=True, stop=True, tile_position=(0, q * 64))
                            return ins
                        P.op("pe", mm, reads=Wt_b + NR_b[2 * g:2 * g + 2] + NR_b[4 + 2 * g:4 + 2 * g + 2], writes=[bk_b])
                        P.op("dve", lambda e, bk=bk, g=g, RhT=RhT, KR=KR: e.tensor_tensor(
                            out=RhT[:, 4 * g:4 * g + 4, :], in0=KR[:, 4 * g:4 * g + 4, 1, :], in1=bk[:, :].rearrange("p (j t) -> p j t", j=4),
                            op=ALU.subtract), reads=[bk_b, KR_b], writes=[RhT_b[g]])
                    yield
                    for g in range(2):
                        bk, bk_b = getbank()

                        def mm(e, bk=bk, g=g, Vt=Vt):
                            ins = None
                            for hh in range(8):
                                h = 8 * g + hh
                                e.matmul(bk[:, hh * 64:(hh + 1) * 64], BK[:, h, 1, :], Vt[:, hcol(h)], start=True, stop=False)
                                ins = e.matmul(bk[:, hh * 64:(hh + 1) * 64], NR[:, h, 1, :], nUt[:, icol(h)], start=False, stop=True)
                            return ins
                        P.op("pe", mm, reads=BK_b[4 * g:4 * g + 4] + NR_b[4 * g:4 * g + 4] + [Vt_b, nUt_b[g]], writes=[bk_b])
                        P.op("act", lambda e, bk=bk, g=g, Yl=Yl: e.copy(Yl[:, g * 512:(g + 1) * 512], bk[:, :]), reads=[bk_b], writes=[Yl_b[g]])
                    yield

                ust = {}

                def stageBC(it_):
                    (b, dr_, bi, isfirst) = iters[it_]
                    corder = (0, 1) if dr_ == 0 else (1, 0)
                    osl = it_ % 2; it = it_ + 1
                    O = OUT[osl]
                    tb0 = b * S + bi * 128
                    GT, GT_b = O["GT"]; H, H_b = O["H"]; RhT, RhT_b = O["RhT"]; Yl, Yl_b = O["Yl"]
                    if isfirst:
                        u_cur, u_cur_b = nextU()
                        P.op("pool", lambda e, u_cur=u_cur: e.memset(u_cur, 0.0), writes=[u_cur_b])
                    else:
                        u_cur, u_cur_b = ust["u"]
                    Uc = {}
                    for cch in corder:
                        Uc[cch] = (u_cur, u_cur_b)
                        u_new, u_new_b = nextU()
                        for q in range(2):
                            bk, bk_b = getbank()

                            def mm(e, bk=bk, cch=cch, GT=GT, u_cur=u_cur, q=q):
                                ins = None
                                for j in range(8):
                                    ins = e.matmul(bk[qs(q), j * 64:(j + 1) * 64], GT[qs(q), cch, j, :], u_cur[qs(q), j, :], start=True, stop=True,
                                                   tile_position=(q * 64, q * 64))
                                return ins
                            P.op("pe", mm, reads=[GT_b[cch], u_cur_b], writes=[bk_b])
                            P.op("dve", lambda e, bk=bk, cch=cch, H=H, u_new=u_new, q=q: e.tensor_tensor(
                                out=u_new[qs(q)].rearrange("p j v -> p (j v)"), in0=bk[qs(q), :], in1=H[qs(q), cch, :], op=ALU.add),
                                reads=[bk_b, H_b[cch]], writes=[u_new_b])
                        u_cur, u_cur_b = u_new, u_new_b
                        yield
                    yo_ap, yo_b = yo[it % 2]
                    for g in range(2):
                        bk, bk_b = getbank()

                        def mm(e, bk=bk, g=g, RhT=RhT, Uc=dict(Uc)):
                            ins = None
                            for cch in range(2):
                                uu = Uc[cch][0]
                                for hh in range(8):
                                    j = hh; q = g
                                    ins = e.matmul(bk[cch * 64:(cch + 1) * 64, hh * 64:(hh + 1) * 64], RhT[qs(q), j, cch * 64:(cch + 1) * 64],
                                                   uu[qs(q), j, :], start=True, stop=True, tile_position=(q * 64, cch * 64))
                            return ins
                        P.op("pe", mm, reads=RhT_b + [Uc[0][1], Uc[1][1]], writes=[bk_b])
                        P.op("dve", lambda e, bk=bk, g=g, Yl=Yl, yo_ap=yo_ap: e.tensor_tensor(
                            out=yo_ap.rearrange("p (j q v) -> p j q v", j=8, q=2)[:, :, g, :], in0=bk[:, :].rearrange("p (j v) -> p j v", j=8),
                            in1=Yl[:, g * 512:(g + 1) * 512].rearrange("p (j v) -> p j v", j=8), op=ALU.add),
                            reads=[bk_b, Yl_b[g]], writes=[yo_b[g]])
                    P.dma("pool", [lambda e, yo_ap=yo_ap, tb0=tb0, dr_=dr_: e.dma_start(out=dr["s_y%d" % dr_][tb0:tb0 + 128, :], in_=yo_ap)],
                          reads=yo_b, writes=[dbuf["s_y%d" % dr_]], semkey="yst%d" % (it % 2))
                    ust["u"] = (u_cur, u_cur_b)
                    yield

                def drive2(gens):
                    gens = [g for g in gens if g is not None]
                    while gens:
                        for g in list(gens):
                            try:
                                next(g)
                            except StopIteration:
                                gens.remove(g)

                drive2([stageA(0)])
                for it_ in range(len(iters)):
                    drive2([stageBC(it_), stageA(it_ + 1) if it_ + 1 < len(iters) else None])
        P.barrier()
        A.release(m)

    def post():
        m = A.mark()
        pstage = [(A.alloc(D, F32), Buf("wst%d" % i)) for i in range(2)]
        Wo = A.alloc(KD * D, BF16).rearrange("p (k n) -> p k n", k=KD); Wo_b = [Buf("wo%d" % k) for k in range(KD)]
        wv = dr["w_o"].rearrange("(k p) n -> p k n", p=128)
        for k in range(KD):
            stg, stg_b = pstage[k % 2]
            P.dma("sp", [lambda e, k=k, stg=stg: e.dma_start(out=stg, in_=wv[:, k, :])], writes=[stg_b], semkey="wst%d" % (k % 2))
            P.op("dve" if k % 2 == 0 else "act", (lambda e, k=k, stg=stg: e.tensor_copy(Wo[:, k, :], stg)) if k % 2 == 0 else (lambda e, k=k, stg=stg: e.copy(Wo[:, k, :], stg)),
                 reads=[stg_b], writes=[Wo_b[k]])
        rows = load_rows(["ln_w", "ln_b"])
        SETS = []
        for i in range(2):
            d_ = {}
            for n in ("y0", "y1", "bon", "gg", "tz"):
                d_[n] = (A.alloc(D, F32), Buf("%s_%d" % (n, i)))
            d_["ob"] = (A.alloc(D, BF16), Buf("ob_%d" % i))
            d_["st1"] = (A.alloc(16, F32), Buf("st1_%d" % i))
            d_["st2"] = (A.alloc(16, F32), Buf("st2_%d" % i))
            SETS.append(d_)
        TQ = 512
        oTs = [A.alloc(KD * TQ, BF16).rearrange("p (k n) -> p k n", k=KD) for _ in range(2)]
        oTs_b = [[Buf("oT%d_%d" % (i, b_)) for b_ in range(TQ // 128)] for i in range(2)]
        xts = [A.alloc(KD * TQ, F32).rearrange("p (k n) -> p k n", k=KD) for _ in range(2)]
        xts_b = [Buf("xt0"), Buf("xt1")]

        def blkgen(ti, blk):
            Sx = SETS[blk % 2]
            y0, y0_b = Sx["y0"]; y1, y1_b = Sx["y1"]; bon, bon_b = Sx["bon"]; gg, gg_b = Sx["gg"]; tz, tz_b = Sx["tz"]
            ob, ob_b = Sx["ob"]; st1, st1_b = Sx["st1"]; st2, st2_b = Sx["st2"]
            oT = oTs[ti % 2]; oT_b = oTs_b[ti % 2]
            tb0 = ti * TQ + blk * 128
            for (ap, b_, scr) in ((y0, y0_b, "s_y0"), (y1, y1_b, "s_y1"), (bon, bon_b, "s_bonus"), (gg, gg_b, "s_g")):
                P.dma("sp", [lambda e, ap=ap, scr=scr, tb0=tb0: e.dma_start(out=ap, in_=dr[scr][tb0:tb0 + 128, :])],
                      reads=[dbuf[scr]], writes=[b_], semkey="ld_%s_%d" % (scr, blk % 2))
            yield
            P.op("dve", lambda e: e.tensor_tensor(out=y0, in0=y0, in1=y1, op=ALU.add), reads=[y0_b, y1_b], writes=[y0_b])
            P.op("dve", lambda e: e.tensor_reduce(out=st1, in_=h3(y0), axis=AX.X, op=ALU.add), reads=[y0_b], writes=[st1_b])
            P.op("dve", lambda e: e.tensor_scalar(st1, st1, 1.0 / 64, None, ALU.mult), reads=[st1_b], writes=[st1_b])
            yield
            P.op("dve", lambda e: e.tensor_tensor(out=h3(y0), in0=h3(y0), in1=st1.unsqueeze(2).to_broadcast([128, 16, 64]), op=ALU.subtract),
                 reads=[y0_b, st1_b], writes=[y0_b])
            P.op("act", lambda e: e.activation(out=tz, in_=y0, func=AF.Square), reads=[y0_b], writes=[tz_b])
            P.op("pool", lambda e: e.tensor_tensor(out=bon, in0=bon, in1=rows["ln_b"][0], op=ALU.add), reads=[bon_b, rows["ln_b"][1]], writes=[bon_b])
            yield
            P.op("dve", lambda e: e.tensor_reduce(out=st2, in_=h3(tz), axis=AX.X, op=ALU.add), reads=[tz_b], writes=[st2_b])
            P.op("act", lambda e: e.activation(out=st2, in_=st2, func=AF.Sqrt, bias=64e-5, scale=1.0 / 64), reads=[st2_b], writes=[st2_b])
            P.op("dve", lambda e: e.reciprocal(st2, st2), reads=[st2_b], writes=[st2_b])
            yield
            P.op("dve", lambda e: e.tensor_tensor(out=h3(y0), in0=h3(y0), in1=st2.unsqueeze(2).to_broadcast([128, 16, 64]), op=ALU.mult),
                 reads=[y0_b, st2_b], writes=[y0_b])
            P.op("pool", lambda e: e.tensor_tensor(out=y0, in0=y0, in1=rows["ln_w"][0], op=ALU.mult), reads=[y0_b, rows["ln_w"][1]], writes=[y0_b])
            yield
            P.op("dve", lambda e: e.tensor_tensor(out=y0, in0=y0, in1=bon, op=ALU.add), reads=[y0_b, bon_b], writes=[y0_b])
            P.op("pool", lambda e: e.tensor_tensor(out=ob, in0=y0, in1=gg, op=ALU.mult), reads=[y0_b, gg_b], writes=[ob_b])
            yield
            for half in range(2):
                bk, bk_b = getbank()
                bkb = bk[:, :].bitcast(BF16)

                def tr(e, bkb=bkb, half=half):
                    ins = None
                    for q in range(4):
                        k = half * 4 + q
                        ins = e.transpose(bkb[:, q * 128:(q + 1) * 128], ob[:, k * 128:(k + 1) * 128], ident[:, :])
                    return ins
                P.op("pe", tr, reads=[ob_b, ident_b], writes=[bk_b])
                P.op("act", lambda e, bkb=bkb, half=half: e.copy(
                    oT[:, half * 4:(half + 1) * 4, blk * 128:(blk + 1) * 128], bkb[:, 0:512].rearrange("p (q n) -> p q n", q=4)),
                    reads=[bk_b], writes=[oT_b[blk]])
            yield

        def fingen(ti):
            t0 = ti * TQ
            oT = oTs[ti % 2]; oT_b = oTs_b[ti % 2]
            xt = xts[ti % 2]; xt_b = xts_b[ti % 2]
            P.dma("sp", [lambda e, t0=t0: e.dma_start(out=xt, in_=sv[:, :, t0:t0 + TQ])], reads=[dbuf[src]], writes=[xt_b], semkey="xt%d" % (ti % 2))
            yield
            for do in range(KD):
                bk, bk_b = getbank()

                def mm(e, bk=bk, do=do):
                    ins = None
                    for k in range(KD):
                        ins = e.matmul(bk[:, :], Wo[:, k, do * 128:(do + 1) * 128], oT[:, k, :], start=(k == 0), stop=(k == KD - 1))
                    return ins
                P.op("pe", mm, reads=oT_b + Wo_b, writes=[bk_b])
                P.op("dve", lambda e, bk=bk, do=do: e.tensor_tensor(out=xt[:, do, :], in0=xt[:, do, :], in1=bk[:, :], op=ALU.add),
                     reads=[bk_b, xt_b], writes=[xt_b])
                if do % 2 == 1:
                    yield
            P.dma("pool", [lambda e, t0=t0: e.dma_start(out=dv[:, :, t0:t0 + TQ], in_=xt)], reads=[xt_b], writes=[dbuf[dst]], semkey="xo%d" % (ti % 2))
            yield

        def drive3(gens):
            gens = [g for g in gens if g is not None]
            while gens:
                for g in list(gens):
                    try:
                        next(g)
                    except StopIteration:
                        gens.remove(g)

        prev_fin = None
        for ti in range(T // TQ):
            drive3([blkgen(ti, 0), blkgen(ti, 1), prev_fin])
            drive3([blkgen(ti, 2), blkgen(ti, 3)])
            prev_fin = fingen(ti)
        drive3([prev_fin])
        P.barrier()
        A.release(m)

    if "prep" in sub:
        prep()
    if "scan" in sub:
        scan()
    if "post" in sub:
        post()


F32 = mybir.dt.float32
BF16 = mybir.dt.bfloat16
ALU = mybir.AluOpType
AF = mybir.ActivationFunctionType

D = 1024; KD = 8; FF = 2816; KF = 22; T = 4096; S = 2048; TT = 512; NTT = T // TT
ARENA_WORDS = 52800
VEC_NAMES = ["nm0", "nm1", "nf0", "nf1", "nfin", "mu0", "mu1", "mu2", "mu3", "mu4", "mu5",
             "w0_0", "w0_1", "a0_0", "a0_1", "k_k", "k_a", "r_k", "ln_w", "ln_b"]
VI = {n: i for i, n in enumerate(VEC_NAMES)}
NV = len(VEC_NAMES)


class Ctx:
    pass


def build(phases=("l0mix", "ffn0", "l1mix", "ffn1", "final"), debug=False, rwkv_sub=("prep", "scan", "post"), dbg_scr=False):
    nc = bass.Bass("TRN2", target_bir_lowering=False)
    st = ExitStack()
    P = Prog(nc, st)
    dr = {}

    drh = {}

    def din(name, shape, dt=F32):
        drh[name] = nc.dram_tensor(name, list(shape), dt, kind="ExternalInput")
        dr[name] = drh[name].ap()

    def dscr(name, shape, dt=F32):
        kind = "ExternalOutput" if debug else "Internal"
        dr[name] = nc.dram_tensor(name, list(shape), dt, kind=kind).ap()

    din("xT", [D, T]); din("vecs", [128, NV * 8])
    din("fno_w", [D, D])
    din("wg", [2, D, FF]); din("wu", [2, D, FF]); din("wd", [2, FF, D])
    din("cs1", [128, 2, 512], BF16); din("cs2", [4, 128, 16, 2, 512], BF16)
    dr["outT"] = nc.dram_tensor("outT", [D, T], F32, kind="ExternalOutput").ap()
    dscr("xa", [D, T]); dscr("xb", [D, T])
    if "l1mix" in phases:
        declare(dr, drh, nc, din, dbg_scr)

    arena_t = st.enter_context(nc.sbuf_tensor("arena", [128, ARENA_WORDS], F32))
    A = Arena(arena_t, ARENA_WORDS)
    banks = []
    for i in range(8):
        pt = st.enter_context(nc.psum_tensor("bank%d" % i, [128, 512], F32))
        banks.append((pt, Buf("bank%d" % i)))
    bank_rr = [0]

    def getbank():
        b = banks[bank_rr[0] % 8]
        bank_rr[0] += 1
        return b

    vecs = A.alloc(NV * 8, F32); vecs_b = Buf("vecs")
    P.dma("sp", [lambda e: e.dma_start(out=vecs, in_=dr["vecs"])], writes=[vecs_b], semkey="vecs")
    onesD = A.alloc(128, BF16); ones_b = Buf("ones")
    P.op("pool", lambda e: e.memset(onesD, 1.0 / D), writes=[ones_b])
    persist_mark = A.mark()

    def vcol(name, k):
        i = VI[name] * 8 + k
        return vecs[:, i:i + 1]

    def dview(name):
        return dr[name].rearrange("(k p) t -> p k t", p=128)

    rr = {"cast": 0}

    def load_w_gen(w2d, K, N, tag, stage, stage_bufs, out):
        dst = A.alloc(K * N, BF16).rearrange("p (k n) -> p k n", k=K)
        bufs = [Buf("%s_k%d" % (tag, k)) for k in range(K)]
        out.append((dst, bufs))
        wv = w2d.rearrange("(k p) n -> p k n", p=128)
        for k in range(K):
            s = rr["cast"] % len(stage); rr["cast"] += 1
            sap = stage[s][:, 0:N]
            P.dma("sp", [lambda e, sap=sap, k=k: e.dma_start(out=sap, in_=wv[:, k, :])],
                  writes=stage_bufs[s], semkey="wst%d" % s)
            eng = "dve" if (k % 2 == 0) else "act"
            if eng == "dve":
                P.op("dve", lambda e, sap=sap, k=k: e.tensor_copy(dst[:, k, :], sap),
                     reads=stage_bufs[s], writes=[bufs[k]])
            else:
                P.op("act", lambda e, sap=sap, k=k: e.copy(dst[:, k, :], sap),
                     reads=stage_bufs[s], writes=[bufs[k]])
            yield

    def load_w_bf16(w2d, K, N, tag, stage, stage_bufs):
        out = []
        sb = [b if isinstance(b, list) else [b] for b in stage_bufs]
        for _ in load_w_gen(w2d, K, N, tag, stage, sb, out):
            pass
        return out[0]

    def drive(gens):
        gens = [g for g in gens if g is not None]
        while gens:
            for g in list(gens):
                try:
                    next(g)
                except StopIteration:
                    gens.remove(g)

    def rmsnorm(xt, xt_b, gname, hT, hT_b, sq, sq_b, rstd, rstd_b, n=TT):
        P.op("pool", lambda e: e.tensor_tensor(out=sq, in0=xt, in1=xt, op=ALU.mult), reads=[xt_b], writes=sq_b)
        bk, bk_b = getbank()

        def mm(e):
            ins = None
            for k in range(KD):
                ins = e.matmul(bk[:, 0:n], onesD[:, :], sq[:, k, :], start=(k == 0), stop=(k == KD - 1))
            return ins
        P.op("pe", mm, reads=sq_b + [ones_b], writes=[bk_b])
        P.op("act", lambda e: e.activation(out=rstd, in_=bk[:, 0:n], func=AF.Sqrt, bias=1e-6, scale=1.0),
             reads=[bk_b], writes=[rstd_b])
        P.op("dve", lambda e: e.reciprocal(rstd, rstd), reads=[rstd_b], writes=[rstd_b])
        for k in range(KD):
            eng = "dve"
            P.op(eng, lambda e, k=k: e.scalar_tensor_tensor(out=hT[:, k, :], in0=xt[:, k, :], scalar=vcol(gname, k),
                                                          in1=rstd, op0=ALU.mult, op1=ALU.mult),
                 reads=[xt_b, rstd_b, vecs_b], writes=[hT_b[k]])

    def phase_fourier(src, dst):
        m = A.mark()
        stage = [A.alloc(D, F32) for _ in range(2)]
        stage_bufs = [Buf("wst0"), Buf("wst1")]
        Wf, Wf_b = load_w_bf16(dr["fno_w"], KD, D, "wf", stage, stage_bufs)
        cs1 = A.alloc(2 * 512, BF16).rearrange("p (k n) -> p k n", k=2); cs1_b = Buf("cs1")
        P.dma("sp", [lambda e: e.dma_start(out=cs1, in_=dr["cs1"])], writes=[cs1_b], semkey="cs1")
        cs2 = [A.alloc(16 * 2 * 512, BF16).rearrange("p (s c n) -> p s c n", s=16, c=2) for _ in range(2)]
        cs2_b = [Buf("cs2_0"), Buf("cs2_1")]
        AB = A.alloc(2 * 16 * D, BF16).rearrange("p (c s d) -> p c s d", c=2, s=16)
        AB_b = [[Buf("AB%d_%d" % (i, j)) for j in range(4)] for i in range(16)]
        AB_all = [b_ for l_ in AB_b for b_ in l_]
        xt = [A.alloc(KD * TT, F32).rearrange("p (k n) -> p k n", k=KD) for _ in range(2)]
        xt_b = [Buf("xt0"), Buf("xt1")]
        hT = A.alloc(KD * TT, BF16).rearrange("p (k n) -> p k n", k=KD); hT_b = [Buf("hT%d" % i) for i in range(KD)]
        sq = A.alloc(KD * TT, BF16).rearrange("p (k n) -> p k n", k=KD); sq_b = [Buf("sq%d" % i) for i in range(KD)]
        rstd = A.alloc(TT, F32); rstd_b = Buf("rstd")
        fT = sq; fT_b = sq_b
        sv = dview(src); dv = dview(dst)
        ev = [0]
        for b in range(2):
            for tt in range(4):
                t0 = b * S + tt * TT
                x_ = xt[tt % 2]; x_b = xt_b[tt % 2]
                P.dma("sp", [lambda e, x_=x_, t0=t0: e.dma_start(out=x_, in_=sv[:, :, t0:t0 + TT])],
                      reads=[dbuf[src]], writes=[x_b], semkey="xt%d" % (tt % 2))
                rmsnorm(x_, x_b, "nm0", hT, hT_b, sq, sq_b, rstd, rstd_b)
                for blk in range(4):
                    sc = tt * 4 + blk
                    for g in range(4):
                        bk, bk_b = getbank()

                        def mm(e, bk=bk, blk=blk, g=g):
                            ins = None
                            for kk in range(2):
                                ins = e.matmul(bk[:, :], hT[:, 2 * g + kk, blk * 128:(blk + 1) * 128], cs1[:, kk, :],
                                               start=(kk == 0), stop=(kk == 1))
                            return ins
                        P.op("pe", mm, reads=hT_b + [cs1_b], writes=[bk_b])
                        eng = "act" if ev[0] % 2 == 0 else "dve"; ev[0] += 1
                        outv = AB[:, :, sc, g * 256:(g + 1) * 256]
                        inv = bk[:, :].rearrange("p (c n) -> p c n", c=2)
                        if eng == "act":
                            P.op("act", lambda e, outv=outv, inv=inv: e.copy(outv, inv), reads=[bk_b], writes=[AB_b[sc][g]])
                        else:
                            P.op("dve", lambda e, outv=outv, inv=inv: e.tensor_copy(outv, inv), reads=[bk_b], writes=[AB_b[sc][g]])
            for stl in range(4):
                c2 = cs2[stl % 2]; c2_b = cs2_b[stl % 2]
                P.dma("sp", [lambda e, c2=c2, stl=stl: e.dma_start(out=c2, in_=dr["cs2"][stl])],
                      writes=[c2_b], semkey="cs2_%d" % (stl % 2))
                t0 = b * S + stl * TT
                x_ = xt[stl % 2]; x_b = xt_b[stl % 2]
                P.dma("sp", [lambda e, x_=x_, t0=t0: e.dma_start(out=x_, in_=sv[:, :, t0:t0 + TT])],
                      reads=[dbuf[src]], writes=[x_b], semkey="xt%d" % (stl % 2))
                for dc in range(KD):
                    bk, bk_b = getbank()

                    def mm(e, bk=bk, dc=dc, c2=c2):
                        ins = None
                        for sc in range(16):
                            for c in range(2):
                                ins = e.matmul(bk[:, :], AB[:, c, sc, dc * 128:(dc + 1) * 128], c2[:, sc, c, :],
                                               start=(sc == 0 and c == 0), stop=(sc == 15 and c == 1))
                        return ins
                    P.op("pe", mm, reads=AB_all + [c2_b], writes=[bk_b])
                    if dc % 2 == 0:
                        P.op("act", lambda e, bk=bk, dc=dc: e.copy(fT[:, dc, :], bk[:, :]), reads=[bk_b], writes=[fT_b[dc]])
                    else:
                        P.op("dve", lambda e, bk=bk, dc=dc: e.tensor_copy(fT[:, dc, :], bk[:, :]), reads=[bk_b], writes=[fT_b[dc]])
                for do in range(KD):
                    bk, bk_b = getbank()

                    def mm(e, bk=bk, do=do):
                        ins = None
                        for k in range(KD):
                            ins = e.matmul(bk[:, :], Wf[:, k, do * 128:(do + 1) * 128], fT[:, k, :],
                                           start=(k == 0), stop=(k == KD - 1))
                        return ins
                    P.op("pe", mm, reads=fT_b + Wf_b, writes=[bk_b])
                    P.op("dve", lambda e, bk=bk, do=do, x_=x_: e.tensor_tensor(out=x_[:, do, :], in0=x_[:, do, :], in1=bk[:, :], op=ALU.add),
                         reads=[bk_b, x_b], writes=[x_b])
                P.dma("act", [lambda e, x_=x_, t0=t0: e.dma_start(out=dv[:, :, t0:t0 + TT], in_=x_)],
                      reads=[x_b], writes=[dbuf[dst]], semkey="xo%d" % (stl % 2))
        P.barrier()
        A.release(m)

    def phase_ffn(layer, src, dst):
        m = A.mark()
        xt_flat = A.alloc(KD * TT, F32)
        xt = [xt_flat.rearrange("p (k n) -> p k n", k=KD)]
        xt_b = [Buf("xt0")]
        hT = A.alloc(KD * TT, BF16).rearrange("p (k n) -> p k n", k=KD); hT_b = [Buf("hT%d" % i) for i in range(KD)]
        rstd = A.alloc(TT, F32); rstd_b = Buf("rstd")
        actT_w = A.alloc(KF * TT // 2, F32)
        actT = actT_w.bitcast(BF16).rearrange("p (k n) -> p k n", k=KF)
        act_b = [Buf("act%d" % i) for i in range(KF)]
        sq = actT[:, 0:KD, :]; sq_b = act_b[0:KD]
        sg = [A.alloc(TT, F32) for _ in range(2)]; sg_b = [Buf("sg0"), Buf("sg1")]
        st0 = A.alloc(FF, F32); st0_b = [Buf("wst0a"), Buf("wst0b")]
        HW_ = FF // 2
        stage4 = [st0, xt_flat[:, 0:FF], actT_w[:, 0:FF], actT_w[:, FF:2 * FF]]
        stage4_b = [st0_b, [xt_b[0]], act_b[0:11], act_b[11:22]]
        Wg, Wg_b = load_w_bf16(dr["wg"][layer], KD, FF, "wg", stage4, stage4_b)
        Wu, Wu_b = load_w_bf16(dr["wu"][layer], KD, FF, "wu", stage4, stage4_b)
        stageD = [st0[:, 0:D], st0[:, HW_:HW_ + D]]
        stageD_b = [[st0_b[0]], [st0_b[1]]]
        wd_out = []
        wd_gen = load_w_gen(dr["wd"][layer], KF, D, "wd", stageD, stageD_b, wd_out)
        next(wd_gen)
        Wd, Wd_b = wd_out[0]
        sv = dview(src); dv = dview(dst)
        gname = "nf%d" % layer

        def tile_gen(tt):
            t0 = tt * TT
            x_ = xt[0]; x_b = xt_b[0]
            P.dma("sp", [lambda e, x_=x_, t0=t0: e.dma_start(out=x_, in_=sv[:, :, t0:t0 + TT])],
                  reads=[dbuf[src]], writes=[x_b], semkey="xt0")
            rmsnorm(x_, x_b, gname, hT, hT_b, sq, sq_b, rstd, rstd_b)
            yield
            for fc in range(KF):
                bg, bg_b = getbank()
                bu, bu_b = getbank()

                def mmg(e, bk=bg, fc=fc):
                    ins = None
                    for k in range(KD):
                        ins = e.matmul(bk[:, :], Wg[:, k, fc * 128:(fc + 1) * 128], hT[:, k, :], start=(k == 0), stop=(k == KD - 1))
                    return ins

                def mmu(e, bk=bu, fc=fc):
                    ins = None
                    for k in range(KD):
                        ins = e.matmul(bk[:, :], Wu[:, k, fc * 128:(fc + 1) * 128], hT[:, k, :], start=(k == 0), stop=(k == KD - 1))
                    return ins
                P.op("pe", mmg, reads=hT_b + Wg_b, writes=[bg_b])
                P.op("pe", mmu, reads=hT_b + Wu_b, writes=[bu_b])
                s_ = sg[fc % 2]; s_b = sg_b[fc % 2]
                P.op("act", lambda e, s_=s_, bg=bg: e.activation(out=s_, in_=bg[:, :], func=AF.Silu), reads=[bg_b], writes=[s_b])
                P.op("dve", lambda e, s_=s_, bu=bu, fc=fc: e.tensor_tensor(out=actT[:, fc, :], in0=s_, in1=bu[:, :], op=ALU.mult),
                     reads=[s_b, bu_b], writes=[act_b[fc]])
                yield
            for do in range(KD):
                bk, bk_b = getbank()

                def mm(e, bk=bk, do=do):
                    ins = None
                    for fc in range(KF):
                        ins = e.matmul(bk[:, :], Wd[:, fc, do * 128:(do + 1) * 128], actT[:, fc, :], start=(fc == 0), stop=(fc == KF - 1))
                    return ins
                P.op("pe", mm, reads=act_b + Wd_b, writes=[bk_b])
                P.op("dve", lambda e, bk=bk, do=do, x_=x_: e.tensor_tensor(out=x_[:, do, :], in0=x_[:, do, :], in1=bk[:, :], op=ALU.add),
                     reads=[bk_b, x_b], writes=[x_b])
            P.dma("act", [lambda e, x_=x_, t0=t0: e.dma_start(out=dv[:, :, t0:t0 + TT], in_=x_)],
                  reads=[x_b], writes=[dbuf[dst]], semkey="xo0")
            yield

        drive([wd_gen, tile_gen(0)])
        for tt in range(1, NTT):
            drive([tile_gen(tt)])
        P.barrier()
        A.release(m)

    def phase_final(src):
        m = A.mark()
        xt = [A.alloc(KD * TT, F32).rearrange("p (k n) -> p k n", k=KD) for _ in range(2)]
        xt_b = [Buf("xt0"), Buf("xt1")]
        ot = [A.alloc(KD * TT, F32).rearrange("p (k n) -> p k n", k=KD) for _ in range(2)]
        ot_b = [[Buf("ot0_%d" % i) for i in range(KD)], [Buf("ot1_%d" % i) for i in range(KD)]]
        sq = A.alloc(KD * TT, BF16).rearrange("p (k n) -> p k n", k=KD); sq_b = [Buf("sq%d" % i) for i in range(KD)]
        rstd = A.alloc(TT, F32); rstd_b = Buf("rstd")
        sv = dview(src); dv = dview("outT")
        for tt in range(NTT):
            t0 = tt * TT
            x_ = xt[tt % 2]; x_b = xt_b[tt % 2]
            P.dma("sp", [lambda e, x_=x_, t0=t0: e.dma_start(out=x_, in_=sv[:, :, t0:t0 + TT])],
                  reads=[dbuf[src]], writes=[x_b], semkey="xt%d" % (tt % 2))
            rmsnorm(x_, x_b, "nfin", ot[tt % 2], ot_b[tt % 2], sq, sq_b, rstd, rstd_b)
            P.dma("act", [lambda e, o_=ot[tt % 2], t0=t0: e.dma_start(out=dv[:, :, t0:t0 + TT], in_=o_)],
                  reads=ot_b[tt % 2], writes=[dbuf["outT"]], semkey="xo%d" % (tt % 2))
        P.barrier()
        A.release(m)

    dbuf = {n: Buf("dram_" + n) for n in ("xT", "xa", "xb", "outT")}
    cur = "xT"
    ctx = Ctx()
    ctx.__dict__.update(locals())
    if "l0mix" in phases:
        phase_fourier(cur, "xa"); cur = "xa"
    if "ffn0" in phases:
        phase_ffn(0, cur, "xb"); cur = "xb"
    if "l1mix" in phases:
        nxt = "xb" if cur == "xa" else "xa"
        phase_rwkv(ctx, cur, nxt, sub=rwkv_sub); cur = nxt
    if "ffn1" in phases:
        nxt = "xb" if cur == "xa" else "xa"
        phase_ffn(1, cur, nxt); cur = nxt
    if "final" in phases:
        phase_final(cur)
    P.barrier()
    stats = P.finalize()
    st.close()
    return nc, stats


def host_consts():
    c = np.arange(256)
    ang1 = 2 * np.pi * ((c[:, None] * c[None, :]) % 256) / 256.0
    cs1 = np.concatenate([np.cos(ang1), np.sin(ang1)], axis=1) / 16.0
    cs1 = cs1.reshape(2, 128, 512).transpose(1, 0, 2)
    s = np.arange(S)
    ang2 = 2 * np.pi * ((s[:, None] * s[None, :]) % S) / float(S)
    C2 = np.cos(ang2) / np.sqrt(S); S2 = -np.sin(ang2) / np.sqrt(S)
    cs2 = np.stack([C2, S2], axis=1)
    cs2 = cs2.reshape(16, 128, 2, 4, 512).transpose(3, 1, 0, 2, 4)
    return (np.ascontiguousarray(cs1).astype(ml_dtypes.bfloat16),
            np.ascontiguousarray(cs2).astype(ml_dtypes.bfloat16))


def pack_vecs(inp):
    vs = {
        "nm0": inp["norm_mix_g"][0], "nm1": inp["norm_mix_g"][1],
        "nf0": inp["norm_ffn_g"][0], "nf1": inp["norm_ffn_g"][1], "nfin": inp["norm_final_g"],
        "w0_0": inp["rwkv_w0"][0, 0], "w0_1": inp["rwkv_w0"][0, 1],
        "a0_0": inp["rwkv_a0"][0, 0], "a0_1": inp["rwkv_a0"][0, 1],
        "k_k": inp["rwkv_k_k"][0], "k_a": inp["rwkv_k_a"][0], "r_k": inp["rwkv_r_k"][0].reshape(-1),
        "ln_w": inp["rwkv_ln_w"][0], "ln_b": inp["rwkv_ln_b"][0],
    }
    for i in range(6):
        vs["mu%d" % i] = inp["rwkv_mu"][0, i]
    out = np.zeros((128, NV * 8), np.float32)
    for n, i in VI.items():
        out[:, i * 8:(i + 1) * 8] = np.asarray(vs[n], np.float32).reshape(8, 128).T
    return out


_CONSTS = None


def make_inmap(inp, core):
    global _CONSTS
    if _CONSTS is None:
        cs1, cs2 = host_consts()
        rows = np.stack([np.asarray(x, np.float32).reshape(-1) for x in (
            inp["rwkv_k_k"][0], inp["rwkv_k_a"][0], inp["rwkv_w0"][0, 0], inp["rwkv_w0"][0, 1], inp["rwkv_a0"][0, 0],
            inp["rwkv_a0"][0, 1], inp["rwkv_r_k"][0], inp["rwkv_ln_w"][0], inp["rwkv_ln_b"][0])])
        rc = rwkv_consts()
        _CONSTS = dict(cmask=rc["cmask"], ctri=rc["ctri"], cind=rc["cind"], cid2=rc["cid2"], cs1=cs1, cs2=cs2, rows=rows, ident=np.eye(128, dtype=np.float32).astype(ml_dtypes.bfloat16),
                       vecs=pack_vecs(inp))
    c = _CONSTS
    x = inp["x"][2 * core:2 * core + 2]
    return {"xT": np.ascontiguousarray(x.reshape(T, D).T), "vecs": c["vecs"],
            "fno_w": inp["fno_w_out"][0], "wg": inp["ffn_w_gate"], "wu": inp["ffn_w_up"], "wd": inp["ffn_w_down"],
            "cs1": c["cs1"], "cs2": c["cs2"],
            "w_rkv": inp["rwkv_w_rkv"][0], "w_o": inp["rwkv_w_o"][0], "w1": inp["rwkv_w1"][0], "w2": inp["rwkv_w2"][0],
            "a1": inp["rwkv_a1"][0], "a2": inp["rwkv_a2"][0], "g1": inp["rwkv_g1"][0], "g2": inp["rwkv_g2"][0],
            "rows": c["rows"], "ident": c["ident"], "cmask": c["cmask"], "ctri": c["ctri"], "cind": c["cind"], "cid2": c["cid2"]}


def kernel(**inputs):
    inp = {k: np.asarray(v) for k, v in inputs.items()}
    nc, _ = build()
    in_maps = [make_inmap(inp, c) for c in range(8)]
    res = run_bass_kernel_spmd(nc, in_maps, core_ids=list(range(8)))
    out = np.empty((16, S, D), np.float32)
    for c in range(8):
        out[2 * c:2 * c + 2] = np.asarray(res.results[c]["outT"]).T.reshape(2, S, D)
    return out
```

```python
import numpy as np
import ml_dtypes
from contextlib import ExitStack
import concourse.bass as bass
import concourse.mybir as mybir
from concourse.bass_utils import run_bass_kernel_spmd

ENG_EPOCH = 20000


class Buf:
    __slots__ = ("name", "w", "r")

    def __init__(self, name):
        self.name = name
        self.w = []
        self.r = []


class Op:
    __slots__ = ("eng", "emit", "deps", "sig", "is_dma", "semkey", "ndma", "tok", "idx")


class Prog:
    def __init__(self, nc, stack):
        self.nc = nc
        self.stack = stack
        self.ops = []
        self.engs = {"pe": nc.tensor, "dve": nc.vector, "act": nc.scalar, "pool": nc.gpsimd, "sp": nc.sync}
        self.last_dma = {}

    def _record(self, op, reads, writes):
        deps = []
        raw = set()
        for b in reads:
            deps.extend(b.w)
            for d in b.w:
                raw.add(id(d))
        for b in writes:
            deps.extend(b.w)
            deps.extend(b.r)
        out = []
        seen = set()
        for d in deps:
            if id(d) in seen or d is op:
                continue
            seen.add(id(d))
            if (not d.is_dma) and d.eng == op.eng and not op.is_dma:
                if op.eng == "pe":
                    continue
            out.append(d)
        op.deps = out
        for d in out:
            d.sig = True
        for b in reads:
            b.r.append(op)
        for b in writes:
            if b.r:
                b.w = [op]
            else:
                b.w = b.w + [op]
            b.r = []
        op.idx = len(self.ops)
        self.ops.append(op)
        return op

    def op(self, eng, emit, reads=(), writes=()):
        o = Op()
        o.eng = eng; o.emit = emit; o.sig = False; o.is_dma = False
        o.semkey = None; o.ndma = 0; o.tok = None
        return self._record(o, list(reads), list(writes))

    def dma(self, eng, emits, reads=(), writes=(), semkey=None):
        o = Op()
        o.eng = eng; o.emit = emits; o.sig = True; o.is_dma = True
        o.semkey = semkey; o.ndma = len(emits); o.tok = None
        prev = self.last_dma.get(semkey)
        self._record(o, list(reads), list(writes))
        if prev is not None and prev not in o.deps:
            o.deps.append(prev)
        self.last_dma[semkey] = o
        return o

    def barrier(self, bufs=()):
        last = {}
        for o in self.ops:
            if o.emit is None:
                continue
            key = ("dma", o.semkey) if o.is_dma else ("eng", o.eng)
            last[key] = o
        deps = list(last.values())
        for e in ("pe", "dve", "act", "pool", "sp"):
            o = Op()
            o.eng = e; o.emit = None; o.sig = False; o.is_dma = False
            o.semkey = None; o.ndma = 0; o.tok = None
            o.deps = [d for d in deps]
            for d in o.deps:
                d.sig = True
            o.idx = len(self.ops)
            self.ops.append(o)

    def finalize(self):
        nc = self.nc
        eng_cnt = {}
        eng_sems = {}
        dma_sems = {}
        dma_cnt = {}

        def eng_sem(e, epoch):
            k = (e, epoch)
            if k not in eng_sems:
                eng_sems[k] = self.stack.enter_context(nc.semaphore("s_%s_%d" % (e, epoch)))
            return eng_sems[k]

        for o in self.ops:
            if o.is_dma:
                if o.semkey not in dma_sems:
                    dma_sems[o.semkey] = self.stack.enter_context(nc.semaphore("d_%s" % (o.semkey,)))
                    dma_cnt[o.semkey] = 0
                dma_cnt[o.semkey] += 16 * o.ndma
                o.tok = (dma_sems[o.semkey], dma_cnt[o.semkey], ("d", o.semkey))
            elif o.sig:
                c = eng_cnt.get(o.eng, 0) + 1
                eng_cnt[o.eng] = c
                epoch = (c - 1) // ENG_EPOCH
                o.tok = (eng_sem(o.eng, epoch), c - epoch * ENG_EPOCH, ("e", o.eng, epoch))
        known = {e: {} for e in self.engs}
        nwait = 0
        for o in self.ops:
            E = self.engs[o.eng]
            kn = known[o.eng]
            need = {}
            for d in o.deps:
                sem, val, key = d.tok
                if kn.get(key, 0) >= val:
                    continue
                if key not in need or need[key][1] < val:
                    need[key] = (sem, val)
            for key, (sem, val) in need.items():
                E.wait_ge(sem, val)
                kn[key] = val
                nwait += 1
            if o.emit is None:
                continue
            if o.is_dma:
                sem = o.tok[0]
                for f in o.emit:
                    f(E).then_inc(sem, 16)
            else:
                ins = o.emit(E)
                if o.sig:
                    ins.then_inc(o.tok[0], 1)
        self.stats = dict(nops=len(self.ops), nwait=nwait, nsem=len(eng_sems) + len(dma_sems))
        return self.stats


class Arena:
    def __init__(self, base_ap, words):
        self.base = base_ap
        self.words = words
        self.top = 0

    def mark(self):
        return self.top

    def release(self, m):
        self.top = m

    def alloc(self, nelem, dtype, parts=128):
        bpe = 2 if dtype == mybir.dt.bfloat16 else 4
        nw = (nelem * bpe + 3) // 4
        nw = (nw + 7) // 8 * 8
        assert self.top + nw <= self.words, "SBUF arena overflow: %d + %d > %d" % (self.top, nw, self.words)
        ap = self.base[0:parts, self.top:self.top + nw]
        self.top += nw
        if dtype != mybir.dt.float32:
            ap = ap.bitcast(dtype)
        return ap[:, 0:nelem]


F32 = mybir.dt.float32
BF16 = mybir.dt.bfloat16
ALU = mybir.AluOpType
AF = mybir.ActivationFunctionType
AX = mybir.AxisListType

D = 1024; KD = 8; T = 4096; S = 2048
TP = 128
NBLK = T // 128
CDEC = float(np.exp(-0.5))
DBG_LIMIT = None
DBG_ITERS = None
ROWS = ["k_k", "k_a", "w0_0", "w0_1", "a0_0", "a0_1", "r_k", "ln_w", "ln_b"]
RI = {n: i for i, n in enumerate(ROWS)}
SCR_F32 = ["s_bonus", "s_g", "s_y0", "s_y1", "s_r32", "s_k32", "s_v32"]
SCR_TOK = ["s_v", "s_ka0", "s_ka1", "s_kh0", "s_kh1", "s_bh0", "s_bh1"]
SCR_CH = ["c_kt0", "c_kt1", "c_rt0", "c_rt1", "c_ktl0", "c_ktl1", "c_bt0", "c_bt1"]


def declare(ctx_dr, drh, nc, din, debug):
    din("w_rkv", [3, D, D]); din("w_o", [D, D])
    din("w1", [2, D, 64]); din("w2", [2, 64, D]); din("a1", [2, D, 64]); din("a2", [2, 64, D])
    din("g1", [D, 128]); din("g2", [128, D]); din("rows", [len(ROWS), D]); din("ident", [128, 128], BF16)
    din("cmask", [128, 2, 5, 128]); din("ctri", [128, 2, 3, 128], BF16); din("cind", [128, 2], BF16); din("cid2", [128, 64])
    for n in SCR_F32:
        h = nc.dram_tensor(n, [T, D], F32, kind=("ExternalOutput" if debug else "Internal")); drh[n] = h; ctx_dr[n] = h.ap()
    for n in SCR_TOK:
        h = nc.dram_tensor(n, [T, D], BF16, kind="Internal"); drh[n] = h; ctx_dr[n] = h.ap()
    for n in SCR_CH:
        h = nc.dram_tensor(n, [D, T], BF16, kind="Internal"); drh[n] = h; ctx_dr[n] = h.ap()
    h = nc.dram_tensor("c_lt", [3, 128, T], BF16, kind="Internal"); drh["c_lt"] = h; ctx_dr["c_lt"] = h.ap()


def rwkv_consts():
    import ml_dtypes
    s = np.arange(128)[:, None]; t = np.arange(128)[None, :]
    same = (s // 64) == (t // 64)
    cmask = np.zeros((128, 2, 5, 128), np.float32)
    ctri = np.zeros((128, 2, 3, 128), np.float32)
    for d in range(2):
        rs = (s < t) if d == 0 else (s > t)
        ri = (s <= t) if d == 0 else (s >= t)
        ro = (s > t) if d == 0 else (s < t)
        cmask[:, d, 0, :] = -1.0 * (rs & same); cmask[:, d, 1, :] = (ri & same)
        cmask[:, d, 2, :] = (rs & same); cmask[:, d, 3, :] = (ri & same)
        cmask[:, d, 4, :] = -1.0 * ((rs & same).T)
        ctri[:, d, 0, :] = (ri & same); ctri[:, d, 1, :] = (rs & same); ctri[:, d, 2, :] = (ro & same)
    cind = np.zeros((128, 2), np.float32); cind[:64, 0] = 1; cind[64:, 1] = 1
    cid2 = np.zeros((128, 64), np.float32); cid2[np.arange(128), np.arange(128) % 64] = 1
    return dict(cmask=cmask, ctri=ctri.astype(ml_dtypes.bfloat16), cind=cind.astype(ml_dtypes.bfloat16), cid2=cid2)


def phase_rwkv(c, src, dst, sub=("prep", "scan", "post")):
    P = c.P; A = c.A; dr = c.dr; dbuf = c.dbuf; getbank = c.getbank; vcol = c.vcol
    vecs_b = c.vecs_b; ones_b = c.ones_b; onesD = c.onesD; drh = c.drh
    for n in SCR_F32 + SCR_TOK + SCR_CH + ["c_lt"]:
        if n not in dbuf:
            dbuf[n] = Buf("dram_" + n)
    sv = dr[src].rearrange("(k p) t -> p k t", p=128)
    dv = dr[dst].rearrange("(k p) t -> p k t", p=128)

    def alloc3(k, n, dt):
        return A.alloc(k * n, dt).rearrange("p (k n) -> p k n", k=k)

    def load_rows(names):
        out = {}
        for n in names:
            ap = A.alloc(D, F32); b = Buf("row_" + n)
            P.dma("sp", [lambda e, ap=ap, n=n: e.dma_start(out=ap, in_=dr["rows"][RI[n]].partition_broadcast(128))],
                  writes=[b], semkey="row_" + n)
            out[n] = (ap, b)
        return out

    def h3(ap):
        return ap.rearrange("p (h n) -> p h n", h=16)

    def cload(name, nelem, dt, shape_str=None, **kw):
        ap = A.alloc(nelem, dt); b = Buf("c_" + name)
        src_ap = dr[name]
        P.dma("sp", [lambda e: e.dma_start(out=ap, in_=src_ap.rearrange(shape_str, **kw) if shape_str else src_ap)], writes=[b], semkey="c_" + name)
        return ap, b

    PC = [A.alloc(8 * 64, F32).rearrange("p (j c) -> p j c", j=8) for _ in range(2)]
    PC_b = [Buf("pc0"), Buf("pc1")]
    ident = A.alloc(128, BF16); ident_b = Buf("ident")
    P.dma("sp", [lambda e: e.dma_start(out=ident, in_=dr["ident"])], writes=[ident_b], semkey="ident")

    def prep():
        m = A.mark()
        stages = [(A.alloc(D, F32), Buf("wst%d" % i)) for i in range(2)]
        srr = [0]

        def stage_cast(src_ap, dst_ap, n_, dst_buf, view=None):
            i = srr[0] % 2; srr[0] += 1
            stg, stg_b = stages[i]
            sview = stg[:, 0:n_] if view is None else view(stg[:, 0:n_])
            wb = dst_buf if isinstance(dst_buf, list) else [dst_buf]
            P.dma("sp", [lambda e: e.dma_start(out=sview, in_=src_ap)], writes=[stg_b], semkey="wst%d" % i)
            if (srr[0] // 2) % 2 == 0:
                P.op("dve", lambda e: e.tensor_copy(dst_ap, sview), reads=[stg_b], writes=wb)
            else:
                P.op("act", lambda e: e.copy(dst_ap, sview), reads=[stg_b], writes=wb)

        def load_w(w2d, K, N, tag, npart=128):
            dstw = A.alloc(K * N, BF16).rearrange("p (k n) -> p k n", k=K)
            bufs = [Buf("%s_%d" % (tag, k)) for k in range(K)]
            wv = w2d.rearrange("(k p) n -> p k n", p=npart)
            for k in range(K):
                stage_cast(wv[:, k, :], dstw[:, k, :], N, bufs[k])
            return dstw, bufs
        Wr, Wr_b = load_w(dr["w_rkv"][0], KD, D, "wr")
        Wk, Wk_b = load_w(dr["w_rkv"][1], KD, D, "wk")
        Wv, Wv_b = load_w(dr["w_rkv"][2], KD, D, "wv")
        w1c = A.alloc(KD * 128, BF16).rearrange("p (k n) -> p k n", k=KD); w1c_b = [Buf("w1c%d" % k) for k in range(KD)]
        a1c = A.alloc(KD * 128, BF16).rearrange("p (k n) -> p k n", k=KD); a1c_b = [Buf("a1c%d" % k) for k in range(KD)]
        for (dstw, bufs, name) in ((w1c, w1c_b, "w1"), (a1c, a1c_b, "a1")):
            for j in range(2):
                wv = dr[name][j].rearrange("(k p) n -> p k n", p=128)
                stage_cast(wv, dstw[:, :, j * 64:(j + 1) * 64], 512, bufs, view=lambda ap: ap.rearrange("p (k n) -> p k n", k=KD))
        G1 = A.alloc(KD * 128, BF16).rearrange("p (k n) -> p k n", k=KD); G1_b = [Buf("g1_%d" % k) for k in range(KD)]
        stage_cast(dr["g1"].rearrange("(k p) n -> p k n", p=128), G1, 1024, G1_b, view=lambda ap: ap.rearrange("p (k n) -> p k n", k=KD))
        NH = TP + 2
        FS = []
        for i in range(3):
            FS.append({"xt": (alloc3(KD, NH, F32), Buf("xt%d" % i)), "hf": (alloc3(KD, NH, F32), Buf("hf%d" % i)),
                       "sq": (alloc3(KD, NH, BF16), Buf("sq%d" % i)), "rstd": (A.alloc(NH, F32), Buf("rstd%d" % i)),
                       "xx": (alloc3(KD, TP, F32), Buf("xx%d" % i)), "tmp": (alloc3(KD, TP, F32), Buf("tmp%d" % i)),
                       "xs": [alloc3(KD, TP, BF16) for _ in range(6)], "xs_b": [Buf("xs%d_%d" % (i, c_)) for c_ in range(6)]})
        W2 = [{n: (A.alloc(D, F32), Buf("%s_%d" % (n, i))) for n in ("tr", "tk", "tv")} for i in range(3)]
        LT = [{n: (A.alloc(TP, BF16), Buf("%s_%d" % (n, i))) for n in ("twT", "taT", "sgT")} for i in range(3)]

        def fe(ti):
            t0 = ti * TP
            ss_ = ti % 3
            F_ = FS[ss_]
            xt, xt_b = F_["xt"]; hf, hf_b = F_["hf"]; sq, sq_b = F_["sq"]; rstd, rstd_b = F_["rstd"]
            xx, xx_b = F_["xx"]; tmp, tmp_b = F_["tmp"]; xs = F_["xs"]; xs_b = F_["xs_b"]
            first = (t0 % S == 0); last = ((t0 + TP) % S == 0)
            lo_ = 1 if first else 0; hi_ = NH - 1 if last else NH
            P.dma("sp", [lambda e, t0=t0, lo_=lo_, hi_=hi_: e.dma_start(out=xt[:, :, lo_:hi_], in_=sv[:, :, t0 - 1 + lo_:t0 - 1 + hi_])],
                  reads=[dbuf[src]], writes=[xt_b], semkey="xt%d" % ss_)
            if first:
                P.op("pool", lambda e: e.memset(xt[:, :, 0:1], 0.0), writes=[xt_b])
            if last:
                P.op("pool", lambda e: e.memset(xt[:, :, NH - 1:NH], 0.0), writes=[xt_b])
            P.op("act", lambda e: e.activation(out=sq, in_=xt, func=AF.Square), reads=[xt_b], writes=[sq_b])
            bk, bk_b = getbank()

            def mm(e, bk=bk):
                ins = None
                for k in range(KD):
                    ins = e.matmul(bk[:, 0:NH], onesD[:, :], sq[:, k, :], start=(k == 0), stop=(k == KD - 1))
                return ins
            P.op("pe", mm, reads=[sq_b, ones_b], writes=[bk_b])
            P.op("act", lambda e, bk=bk: e.activation(out=rstd, in_=bk[:, 0:NH], func=AF.Sqrt, bias=1e-6, scale=1.0),
                 reads=[bk_b], writes=[rstd_b])
            P.op("dve", lambda e: e.reciprocal(rstd, rstd), reads=[rstd_b], writes=[rstd_b])
            yield
            for k in range(KD):
                P.op("dve", lambda e, k=k: e.scalar_tensor_tensor(out=hf[:, k, :], in0=xt[:, k, :], scalar=vcol("nm1", k),
                                                                  in1=rstd, op0=ALU.mult, op1=ALU.mult),
                     reads=[xt_b, rstd_b, vecs_b], writes=[hf_b])
            P.op("pool", lambda e: e.tensor_tensor(out=tmp, in0=hf[:, :, 0:TP], in1=hf[:, :, 2:TP + 2], op=ALU.add),
                 reads=[hf_b], writes=[tmp_b])
            P.op("dve", lambda e: e.scalar_tensor_tensor(out=xx, in0=tmp, scalar=0.5, in1=hf[:, :, 1:TP + 1],
                                                         op0=ALU.mult, op1=ALU.subtract),
                 reads=[tmp_b, hf_b], writes=[xx_b])
            yield
            for ci in (3, 4, 5, 0, 1, 2):
                for k in range(KD):
                    P.op("dve", lambda e, ci=ci, k=k: e.scalar_tensor_tensor(
                        out=xs[ci][:, k, :], in0=xx[:, k, :], scalar=vcol("mu%d" % ci, k), in1=hf[:, k, 1:TP + 1],
                        op0=ALU.mult, op1=ALU.add), reads=[xx_b, hf_b, vecs_b], writes=[xs_b[ci]])
                yield
            for (wc, wc_b, xi, oname, func) in ((w1c, w1c_b, 3, "twT", AF.Tanh), (a1c, a1c_b, 4, "taT", None), (G1, G1_b, 5, "sgT", AF.Sigmoid)):
                outT, outT_b = LT[ss_][oname]
                bk, bk_b = getbank()

                def mm(e, bk=bk, wc=wc, xi=xi):
                    ins = None
                    for k in range(KD):
                        ins = e.matmul(bk[:, 0:TP], wc[:, k, :], xs[xi][:, k, :], start=(k == 0), stop=(k == KD - 1))
                    return ins
                P.op("pe", mm, reads=[xs_b[xi]] + wc_b, writes=[bk_b])
                if func is None:
                    P.op("act", lambda e, bk=bk, outT=outT: e.copy(outT, bk[:, 0:TP]), reads=[bk_b], writes=[outT_b])
                else:
                    P.op("act", lambda e, bk=bk, outT=outT, func=func: e.activation(out=outT, in_=bk[:, 0:TP], func=func),
                         reads=[bk_b], writes=[outT_b])
            yield
            for (xi, Wm, Wm_b, tile) in ((0, Wr, Wr_b, "tr"), (1, Wk, Wk_b, "tk"), (2, Wv, Wv_b, "tv")):
                ap, b = W2[ss_][tile]
                for half in range(2):
                    bk, bk_b = getbank()

                    def mm(e, bk=bk, half=half, xi=xi, Wm=Wm):
                        ins = None
                        for k in range(KD):
                            ins = e.matmul(bk[:, :], xs[xi][:, k, :], Wm[:, k, half * 512:(half + 1) * 512],
                                           start=(k == 0), stop=(k == KD - 1))
                        return ins
                    P.op("pe", mm, reads=[xs_b[xi]] + Wm_b, writes=[bk_b])
                    P.op("act", lambda e, bk=bk, half=half, ap=ap: e.copy(ap[:, half * 512:(half + 1) * 512], bk[:, :]),
                         reads=[bk_b], writes=[b])
                yield
            for (tile, scr) in (("tr", "s_r32"), ("tk", "s_k32"), ("tv", "s_v32")):
                ap, b = W2[ss_][tile]
                P.dma("act", [lambda e, ap=ap, scr=scr, t0=t0: e.dma_start(out=dr[scr][t0:t0 + 128, :], in_=ap)],
                      reads=[b], writes=[dbuf[scr]], semkey="fst_%s_%d" % (tile, ss_))
            for li, oname in enumerate(("twT", "taT", "sgT")):
                ap, b = LT[ss_][oname]
                P.dma("act", [lambda e, ap=ap, li=li, t0=t0: e.dma_start(out=dr["c_lt"][li, :, t0:t0 + 128], in_=ap)],
                      reads=[b], writes=[dbuf["c_lt"]], semkey="fst_%s_%d" % (oname, ss_))
            yield


        def drive_window(make_gen, n, width):
            active = []; nxt = 0
            while nxt < n or active:
                while len(active) < width and nxt < n:
                    active.append(make_gen(nxt)); nxt += 1
                for g in list(active):
                    try:
                        next(g)
                    except StopIteration:
                        active.remove(g)

        nb_ = NBLK if DBG_LIMIT is None else DBG_LIMIT
        drive_window(fe, nb_, 3)
        P.barrier()
        A.release(m)

        m = A.mark()
        stages = [(A.alloc(D, F32), Buf("wsu%d" % i)) for i in range(2)]
        srr[0] = 0
        w2c = A.alloc(D, BF16); w2c_b = Buf("w2c")
        a2c = A.alloc(D, BF16); a2c_b = Buf("a2c")
        G2 = A.alloc(D, BF16); G2_b = Buf("g2")
        for (dstw, b, srcap) in ((w2c, w2c_b, dr["w2"].rearrange("j r n -> (j r) n")),
                                 (a2c, a2c_b, dr["a2"].rearrange("j r n -> (j r) n")), (G2, G2_b, dr["g2"])):
            stage_cast(srcap, dstw, D, b)
        rows = load_rows(["k_k", "k_a", "w0_0", "w0_1", "a0_0", "a0_1", "r_k"])
        ctri_ap, ctri_b = cload("ctri", 2 * 3 * 128, BF16, "p d m t -> p (d m t)")
        ctri = ctri_ap.rearrange("p (d m t) -> p d m t", d=2, m=3)
        cind, cind_b = cload("cind", 2, BF16)
        NCH = 2
        CS = []
        for i in range(NCH):
            names = ["tr", "tk", "tv", "tkap", "ta", "tw", "tx", "ty", "te0", "te1"]
            CS.append({"W": {n: (A.alloc(D, F32), Buf("%s_c%d" % (n, i))) for n in names},
                       "twT": (A.alloc(TP, BF16), Buf("twT_c%d" % i)), "taT": (A.alloc(TP, BF16), Buf("taT_c%d" % i)),
                       "sgT": (A.alloc(TP, BF16), Buf("sgT_c%d" % i)),
                       "hi": (A.alloc(D, BF16), Buf("hi_c%d" % i)), "lo": (A.alloc(D, BF16), Buf("lo_c%d" % i)),
                       "ss": (A.alloc(16, F32), Buf("ss_c%d" % i)),
                       "rk": [A.alloc(16, F32) for _ in range(2)], "rk_b": [Buf("rk0_c%d" % i), Buf("rk1_c%d" % i)], "terr": [0]})
        NOB = 8; NOC = 6
        OB = [(A.alloc(D, BF16), Buf("ob%d" % i)) for i in range(NOB)]
        OC = [(A.alloc(D, BF16), Buf("oc%d" % i)) for i in range(NOC)]
        rrc = {"ob": 0, "oc": 0}

        def be(ti):
            tb0 = ti * TP
            ss_ = ti % NCH
            C_ = CS[ss_]
            W = C_["W"]
            tr, tr_b = W["tr"]; tk, tk_b = W["tk"]; tv, tv_b = W["tv"]
            twT, twT_b = C_["twT"]; taT, taT_b = C_["taT"]; sgT, sgT_b = C_["sgT"]
            hi, hi_b = C_["hi"]; lo, lo_b = C_["lo"]; ss, ss_b = C_["ss"]; rk = C_["rk"]; rk_b = C_["rk_b"]
            Wl = W
            for (tile, scr) in (("tr", "s_r32"), ("tk", "s_k32"), ("tv", "s_v32")):
                ap, b = W[tile]
                P.dma("sp", [lambda e, ap=ap, scr=scr, tb0=tb0: e.dma_start(out=ap, in_=dr[scr][tb0:tb0 + 128, :])],
                      reads=[dbuf[scr]], writes=[b], semkey="bld_%s_%d" % (tile, ss_))
            for li, (ap, b) in enumerate(((twT, twT_b), (taT, taT_b), (sgT, sgT_b))):
                P.dma("sp", [lambda e, ap=ap, li=li, tb0=tb0: e.dma_start(out=ap, in_=dr["c_lt"][li, :, tb0:tb0 + 128])],
                      reads=[dbuf["c_lt"]], writes=[b], semkey="bld_lt%d_%d" % (li, ss_))
            yield

            def st(ap, b, scr, key):
                P.dma("act", [lambda e, ap=ap, scr=scr, tb0=tb0: e.dma_start(out=dr[scr][tb0:tb0 + 128, :], in_=ap)],
                      reads=[b], writes=[dbuf[scr]], semkey="st_" + key)

            def getob():
                i = rrc["ob"] % NOB; rrc["ob"] += 1
                return OB[i] + ("ob%d" % i,)

            def emit_tok(srcname, te, te_b, scr):
                ap, b, key = getob()
                s_ap, s_b = Wl[srcname]
                eng = "pool" if rrc["ob"] % 4 == 0 else "dve"
                P.op(eng, lambda e, ap=ap, s_ap=s_ap, te=te: e.tensor_tensor(out=ap, in0=s_ap, in1=te, op=ALU.mult),
                     reads=[s_b, te_b], writes=[b])
                st(ap, b, scr, key)

            def emit_ch(srcname, te, te_b, scr):
                ap, b, key = getob()
                s_ap, s_b = Wl[srcname]
                eng = "pool" if rrc["ob"] % 4 == 0 else "dve"
                P.op(eng, lambda e, ap=ap, s_ap=s_ap, te=te: e.tensor_tensor(out=ap, in0=s_ap, in1=te, op=ALU.mult),
                     reads=[s_b, te_b], writes=[b])
                bk, bk_b = getbank()
                bkb = bk[:, :].bitcast(BF16)

                def trp(e, bkb=bkb, ap=ap):
                    ins = None
                    for q in range(8):
                        ins = e.transpose(bkb[:, q * 128:(q + 1) * 128], ap[:, q * 128:(q + 1) * 128], ident[:, :])
                    return ins
                P.op("pe", trp, reads=[b, ident_b], writes=[bk_b])
                i = rrc["oc"] % NOC; rrc["oc"] += 1
                oc, oc_b = OC[i]
                P.op("act", lambda e, oc=oc, bkb=bkb: e.copy(oc, bkb), reads=[bk_b], writes=[oc_b])
                P.dma("act", [lambda e, oc=oc, scr=scr, tb0=tb0: e.dma_start(
                    out=dr[scr].rearrange("(j p) t -> p j t", p=128)[:, :, tb0:tb0 + 128], in_=oc.rearrange("p (j t) -> p j t", j=8))],
                    reads=[oc_b], writes=[dbuf[scr]], semkey="stc%d" % i)

            tkap, tkap_b = W["tkap"]
            tx, tx_b = W["tx"]; ty, ty_b = W["ty"]; ta, ta_b = W["ta"]; tw, tw_b = W["tw"]
            vb, vb_b, vkey = getob()
            P.op("act", lambda e, vb=vb: e.copy(vb, tv), reads=[tv_b], writes=[vb_b])
            st(vb, vb_b, "s_v", vkey)
            P.op("dve", lambda e: e.tensor_tensor(out=tx, in0=tk, in1=rows["k_k"][0], op=ALU.mult),
                 reads=[tk_b, rows["k_k"][1]], writes=[tx_b])
            P.op("act", lambda e: e.activation(out=ty, in_=tx, func=AF.Square), reads=[tx_b], writes=[ty_b])
            P.op("dve", lambda e: e.tensor_reduce(out=ss, in_=h3(ty), axis=AX.X, op=ALU.add), reads=[ty_b], writes=[ss_b])
            P.op("dve", lambda e: e.tensor_scalar(ss, ss, 1e-24, None, ALU.max), reads=[ss_b], writes=[ss_b])
            P.op("act", lambda e: e.activation(out=ss, in_=ss, func=AF.Sqrt), reads=[ss_b], writes=[ss_b])
            P.op("dve", lambda e: e.reciprocal(ss, ss), reads=[ss_b], writes=[ss_b])
            P.op("dve", lambda e: e.tensor_tensor(out=h3(tkap), in0=h3(tx), in1=ss.unsqueeze(2).to_broadcast([128, 16, 64]), op=ALU.mult),
                 reads=[tx_b, ss_b], writes=[tkap_b])
            yield
            for j in range(2):
                for (cT, cT_b, c2, c2_b, dstt, dstt_b, rown) in ((twT, twT_b, w2c, w2c_b, tw, tw_b, "w0_%d" % j),
                                                                (taT, taT_b, a2c, a2c_b, ta, ta_b, "a0_%d" % j)):
                    for half in range(2):
                        bk, bk_b = getbank()
                        P.op("pe", lambda e, bk=bk, cT=cT, c2=c2, half=half, j=j: e.matmul(
                            bk[:, :], cT[j * 64:(j + 1) * 64, :], c2[j * 64:(j + 1) * 64, half * 512:(half + 1) * 512],
                            start=True, stop=True), reads=[cT_b, c2_b], writes=[bk_b])
                        P.op("dve", lambda e, bk=bk, dstt=dstt, half=half, rown=rown: e.tensor_tensor(
                            out=dstt[:, half * 512:(half + 1) * 512], in0=bk[:, :], in1=rows[rown][0][:, half * 512:(half + 1) * 512],
                            op=ALU.add), reads=[bk_b, rows[rown][1]], writes=[dstt_b])
                P.op("act", lambda e: e.activation(out=tw, in_=tw, func=AF.Sigmoid), reads=[tw_b], writes=[tw_b])
                P.op("act", lambda e: e.activation(out=ta, in_=ta, func=AF.Sigmoid), reads=[ta_b], writes=[ta_b])
                yield
                P.op("pool", lambda e: e.tensor_tensor(out=tx, in0=tkap, in1=ta, op=ALU.mult), reads=[tkap_b, ta_b], writes=[tx_b])
                P.op("dve", lambda e: e.scalar_tensor_tensor(out=ty, in0=ta, scalar=-1.0, in1=rows["k_a"][0], op0=ALU.add, op1=ALU.mult),
                     reads=[ta_b, rows["k_a"][1]], writes=[ty_b])
                P.op("dve", lambda e: e.scalar_tensor_tensor(out=ta, in0=ty, scalar=1.0, in1=tk, op0=ALU.add, op1=ALU.mult),
                     reads=[ty_b, tk_b], writes=[ta_b])
                P.op("pool", lambda e: e.tensor_tensor(out=ty, in0=tr, in1=rows["r_k"][0], op=ALU.mult),
                     reads=[tr_b, rows["r_k"][1]], writes=[ty_b])
                P.op("pool", lambda e: e.tensor_tensor(out=ty, in0=ty, in1=ta, op=ALU.mult), reads=[ty_b, ta_b], writes=[ty_b])
                P.op("dve", lambda e, j=j: e.tensor_reduce(out=rk[j], in_=h3(ty), axis=AX.X, op=ALU.add), reads=[ty_b], writes=[rk_b[j]])
                P.op("act", lambda e: e.copy(hi, tw), reads=[tw_b], writes=[hi_b])
                P.op("dve", lambda e: e.tensor_tensor(out=lo, in0=tw, in1=hi, op=ALU.subtract), reads=[tw_b, hi_b], writes=[lo_b])
                yield
                bk, bk_b = getbank()

                def mmp(e, bk=bk):
                    ins = None
                    for pj in range(8):
                        for (src_, fl) in ((hi, 0), (lo, 1)):
                            ins = e.matmul(bk[:, pj * 2:pj * 2 + 2], src_[:, pj * 128:(pj + 1) * 128], cind[:, :], start=(fl == 0), stop=(fl == 1))
                    return ins
                P.op("pe", mmp, reads=[hi_b, lo_b, cind_b], writes=[bk_b])
                P.op("act", lambda e, bk=bk, j=j, ti=ti: e.activation(out=PC[j][:, :, 2 * ti:2 * ti + 2],
                                                                     in_=bk[:, 0:16].rearrange("p (j c) -> p j c", j=8), func=AF.Exp, scale=-CDEC),
                     reads=[bk_b], writes=[PC_b[j]])
                for mi in range(3):
                    cb = {}
                    for half in range(2):
                        bk, bk_b = getbank()

                        def mmc(e, bk=bk, mi=mi, half=half, j=j):
                            e.matmul(bk[:, :], ctri[:, j, mi, :], hi[:, half * 512:(half + 1) * 512], start=True, stop=False)
                            return e.matmul(bk[:, :], ctri[:, j, mi, :], lo[:, half * 512:(half + 1) * 512], start=False, stop=True)
                        P.op("pe", mmc, reads=[hi_b, lo_b, ctri_b], writes=[bk_b])
                        cb[half] = (bk, bk_b)

                    def expo(scale, cb=cb):
                        i = C_["terr"][0] % 2; C_["terr"][0] += 1
                        te, te_b = W["te%d" % i]
                        for half in range(2):
                            bk, bk_b = cb[half]
                            P.op("act", lambda e, te=te, bk=bk, half=half, scale=scale: e.activation(
                                out=te[:, half * 512:(half + 1) * 512], in_=bk[:, :], func=AF.Exp, scale=scale), reads=[bk_b], writes=[te_b])
                        return te, te_b
                    if mi == 0:
                        te, te_b = expo(-CDEC)
                        te2, te2_b = expo(CDEC)
                        emit_ch("tr", te, te_b, "c_rt%d" % j)
                        yield
                        emit_ch("ta", te2, te2_b, "c_ktl%d" % j)
                        emit_ch("tx", te2, te2_b, "c_bt%d" % j)
                    elif mi == 1:
                        te, te_b = expo(-CDEC)
                        emit_tok("tkap", te, te_b, "s_ka%d" % j)
                        emit_ch("tkap", te, te_b, "c_kt%d" % j)
                    else:
                        te, te_b = expo(-CDEC)
                        emit_tok("ta", te, te_b, "s_kh%d" % j)
                        emit_tok("tx", te, te_b, "s_bh%d" % j)
                    yield
            P.op("dve", lambda e: e.tensor_tensor(out=rk[0], in0=rk[0], in1=rk[1], op=ALU.add), reads=[rk_b[0], rk_b[1]], writes=[rk_b[0]])
            P.op("dve", lambda e: e.tensor_tensor(out=h3(ty), in0=h3(tv), in1=rk[0].unsqueeze(2).to_broadcast([128, 16, 64]), op=ALU.mult),
                 reads=[tv_b, rk_b[0]], writes=[ty_b])
            st(ty, ty_b, "s_bonus", "ty")
            for half in range(2):
                bk, bk_b = getbank()
                P.op("pe", lambda e, bk=bk, half=half: e.matmul(bk[:, :], sgT[:, :], G2[:, half * 512:(half + 1) * 512],
                                                                start=True, stop=True), reads=[sgT_b, G2_b], writes=[bk_b])
                P.op("act", lambda e, bk=bk, half=half: e.copy(tw[:, half * 512:(half + 1) * 512], bk[:, :]),
                     reads=[bk_b], writes=[tw_b])
            st(tw, tw_b, "s_g", "tw")
            yield


        drive_window(be, nb_, NCH)
        P.barrier()
        A.release(m)

    def scan():
        m = A.mark()
        cm_ap, cm_b = cload("cmask", 2 * 5 * 128, F32, "p d m t -> p (d m t)")
        cmask = cm_ap.rearrange("p (d m t) -> p d m t", d=2, m=5)
        id2, id2_b = cload("cid2", 64, F32)

        def a4(n_, dt=BF16):
            return A.alloc(16 * n_, dt).rearrange("p (h n) -> p h n", h=16)
        NSLOT = 2
        IN = []
        for s_ in range(NSLOT):
            d_ = {}
            d_["KR"] = (A.alloc(8 * 2 * 128, BF16).rearrange("p (j c t) -> p j c t", j=8, c=2), Buf("KR%d" % s_))
            d_["KT"] = (alloc3(8, 128, BF16), Buf("KT%d" % s_))
            d_["BT"] = (alloc3(8, 128, BF16), Buf("BT%d" % s_))
            for n in ("V", "KA", "KH", "BH"):
                d_[n] = (A.alloc(D, BF16), Buf("%s%d" % (n, s_)))
            IN.append(d_)
        NR = A.alloc(16 * 2 * 128, BF16).rearrange("p (h c t) -> p h c t", h=16, c=2); NR_b = [Buf("NR%d" % i) for i in range(8)]
        BK = A.alloc(16 * 2 * 128, BF16).rearrange("p (h c t) -> p h c t", h=16, c=2); BK_b = [Buf("BK%d" % i) for i in range(8)]
        Nn = a4(128); Nn_b = [Buf("Nn%d" % i) for i in range(4)]
        SS = [A.alloc(16 * 2 * 128, BF16).rearrange("p (h c t) -> p h c t", h=16, c=2) for _ in range(2)]
        SS_b = [[Buf("SS%d_%d" % (s_, i)) for i in range(8)] for s_ in range(2)]
        QT = [a4(128) for _ in range(2)]; QT_b = [[Buf("QT%d_%d" % (s_, i)) for i in range(4)] for s_ in range(2)]
        Wt = A.alloc(D, BF16); Wt_b = [Buf("Wt0"), Buf("Wt1")]
        BVt = A.alloc(D, BF16); BVt_b = [Buf("BV0"), Buf("BV1")]
        nUt = A.alloc(D, BF16); nUt_b = [Buf("nU0"), Buf("nU1")]
        diagP = [A.alloc(512, F32).rearrange("p (j k) -> p j k", j=8) for _ in range(2)]; diagP_b = [Buf("dP0"), Buf("dP1")]
        OUT = []
        for s_ in range(2):
            d_ = {}
            d_["GT"] = (A.alloc(2 * 512, BF16).rearrange("p (c j k) -> p c j k", c=2, j=8), [Buf("GT%d_0" % s_), Buf("GT%d_1" % s_)])
            d_["H"] = (A.alloc(2 * 512, F32).rearrange("p (c n) -> p c n", c=2), [Buf("H%d_0" % s_), Buf("H%d_1" % s_)])
            d_["RhT"] = (alloc3(8, 128, BF16), [Buf("Rh%d_0" % s_), Buf("Rh%d_1" % s_)])
            d_["Yl"] = (A.alloc(D, F32), [Buf("Yl%d_0" % s_), Buf("Yl%d_1" % s_)])
            OUT.append(d_)
        U = [(A.alloc(512, BF16).rearrange("p (j v) -> p j v", j=8), Buf("U%d" % i)) for i in range(4)]
        yo = [(A.alloc(D, F32), [Buf("yo%d_0" % i), Buf("yo%d_1" % i)]) for i in range(2)]
        urr = [0]

        def nextU():
            u = U[urr[0] % 4]; urr[0] += 1
            return u

        def hp(ix):
            return ix % 8, ix // 8

        def hcol(ix):
            hh_ = 2 * (ix % 8) + ix // 8
            return slice(hh_ * 64, (hh_ + 1) * 64)

        def icol(ix):
            return slice(ix * 64, (ix + 1) * 64)

        def qs(q):
            return slice(q * 64, (q + 1) * 64)

        iters = []
        for b in range(2):
            for dr_ in range(2):
                blocks = list(range(16)) if dr_ == 0 else list(range(15, -1, -1))
                for bidx, bi in enumerate(blocks):
                    iters.append((b, dr_, bi, bidx == 0))
        if DBG_ITERS is not None:
            iters = [iters[i_] for i_ in DBG_ITERS]
        chv = lambda n: dr[n].rearrange("(j p) t -> p j t", p=128)

        def issue_loads(n_):
            b, dr_, bi, _ = iters[n_]
            sl = n_ % NSLOT
            I = IN[sl]
            tb0 = b * S + bi * 128
            KR, KR_b = I["KR"]; KT, KT_b = I["KT"]; BT, BT_b = I["BT"]
            P.dma("sp", [lambda e, KR=KR, tb0=tb0, dr_=dr_: e.dma_start(out=KR[:, :, 0, :], in_=chv("c_kt%d" % dr_)[:, :, tb0:tb0 + 128]),
                         lambda e, KR=KR, tb0=tb0, dr_=dr_: e.dma_start(out=KR[:, :, 1, :], in_=chv("c_rt%d" % dr_)[:, :, tb0:tb0 + 128])],
                  reads=[dbuf["c_kt%d" % dr_], dbuf["c_rt%d" % dr_]], writes=[KR_b], semkey="lKR%d" % sl)
            P.dma("sp", [lambda e, KT=KT, tb0=tb0, dr_=dr_: e.dma_start(out=KT, in_=chv("c_ktl%d" % dr_)[:, :, tb0:tb0 + 128])],
                  reads=[dbuf["c_ktl%d" % dr_]], writes=[KT_b], semkey="lKT%d" % sl)
            P.dma("sp", [lambda e, BT=BT, tb0=tb0, dr_=dr_: e.dma_start(out=BT, in_=chv("c_bt%d" % dr_)[:, :, tb0:tb0 + 128])],
                  reads=[dbuf["c_bt%d" % dr_]], writes=[BT_b], semkey="lBT%d" % sl)
            for (n, scr) in (("V", "s_v"), ("KA", "s_ka%d" % dr_), ("KH", "s_kh%d" % dr_), ("BH", "s_bh%d" % dr_)):
                ap, b_ = I[n]
                P.dma("sp", [lambda e, ap=ap, scr=scr, tb0=tb0: e.dma_start(out=ap, in_=dr[scr][tb0:tb0 + 128, :])],
                      reads=[dbuf[scr]], writes=[b_], semkey="l%s%d" % (n, sl))

        issue_loads(0)
        u_cur = u_cur_b = None
        if True:
            if True:
                def stageA(it_):
                    (b, dr_, bi, isfirst) = iters[it_]
                    if it_ + 1 < len(iters):
                        issue_loads(it_ + 1)
                    corder = (0, 1) if dr_ == 0 else (1, 0)
                    sl = it_ % NSLOT; osl = it_ % 2; it = it_ + 1
                    I = IN[sl]; O = OUT[osl]
                    tb0 = b * S + bi * 128
                    gchunk = (b * 16 + bi) * 2
                    KR, KR_b = I["KR"]; KT, KT_b = I["KT"]; BT, BT_b = I["BT"]
                    Vt, Vt_b = I["V"]; KAt, KAt_b = I["KA"]; KHt, KHt_b = I["KH"]; BHt, BHt_b = I["BH"]
                    MK1 = cmask[:, dr_, 0:2, :]; MK2 = cmask[:, dr_, 2:4, :]; MK3 = cmask[:, dr_, 4, :]

                    def qs(q):
                        return slice(q * 64, (q + 1) * 64)

                    for (lhs, lhs_b, dstt, dstt_b, MK) in ((BT, BT_b, NR, NR_b, MK1), (KT, KT_b, BK, BK_b, MK2)):
                        for g in range(8):
                            bk, bk_b = getbank()

                            def mm(e, bk=bk, g=g, lhs=lhs, KR=KR):
                                ins = None
                                for hh in range(2):
                                    h = 2 * g + hh; j, q = hp(h)
                                    ins = e.matmul(bk[:, hh * 256:(hh + 1) * 256], lhs[qs(q), j, :],
                                                   KR[qs(q), j, :, :].rearrange("p c t -> p (c t)"), start=True, stop=True)
                                return ins
                            P.op("pe", mm, reads=[lhs_b, KR_b], writes=[bk_b])
                            P.op("dve", lambda e, bk=bk, g=g, dstt=dstt, MK=MK: e.tensor_tensor(
                                out=dstt[:, 2 * g:2 * g + 2, :, :], in0=bk[:, :].rearrange("p (h c t) -> p h c t", h=2, c=2),
                                in1=MK.unsqueeze(1).to_broadcast([128, 2, 2, 128]), op=ALU.mult),
                                reads=[bk_b, cm_b], writes=[dstt_b[g]])
                    yield
                    for g in range(4):
                        bk, bk_b = getbank()

                        def mm(e, bk=bk, g=g, KR=KR, BT=BT):
                            ins = None
                            for hh in range(4):
                                h = 4 * g + hh; j, q = hp(h)
                                ins = e.matmul(bk[:, hh * 128:(hh + 1) * 128], KR[qs(q), j, 0, :], BT[qs(q), j, :], start=True, stop=True)
                            return ins
                        P.op("pe", mm, reads=[KR_b, BT_b], writes=[bk_b])
                        P.op("dve", lambda e, bk=bk, g=g, MK3=MK3: e.tensor_tensor(
                            out=Nn[:, 4 * g:4 * g + 4, :], in0=bk[:, :].rearrange("p (h t) -> p h t", h=4),
                            in1=MK3.unsqueeze(1).to_broadcast([128, 4, 128]), op=ALU.mult), reads=[bk_b, cm_b], writes=[Nn_b[g]])
                    yield
                    q0 = 0
                    for g in range(4):
                        P.op("pool", lambda e, g=g: e.tensor_tensor(out=QT[0][:, 4 * g:4 * g + 4, :], in0=NR[:, 4 * g:4 * g + 4, 0, :],
                                                                     in1=ident.unsqueeze(1).to_broadcast([128, 4, 128]), op=ALU.add),
                             reads=[NR_b[2 * g], NR_b[2 * g + 1], ident_b], writes=[QT_b[0][g]])
                    for lev in range(5):
                        sidx = lev % 2
                        SSn = SS[sidx]; SSn_b = SS_b[sidx]
                        if lev == 0:
                            Np = lambda h: Nn[:, h, :]; NTp = lambda h: NR[:, h, 0, :]
                            Np_b = lambda h: [Nn_b[h // 4]]; NTp_b = lambda h: [NR_b[h // 2]]
                        else:
                            SSp = SS[1 - sidx]; SSp_b = SS_b[1 - sidx]
                            Np = lambda h, SSp=SSp: SSp[:, h, 0, :]; NTp = lambda h, SSp=SSp: SSp[:, h, 1, :]
                            Np_b = lambda h, SSp_b=SSp_b: [SSp_b[h // 2]]; NTp_b = Np_b
                        for g in range(8):
                            bk, bk_b = getbank()

                            def mm(e, bk=bk, g=g, Np=Np, NTp=NTp):
                                ins = None
                                for hh in range(2):
                                    h = 2 * g + hh
                                    e.matmul(bk[:, hh * 256:hh * 256 + 128], NTp(h), Np(h), start=True, stop=True)
                                    ins = e.matmul(bk[:, hh * 256 + 128:hh * 256 + 256], Np(h), NTp(h), start=True, stop=True)
                                return ins
                            P.op("pe", mm, reads=Np_b(2 * g) + NTp_b(2 * g) + Np_b(2 * g + 1) + NTp_b(2 * g + 1), writes=[bk_b])
                            P.op("act", lambda e, bk=bk, g=g, SSn=SSn: e.copy(SSn[:, 2 * g:2 * g + 2, :, :],
                                                                            bk[:, :].rearrange("p (h c t) -> p h c t", h=2, c=2)),
                                 reads=[bk_b], writes=[SSn_b[g]])
                        yield
                        Qp = QT[q0]; Qn = QT[1 - q0]; Qp_b = QT_b[q0]; Qn_b = QT_b[1 - q0]
                        for g in range(4):
                            bk, bk_b = getbank()

                            def mm(e, bk=bk, g=g, SSn=SSn, Qp=Qp):
                                ins = None
                                for hh in range(4):
                                    h = 4 * g + hh
                                    ins = e.matmul(bk[:, hh * 128:(hh + 1) * 128], SSn[:, h, 0, :], Qp[:, h, :], start=True, stop=True)
                                return ins
                            P.op("pe", mm, reads=[SSn_b[2 * g], SSn_b[2 * g + 1], Qp_b[g]], writes=[bk_b])
                            P.op("dve", lambda e, bk=bk, g=g, Qp=Qp, Qn=Qn: e.tensor_tensor(
                                out=Qn[:, 4 * g:4 * g + 4, :], in0=bk[:, :].rearrange("p (h t) -> p h t", h=4), in1=Qp[:, 4 * g:4 * g + 4, :], op=ALU.add),
                                reads=[bk_b, Qp_b[g]], writes=[Qn_b[g]])
                        q0 = 1 - q0
                    Qf = QT[q0]; Qf_b = QT_b[q0]
                    yield
                    for (kind, dstt, dstt_b) in (("W", Wt, Wt_b), ("BV", BVt, BVt_b), ("nU", nUt, nUt_b)):
                        for g in range(2):
                            bk, bk_b = getbank()

                            def mm(e, bk=bk, g=g, kind=kind, Qf=Qf, KAt=KAt, Vt=Vt):
                                ins = None
                                for hh in range(8):
                                    h = 8 * g + hh
                                    if kind == "W":
                                        ins = e.matmul(bk[:, hh * 64:(hh + 1) * 64], Qf[:, h, :], KAt[:, hcol(h)], start=True, stop=True)
                                    elif kind == "BV":
                                        ins = e.matmul(bk[:, hh * 64:(hh + 1) * 64], BK[:, h, 0, :], Vt[:, hcol(h)], start=True, stop=True)
                                    else:
                                        ins = e.matmul(bk[:, hh * 64:(hh + 1) * 64], Qf[:, h, :], BVt[:, icol(h)], start=True, stop=True)
                                return ins
                            if kind == "W":
                                rd = [Qf_b[2 * g], Qf_b[2 * g + 1], KAt_b]
                            elif kind == "BV":
                                rd = BK_b[4 * g:4 * g + 4] + [Vt_b]
                            else:
                                rd = [Qf_b[2 * g], Qf_b[2 * g + 1], BVt_b[g]]
                            P.op("pe", mm, reads=rd, writes=[bk_b])
                            if kind == "nU":
                                P.op("act", lambda e, bk=bk, g=g, dstt=dstt: e.mul(dstt[:, g * 512:(g + 1) * 512], bk[:, :], -1.0),
                                     reads=[bk_b], writes=[dstt_b[g]])
                            else:
                                P.op("act", lambda e, bk=bk, g=g, dstt=dstt: e.copy(dstt[:, g * 512:(g + 1) * 512], bk[:, :]),
                                     reads=[bk_b], writes=[dstt_b[g]])
                    GT, GT_b = O["GT"]; H, H_b = O["H"]; RhT, RhT_b = O["RhT"]; Yl, Yl_b = O["Yl"]
                    yield
                    for cch in range(2):
                        csl = slice(cch * 64, (cch + 1) * 64)
                        dP = diagP[cch]; dP_b = diagP_b[cch]
                        P.op("pool", lambda e, dP=dP, dr_=dr_, gc=gchunk + cch: e.tensor_tensor(
                            out=dP, in0=id2.unsqueeze(1).to_broadcast([128, 8, 64]),
                            in1=PC[dr_][:, :, gc:gc + 1].to_broadcast([128, 8, 64]), op=ALU.mult),
                            reads=[id2_b, PC_b[dr_]], writes=[dP_b])
                        bk, bk_b = getbank()

                        def mm(e, bk=bk, csl=csl, cch=cch, BHt=BHt):
                            ins = None
                            for h in range(16):
                                j, q = hp(h)
                                ins = e.matmul(bk[qs(q), j * 64:(j + 1) * 64], Wt[csl, icol(h)], BHt[csl, hcol(h)], start=True, stop=True,
                                               tile_position=(cch * 64, q * 64))
                            return ins
                        P.op("pe", mm, reads=Wt_b + [BHt_b], writes=[bk_b])
                        P.op("dve", lambda e, bk=bk, cch=cch, GT=GT, dP=dP: e.tensor_tensor(
                            out=GT[:, cch, :, :], in0=dP, in1=bk[:, :].rearrange("p (j k) -> p j k", j=8), op=ALU.subtract),
                            reads=[bk_b, dP_b], writes=[GT_b[cch]])
                        bk, bk_b = getbank()

                        def mm(e, bk=bk, csl=csl, cch=cch, KHt=KHt, BHt=BHt, Vt=Vt):
                            ins = None
                            for h in range(16):
                                j, q = hp(h)
                                e.matmul(bk[qs(q), j * 64:(j + 1) * 64], KHt[csl, hcol(h)], Vt[csl, hcol(h)], start=True, stop=False,
                                         tile_position=(cch * 64, q * 64))
                                ins = e.matmul(bk[qs(q), j * 64:(j + 1) * 64], BHt[csl, hcol(h)], nUt[csl, icol(h)], start=False, stop=True,
                                               tile_position=(cch * 64, q * 64))
                            return ins
                        P.op("pe", mm, reads=[KHt_b, BHt_b, Vt_b] + nUt_b, writes=[bk_b])
                        P.op("act", lambda e, bk=bk, cch=cch, H=H: e.copy(H[:, cch, :], bk[:, :]), reads=[bk_b], writes=[H_b[cch]])
                    yield
                    for g in range(2):
                        bk, bk_b = getbank()

                        def mm(e, bk=bk, g=g):
                            ins = None
                            for jj in range(4):
                                for q in range(2):
                                    j = 4 * g + jj; h = q * 8 + j
                                    ins = e.matmul(bk[qs(q), jj * 128:(jj + 1) * 128], Wt[:, icol(h)], NR[:, h, 1, :],
                                                   start=True, stop=True, tile_position=(0, q * 64))
                            return ins
                        P.op("pe", mm, reads=Wt_b + NR_b[2 * g:2 * g + 2] + NR_b[4 + 2 * g:4 + 2 * g + 2], writes=[bk_b])
                        P.op("dve", lambda e, bk=bk, g=g, RhT=RhT, KR=KR: e.tensor_tensor(
                            out=RhT[:, 4 * g:4 * g + 4, :], in0=KR[:, 4 * g:4 * g + 4, 1, :], in1=bk[:, :].rearrange("p (j t) -> p j t", j=4),
                            op=ALU.subtract), reads=[bk_b, KR_b], writes=[RhT_b[g]])
                    yield
                    for g in range(2):
                        bk, bk_b = getbank()

                        def mm(e, bk=bk, g=g, Vt=Vt):
                            ins = None
                            for hh in range(8):
                                h = 8 * g + hh
                                e.matmul(bk[:, hh * 64:(hh + 1) * 64], BK[:, h, 1, :], Vt[:, hcol(h)], start=True, stop=False)
                                ins = e.matmul(bk[:, hh * 64:(hh + 1) * 64], NR[:, h, 1, :], nUt[:, icol(h)], start=False, stop=True)
                            return ins
                        P.op("pe", mm, reads=BK_b[4 * g:4 * g + 4] + NR_b[4 * g:4 * g + 4] + [Vt_b, nUt_b[g]], writes=[bk_b])
                        P.op("act", lambda e, bk=bk, g=g, Yl=Yl: e.copy(Yl[:, g * 512:(g + 1) * 512], bk[:, :]), reads=[bk_b], writes=[Yl_b[g]])
                    yield

                ust = {}

                def stageBC(it_):
                    (b, dr_, bi, isfirst) = iters[it_]
                    corder = (0, 1) if dr_ == 0 else (1, 0)
                    osl = it_ % 2; it = it_ + 1
                    O = OUT[osl]
                    tb0 = b * S + bi * 128
                    GT, GT_b = O["GT"]; H, H_b = O["H"]; RhT, RhT_b = O["RhT"]; Yl, Yl_b = O["Yl"]
                    if isfirst:
                        u_cur, u_cur_b = nextU()
                        P.op("pool", lambda e, u_cur=u_cur: e.memset(u_cur, 0.0), writes=[u_cur_b])
                    else:
                        u_cur, u_cur_b = ust["u"]
                    Uc = {}
                    for cch in corder:
                        Uc[cch] = (u_cur, u_cur_b)
                        u_new, u_new_b = nextU()
                        for q in range(2):
                            bk, bk_b = getbank()

                            def mm(e, bk=bk, cch=cch, GT=GT, u_cur=u_cur, q=q):
                                ins = None
                                for j in range(8):
                                    ins = e.matmul(bk[qs(q), j * 64:(j + 1) * 64], GT[qs(q), cch, j, :], u_cur[qs(q), j, :], start=True, stop=True,
                                                   tile_position=(q * 64, q * 64))
                                return ins
                            P.op("pe", mm, reads=[GT_b[cch], u_cur_b], writes=[bk_b])
                            P.op("dve", lambda e, bk=bk, cch=cch, H=H, u_new=u_new, q=q: e.tensor_tensor(
                                out=u_new[qs(q)].rearrange("p j v -> p (j v)"), in0=bk[qs(q), :], in1=H[qs(q), cch, :], op=ALU.add),
                                reads=[bk_b, H_b[cch]], writes=[u_new_b])
                        u_cur, u_cur_b = u_new, u_new_b
                        yield
                    yo_ap, yo_b = yo[it % 2]
                    for g in range(2):
                        bk, bk_b = getbank()

                        def mm(e, bk=bk, g=g, RhT=RhT, Uc=dict(Uc)):
                            ins = None
                            for cch in range(2):
                                uu = Uc[cch][0]
                                for hh in range(8):
                                    j = hh; q = g
                                    ins = e.matmul(bk[cch * 64:(cch + 1) * 64, hh * 64:(hh + 1) * 64], RhT[qs(q), j, cch * 64:(cch + 1) * 64],
                                                   uu[qs(q), j, :], start=True, stop=True, tile_position=(q * 64, cch * 64))
                            return ins
                        P.op("pe", mm, reads=RhT_b + [Uc[0][1], Uc[1][1]], writes=[bk_b])
                        P.op("dve", lambda e, bk=bk, g=g, Yl=Yl, yo_ap=yo_ap: e.tensor_tensor(
                            out=yo_ap.rearrange("p (j q v) -> p j q v", j=8, q=2)[:, :, g, :], in0=bk[:, :].rearrange("p (j v) -> p j v", j=8),
                            in1=Yl[:, g * 512:(g + 1) * 512].rearrange("p (j v) -> p j v", j=8), op=ALU.add),
                            reads=[bk_b, Yl_b[g]], writes=[yo_b[g]])
                    P.dma("pool", [lambda e, yo_ap=yo_ap, tb0=tb0, dr_=dr_: e.dma_start(out=dr["s_y%d" % dr_][tb0:tb0 + 128, :], in_=yo_ap)],
                          reads=yo_b, writes=[dbuf["s_y%d" % dr_]], semkey="yst%d" % (it % 2))
                    ust["u"] = (u_cur, u_cur_b)
                    yield

                def drive2(gens):
                    gens = [g for g in gens if g is not None]
                    while gens:
                        for g in list(gens):
                            try:
                                next(g)
                            except StopIteration:
                                gens.remove(g)

                drive2([stageA(0)])
                for it_ in range(len(iters)):
                    drive2([stageBC(it_), stageA(it_ + 1) if it_ + 1 < len(iters) else None])
        P.barrier()
        A.release(m)

    def post():
        m = A.mark()
        pstage = [(A.alloc(D, F32), Buf("wst%d" % i)) for i in range(2)]
        Wo = A.alloc(KD * D, BF16).rearrange("p (k n) -> p k n", k=KD); Wo_b = [Buf("wo%d" % k) for k in range(KD)]
        wv = dr["w_o"].rearrange("(k p) n -> p k n", p=128)
        for k in range(KD):
            stg, stg_b = pstage[k % 2]
            P.dma("sp", [lambda e, k=k, stg=stg: e.dma_start(out=stg, in_=wv[:, k, :])], writes=[stg_b], semkey="wst%d" % (k % 2))
            P.op("dve" if k % 2 == 0 else "act", (lambda e, k=k, stg=stg: e.tensor_copy(Wo[:, k, :], stg)) if k % 2 == 0 else (lambda e, k=k, stg=stg: e.copy(Wo[:, k, :], stg)),
                 reads=[stg_b], writes=[Wo_b[k]])
        rows = load_rows(["ln_w", "ln_b"])
        SETS = []
        for i in range(2):
            d_ = {}
            for n in ("y0", "y1", "bon", "gg", "tz"):
                d_[n] = (A.alloc(D, F32), Buf("%s_%d" % (n, i)))
            d_["ob"] = (A.alloc(D, BF16), Buf("ob_%d" % i))
            d_["st1"] = (A.alloc(16, F32), Buf("st1_%d" % i))
            d_["st2"] = (A.alloc(16, F32), Buf("st2_%d" % i))
            SETS.append(d_)
        TQ = 512
        oTs = [A.alloc(KD * TQ, BF16).rearrange("p (k n) -> p k n", k=KD) for _ in range(2)]
        oTs_b = [[Buf("oT%d_%d" % (i, b_)) for b_ in range(TQ // 128)] for i in range(2)]
        xts = [A.alloc(KD * TQ, F32).rearrange("p (k n) -> p k n", k=KD) for _ in range(2)]
        xts_b = [Buf("xt0"), Buf("xt1")]

        def blkgen(ti, blk):
            Sx = SETS[blk % 2]
            y0, y0_b = Sx["y0"]; y1, y1_b = Sx["y1"]; bon, bon_b = Sx["bon"]; gg, gg_b = Sx["gg"]; tz, tz_b = Sx["tz"]
            ob, ob_b = Sx["ob"]; st1, st1_b = Sx["st1"]; st2, st2_b = Sx["st2"]
            oT = oTs[ti % 2]; oT_b = oTs_b[ti % 2]
            tb0 = ti * TQ + blk * 128
            for (ap, b_, scr) in ((y0, y0_b, "s_y0"), (y1, y1_b, "s_y1"), (bon, bon_b, "s_bonus"), (gg, gg_b, "s_g")):
                P.dma("sp", [lambda e, ap=ap, scr=scr, tb0=tb0: e.dma_start(out=ap, in_=dr[scr][tb0:tb0 + 128, :])],
                      reads=[dbuf[scr]], writes=[b_], semkey="ld_%s_%d" % (scr, blk % 2))
            yield
            P.op("dve", lambda e: e.tensor_tensor(out=y0, in0=y0, in1=y1, op=ALU.add), reads=[y0_b, y1_b], writes=[y0_b])
            P.op("dve", lambda e: e.tensor_reduce(out=st1, in_=h3(y0), axis=AX.X, op=ALU.add), reads=[y0_b], writes=[st1_b])
            P.op("dve", lambda e: e.tensor_scalar(st1, st1, 1.0 / 64, None, ALU.mult), reads=[st1_b], writes=[st1_b])
            yield
            P.op("dve", lambda e: e.tensor_tensor(out=h3(y0), in0=h3(y0), in1=st1.unsqueeze(2).to_broadcast([128, 16, 64]), op=ALU.subtract),
                 reads=[y0_b, st1_b], writes=[y0_b])
            P.op("act", lambda e: e.activation(out=tz, in_=y0, func=AF.Square), reads=[y0_b], writes=[tz_b])
            P.op("pool", lambda e: e.tensor_tensor(out=bon, in0=bon, in1=rows["ln_b"][0], op=ALU.add), reads=[bon_b, rows["ln_b"][1]], writes=[bon_b])
            yield
            P.op("dve", lambda e: e.tensor_reduce(out=st2, in_=h3(tz), axis=AX.X, op=ALU.add), reads=[tz_b], writes=[st2_b])
            P.op("act", lambda e: e.activation(out=st2, in_=st2, func=AF.Sqrt, bias=64e-5, scale=1.0 / 64), reads=[st2_b], writes=[st2_b])
            P.op("dve", lambda e: e.reciprocal(st2, st2), reads=[st2_b], writes=[st2_b])
            yield
            P.op("dve", lambda e: e.tensor_tensor(out=h3(y0), in0=h3(y0), in1=st2.unsqueeze(2).to_broadcast([128, 16, 64]), op=ALU.mult),
                 reads=[y0_b, st2_b], writes=[y0_b])
            P.op("pool", lambda e: e.tensor_tensor(out=y0, in0=y0, in1=rows["ln_w"][0], op=ALU.mult), reads=[y0_b, rows["ln_w"][1]], writes=[y0_b])
            yield
            P.op("dve", lambda e: e.tensor_tensor(out=y0, in0=y0, in1=bon, op=ALU.add), reads=[y0_b, bon_b], writes=[y0_b])
            P.op("pool", lambda e: e.tensor_tensor(out=ob, in0=y0, in1=gg, op=ALU.mult), reads=[y0_b, gg_b], writes=[ob_b])
            yield
            for half in range(2):
                bk, bk_b = getbank()
                bkb = bk[:, :].bitcast(BF16)

                def tr(e, bkb=bkb, half=half):
                    ins = None
                    for q in range(4):
                        k = half * 4 + q
                        ins = e.transpose(bkb[:, q * 128:(q + 1) * 128], ob[:, k * 128:(k + 1) * 128], ident[:, :])
                    return ins
                P.op("pe", tr, reads=[ob_b, ident_b], writes=[bk_b])
                P.op("act", lambda e, bkb=bkb, half=half: e.copy(
                    oT[:, half * 4:(half + 1) * 4, blk * 128:(blk + 1) * 128], bkb[:, 0:512].rearrange("p (q n) -> p q n", q=4)),
                    reads=[bk_b], writes=[oT_b[blk]])
            yield

        def fingen(ti):
            t0 = ti * TQ
            oT = oTs[ti % 2]; oT_b = oTs_b[ti % 2]
            xt = xts[ti % 2]; xt_b = xts_b[ti % 2]
            P.dma("sp", [lambda e, t0=t0: e.dma_start(out=xt, in_=sv[:, :, t0:t0 + TQ])], reads=[dbuf[src]], writes=[xt_b], semkey="xt%d" % (ti % 2))
            yield
            for do in range(KD):
                bk, bk_b = getbank()

                def mm(e, bk=bk, do=do):
                    ins = None
                    for k in range(KD):
                        ins = e.matmul(bk[:, :], Wo[:, k, do * 128:(do + 1) * 128], oT[:, k, :], start=(k == 0), stop=(k == KD - 1))
                    return ins
                P.op("pe", mm, reads=oT_b + Wo_b, writes=[bk_b])
                P.op("dve", lambda e, bk=bk, do=do: e.tensor_tensor(out=xt[:, do, :], in0=xt[:, do, :], in1=bk[:, :], op=ALU.add),
                     reads=[bk_b, xt_b], writes=[xt_b])
                if do % 2 == 1:
                    yield
            P.dma("pool", [lambda e, t0=t0: e.dma_start(out=dv[:, :, t0:t0 + TQ], in_=xt)], reads=[xt_b], writes=[dbuf[dst]], semkey="xo%d" % (ti % 2))
            yield

        def drive3(gens):
            gens = [g for g in gens if g is not None]
            while gens:
                for g in list(gens):
                    try:
                        next(g)
                    except StopIteration:
                        gens.remove(g)

        prev_fin = None
        for ti in range(T // TQ):
            drive3([blkgen(ti, 0), blkgen(ti, 1), prev_fin])
            drive3([blkgen(ti, 2), blkgen(ti, 3)])
            prev_fin = fingen(ti)
        drive3([prev_fin])
        P.barrier()
        A.release(m)

    if "prep" in sub:
        prep()
    if "scan" in sub:
        scan()
    if "post" in sub:
        post()


F32 = mybir.dt.float32
BF16 = mybir.dt.bfloat16
ALU = mybir.AluOpType
AF = mybir.ActivationFunctionType

D = 1024; KD = 8; FF = 2816; KF = 22; T = 4096; S = 2048; TT = 512; NTT = T // TT
ARENA_WORDS = 52800
VEC_NAMES = ["nm0", "nm1", "nf0", "nf1", "nfin", "mu0", "mu1", "mu2", "mu3", "mu4", "mu5",
             "w0_0", "w0_1", "a0_0", "a0_1", "k_k", "k_a", "r_k", "ln_w", "ln_b"]
VI = {n: i for i, n in enumerate(VEC_NAMES)}
NV = len(VEC_NAMES)


class Ctx:
    pass


def build(phases=("l0mix", "ffn0", "l1mix", "ffn1", "final"), debug=False, rwkv_sub=("prep", "scan", "post"), dbg_scr=False):
    nc = bass.Bass("TRN2", target_bir_lowering=False)
    st = ExitStack()
    P = Prog(nc, st)
    dr = {}

    drh = {}

    def din(name, shape, dt=F32):
        drh[name] = nc.dram_tensor(name, list(shape), dt, kind="ExternalInput")
        dr[name] = drh[name].ap()

    def dscr(name, shape, dt=F32):
        kind = "ExternalOutput" if debug else "Internal"
        dr[name] = nc.dram_tensor(name, list(shape), dt, kind=kind).ap()

    din("xT", [D, T]); din("vecs", [128, NV * 8])
    din("fno_w", [D, D])
    din("wg", [2, D, FF]); din("wu", [2, D, FF]); din("wd", [2, FF, D])
    din("cs1", [128, 2, 512], BF16); din("cs2", [4, 128, 16, 2, 512], BF16)
    dr["outT"] = nc.dram_tensor("outT", [D, T], F32, kind="ExternalOutput").ap()
    dscr("xa", [D, T]); dscr("xb", [D, T])
    if "l1mix" in phases:
        declare(dr, drh, nc, din, dbg_scr)

    arena_t = st.enter_context(nc.sbuf_tensor("arena", [128, ARENA_WORDS], F32))
    A = Arena(arena_t, ARENA_WORDS)
    banks = []
    for i in range(8):
        pt = st.enter_context(nc.psum_tensor("bank%d" % i, [128, 512], F32))
        banks.append((pt, Buf("bank%d" % i)))
    bank_rr = [0]

    def getbank():
        b = banks[bank_rr[0] % 8]
        bank_rr[0] += 1
        return b

    vecs = A.alloc(NV * 8, F32); vecs_b = Buf("vecs")
    P.dma("sp", [lambda e: e.dma_start(out=vecs, in_=dr["vecs"])], writes=[vecs_b], semkey="vecs")
    onesD = A.alloc(128, BF16); ones_b = Buf("ones")
    P.op("pool", lambda e: e.memset(onesD, 1.0 / D), writes=[ones_b])
    persist_mark = A.mark()

    def vcol(name, k):
        i = VI[name] * 8 + k
        return vecs[:, i:i + 1]

    def dview(name):
        return dr[name].rearrange("(k p) t -> p k t", p=128)

    rr = {"cast": 0}

    def load_w_gen(w2d, K, N, tag, stage, stage_bufs, out):
        dst = A.alloc(K * N, BF16).rearrange("p (k n) -> p k n", k=K)
        bufs = [Buf("%s_k%d" % (tag, k)) for k in range(K)]
        out.append((dst, bufs))
        wv = w2d.rearrange("(k p) n -> p k n", p=128)
        for k in range(K):
            s = rr["cast"] % len(stage); rr["cast"] += 1
            sap = stage[s][:, 0:N]
            P.dma("sp", [lambda e, sap=sap, k=k: e.dma_start(out=sap, in_=wv[:, k, :])],
                  writes=stage_bufs[s], semkey="wst%d" % s)
            eng = "dve" if (k % 2 == 0) else "act"
            if eng == "dve":
                P.op("dve", lambda e, sap=sap, k=k: e.tensor_copy(dst[:, k, :], sap),
                     reads=stage_bufs[s], writes=[bufs[k]])
            else:
                P.op("act", lambda e, sap=sap, k=k: e.copy(dst[:, k, :], sap),
                     reads=stage_bufs[s], writes=[bufs[k]])
            yield

    def load_w_bf16(w2d, K, N, tag, stage, stage_bufs):
        out = []
        sb = [b if isinstance(b, list) else [b] for b in stage_bufs]
        for _ in load_w_gen(w2d, K, N, tag, stage, sb, out):
            pass
        return out[0]

    def drive(gens):
        gens = [g for g in gens if g is not None]
        while gens:
            for g in list(gens):
                try:
                    next(g)
                except StopIteration:
                    gens.remove(g)

    def rmsnorm(xt, xt_b, gname, hT, hT_b, sq, sq_b, rstd, rstd_b, n=TT):
        P.op("pool", lambda e: e.tensor_tensor(out=sq, in0=xt, in1=xt, op=ALU.mult), reads=[xt_b], writes=sq_b)
        bk, bk_b = getbank()

        def mm(e):
            ins = None
            for k in range(KD):
                ins = e.matmul(bk[:, 0:n], onesD[:, :], sq[:, k, :], start=(k == 0), stop=(k == KD - 1))
            return ins
        P.op("pe", mm, reads=sq_b + [ones_b], writes=[bk_b])
        P.op("act", lambda e: e.activation(out=rstd, in_=bk[:, 0:n], func=AF.Sqrt, bias=1e-6, scale=1.0),
             reads=[bk_b], writes=[rstd_b])
        P.op("dve", lambda e: e.reciprocal(rstd, rstd), reads=[rstd_b], writes=[rstd_b])
        for k in range(KD):
            eng = "dve"
            P.op(eng, lambda e, k=k: e.scalar_tensor_tensor(out=hT[:, k, :], in0=xt[:, k, :], scalar=vcol(gname, k),
                                                          in1=rstd, op0=ALU.mult, op1=ALU.mult),
                 reads=[xt_b, rstd_b, vecs_b], writes=[hT_b[k]])

    def phase_fourier(src, dst):
        m = A.mark()
        stage = [A.alloc(D, F32) for _ in range(2)]
        stage_bufs = [Buf("wst0"), Buf("wst1")]
        Wf, Wf_b = load_w_bf16(dr["fno_w"], KD, D, "wf", stage, stage_bufs)
        cs1 = A.alloc(2 * 512, BF16).rearrange("p (k n) -> p k n", k=2); cs1_b = Buf("cs1")
        P.dma("sp", [lambda e: e.dma_start(out=cs1, in_=dr["cs1"])], writes=[cs1_b], semkey="cs1")
        cs2 = [A.alloc(16 * 2 * 512, BF16).rearrange("p (s c n) -> p s c n", s=16, c=2) for _ in range(2)]
        cs2_b = [Buf("cs2_0"), Buf("cs2_1")]
        AB = A.alloc(2 * 16 * D, BF16).rearrange("p (c s d) -> p c s d", c=2, s=16)
        AB_b = [[Buf("AB%d_%d" % (i, j)) for j in range(4)] for i in range(16)]
        AB_all = [b_ for l_ in AB_b for b_ in l_]
        xt = [A.alloc(KD * TT, F32).rearrange("p (k n) -> p k n", k=KD) for _ in range(2)]
        xt_b = [Buf("xt0"), Buf("xt1")]
        hT = A.alloc(KD * TT, BF16).rearrange("p (k n) -> p k n", k=KD); hT_b = [Buf("hT%d" % i) for i in range(KD)]
        sq = A.alloc(KD * TT, BF16).rearrange("p (k n) -> p k n", k=KD); sq_b = [Buf("sq%d" % i) for i in range(KD)]
        rstd = A.alloc(TT, F32); rstd_b = Buf("rstd")
        fT = sq; fT_b = sq_b
        sv = dview(src); dv = dview(dst)
        ev = [0]
        for b in range(2):
            for tt in range(4):
                t0 = b * S + tt * TT
                x_ = xt[tt % 2]; x_b = xt_b[tt % 2]
                P.dma("sp", [lambda e, x_=x_, t0=t0: e.dma_start(out=x_, in_=sv[:, :, t0:t0 + TT])],
                      reads=[dbuf[src]], writes=[x_b], semkey="xt%d" % (tt % 2))
                rmsnorm(x_, x_b, "nm0", hT, hT_b, sq, sq_b, rstd, rstd_b)
                for blk in range(4):
                    sc = tt * 4 + blk
                    for g in range(4):
                        bk, bk_b = getbank()

                        def mm(e, bk=bk, blk=blk, g=g):
                            ins = None
                            for kk in range(2):
                                ins = e.matmul(bk[:, :], hT[:, 2 * g + kk, blk * 128:(blk + 1) * 128], cs1[:, kk, :],
                                               start=(kk == 0), stop=(kk == 1))
                            return ins
                        P.op("pe", mm, reads=hT_b + [cs1_b], writes=[bk_b])
                        eng = "act" if ev[0] % 2 == 0 else "dve"; ev[0] += 1
                        outv = AB[:, :, sc, g * 256:(g + 1) * 256]
                        inv = bk[:, :].rearrange("p (c n) -> p c n", c=2)
                        if eng == "act":
                            P.op("act", lambda e, outv=outv, inv=inv: e.copy(outv, inv), reads=[bk_b], writes=[AB_b[sc][g]])
                        else:
                            P.op("dve", lambda e, outv=outv, inv=inv: e.tensor_copy(outv, inv), reads=[bk_b], writes=[AB_b[sc][g]])
            for stl in range(4):
                c2 = cs2[stl % 2]; c2_b = cs2_b[stl % 2]
                P.dma("sp", [lambda e, c2=c2, stl=stl: e.dma_start(out=c2, in_=dr["cs2"][stl])],
                      writes=[c2_b], semkey="cs2_%d" % (stl % 2))
                t0 = b * S + stl * TT
                x_ = xt[stl % 2]; x_b = xt_b[stl % 2]
                P.dma("sp", [lambda e, x_=x_, t0=t0: e.dma_start(out=x_, in_=sv[:, :, t0:t0 + TT])],
                      reads=[dbuf[src]], writes=[x_b], semkey="xt%d" % (stl % 2))
                for dc in range(KD):
                    bk, bk_b = getbank()

                    def mm(e, bk=bk, dc=dc, c2=c2):
                        ins = None
                        for sc in range(16):
                            for c in range(2):
                                ins = e.matmul(bk[:, :], AB[:, c, sc, dc * 128:(dc + 1) * 128], c2[:, sc, c, :],
                                               start=(sc == 0 and c == 0), stop=(sc == 15 and c == 1))
                        return ins
                    P.op("pe", mm, reads=AB_all + [c2_b], writes=[bk_b])
                    if dc % 2 == 0:
                        P.op("act", lambda e, bk=bk, dc=dc: e.copy(fT[:, dc, :], bk[:, :]), reads=[bk_b], writes=[fT_b[dc]])
                    else:
                        P.op("dve", lambda e, bk=bk, dc=dc: e.tensor_copy(fT[:, dc, :], bk[:, :]), reads=[bk_b], writes=[fT_b[dc]])
                for do in range(KD):
                    bk, bk_b = getbank()

                    def mm(e, bk=bk, do=do):
                        ins = None
                        for k in range(KD):
                            ins = e.matmul(bk[:, :], Wf[:, k, do * 128:(do + 1) * 128], fT[:, k, :],
                                           start=(k == 0), stop=(k == KD - 1))
                        return ins
                    P.op("pe", mm, reads=fT_b + Wf_b, writes=[bk_b])
                    P.op("dve", lambda e, bk=bk, do=do, x_=x_: e.tensor_tensor(out=x_[:, do, :], in0=x_[:, do, :], in1=bk[:, :], op=ALU.add),
                         reads=[bk_b, x_b], writes=[x_b])
                P.dma("act", [lambda e, x_=x_, t0=t0: e.dma_start(out=dv[:, :, t0:t0 + TT], in_=x_)],
                      reads=[x_b], writes=[dbuf[dst]], semkey="xo%d" % (stl % 2))
        P.barrier()
        A.release(m)

    def phase_ffn(layer, src, dst):
        m = A.mark()
        xt_flat = A.alloc(KD * TT, F32)
        xt = [xt_flat.rearrange("p (k n) -> p k n", k=KD)]
        xt_b = [Buf("xt0")]
        hT = A.alloc(KD * TT, BF16).rearrange("p (k n) -> p k n", k=KD); hT_b = [Buf("hT%d" % i) for i in range(KD)]
        rstd = A.alloc(TT, F32); rstd_b = Buf("rstd")
        actT_w = A.alloc(KF * TT // 2, F32)
        actT = actT_w.bitcast(BF16).rearrange("p (k n) -> p k n", k=KF)
        act_b = [Buf("act%d" % i) for i in range(KF)]
        sq = actT[:, 0:KD, :]; sq_b = act_b[0:KD]
        sg = [A.alloc(TT, F32) for _ in range(2)]; sg_b = [Buf("sg0"), Buf("sg1")]
        st0 = A.alloc(FF, F32); st0_b = [Buf("wst0a"), Buf("wst0b")]
        HW_ = FF // 2
        stage4 = [st0, xt_flat[:, 0:FF], actT_w[:, 0:FF], actT_w[:, FF:2 * FF]]
        stage4_b = [st0_b, [xt_b[0]], act_b[0:11], act_b[11:22]]
        Wg, Wg_b = load_w_bf16(dr["wg"][layer], KD, FF, "wg", stage4, stage4_b)
        Wu, Wu_b = load_w_bf16(dr["wu"][layer], KD, FF, "wu", stage4, stage4_b)
        stageD = [st0[:, 0:D], st0[:, HW_:HW_ + D]]
        stageD_b = [[st0_b[0]], [st0_b[1]]]
        wd_out = []
        wd_gen = load_w_gen(dr["wd"][layer], KF, D, "wd", stageD, stageD_b, wd_out)
        next(wd_gen)
        Wd, Wd_b = wd_out[0]
        sv = dview(src); dv = dview(dst)
        gname = "nf%d" % layer

        def tile_gen(tt):
            t0 = tt * TT
            x_ = xt[0]; x_b = xt_b[0]
            P.dma("sp", [lambda e, x_=x_, t0=t0: e.dma_start(out=x_, in_=sv[:, :, t0:t0 + TT])],
                  reads=[dbuf[src]], writes=[x_b], semkey="xt0")
            rmsnorm(x_, x_b, gname, hT, hT_b, sq, sq_b, rstd, rstd_b)
            yield
            for fc in range(KF):
                bg, bg_b = getbank()
                bu, bu_b = getbank()

                def mmg(e, bk=bg, fc=fc):
                    ins = None
                    for k in range(KD):
                        ins = e.matmul(bk[:, :], Wg[:, k, fc * 128:(fc + 1) * 128], hT[:, k, :], start=(k == 0), stop=(k == KD - 1))
                    return ins

                def mmu(e, bk=bu, fc=fc):
                    ins = None
                    for k in range(KD):
                        ins = e.matmul(bk[:, :], Wu[:, k, fc * 128:(fc + 1) * 128], hT[:, k, :], start=(k == 0), stop=(k == KD - 1))
                    return ins
                P.op("pe", mmg, reads=hT_b + Wg_b, writes=[bg_b])
                P.op("pe", mmu, reads=hT_b + Wu_b, writes=[bu_b])
                s_ = sg[fc % 2]; s_b = sg_b[fc % 2]
                P.op("act", lambda e, s_=s_, bg=bg: e.activation(out=s_, in_=bg[:, :], func=AF.Silu), reads=[bg_b], writes=[s_b])
                P.op("dve", lambda e, s_=s_, bu=bu, fc=fc: e.tensor_tensor(out=actT[:, fc, :], in0=s_, in1=bu[:, :], op=ALU.mult),
                     reads=[s_b, bu_b], writes=[act_b[fc]])
                yield
            for do in range(KD):
                bk, bk_b = getbank()

                def mm(e, bk=bk, do=do):
                    ins = None
                    for fc in range(KF):
                        ins = e.matmul(bk[:, :], Wd[:, fc, do * 128:(do + 1) * 128], actT[:, fc, :], start=(fc == 0), stop=(fc == KF - 1))
                    return ins
                P.op("pe", mm, reads=act_b + Wd_b, writes=[bk_b])
                P.op("dve", lambda e, bk=bk, do=do, x_=x_: e.tensor_tensor(out=x_[:, do, :], in0=x_[:, do, :], in1=bk[:, :], op=ALU.add),
                     reads=[bk_b, x_b], writes=[x_b])
            P.dma("act", [lambda e, x_=x_, t0=t0: e.dma_start(out=dv[:, :, t0:t0 + TT], in_=x_)],
                  reads=[x_b], writes=[dbuf[dst]], semkey="xo0")
            yield

        drive([wd_gen, tile_gen(0)])
        for tt in range(1, NTT):
            drive([tile_gen(tt)])
        P.barrier()
        A.release(m)

    def phase_final(src):
        m = A.mark()
        xt = [A.alloc(KD * TT, F32).rearrange("p (k n) -> p k n", k=KD) for _ in range(2)]
        xt_b = [Buf("xt0"), Buf("xt1")]
        ot = [A.alloc(KD * TT, F32).rearrange("p (k n) -> p k n", k=KD) for _ in range(2)]
        ot_b = [[Buf("ot0_%d" % i) for i in range(KD)], [Buf("ot1_%d" % i) for i in range(KD)]]
        sq = A.alloc(KD * TT, BF16).rearrange("p (k n) -> p k n", k=KD); sq_b = [Buf("sq%d" % i) for i in range(KD)]
        rstd = A.alloc(TT, F32); rstd_b = Buf("rstd")
        sv = dview(src); dv = dview("outT")
        for tt in range(NTT):
            t0 = tt * TT
            x_ = xt[tt % 2]; x_b = xt_b[tt % 2]
            P.dma("sp", [lambda e, x_=x_, t0=t0: e.dma_start(out=x_, in_=sv[:, :, t0:t0 + TT])],
                  reads=[dbuf[src]], writes=[x_b], semkey="xt%d" % (tt % 2))
            rmsnorm(x_, x_b, "nfin", ot[tt % 2], ot_b[tt % 2], sq, sq_b, rstd, rstd_b)
            P.dma("act", [lambda e, o_=ot[tt % 2], t0=t0: e.dma_start(out=dv[:, :, t0:t0 + TT], in_=o_)],
                  reads=ot_b[tt % 2], writes=[dbuf["outT"]], semkey="xo%d" % (tt % 2))
        P.barrier()
        A.release(m)

    dbuf = {n: Buf("dram_" + n) for n in ("xT", "xa", "xb", "outT")}
    cur = "xT"
    ctx = Ctx()
    ctx.__dict__.update(locals())
    if "l0mix" in phases:
        phase_fourier(cur, "xa"); cur = "xa"
    if "ffn0" in phases:
        phase_ffn(0, cur, "xb"); cur = "xb"
    if "l1mix" in phases:
        nxt = "xb" if cur == "xa" else "xa"
        phase_rwkv(ctx, cur, nxt, sub=rwkv_sub); cur = nxt
    if "ffn1" in phases:
        nxt = "xb" if cur == "xa" else "xa"
        phase_ffn(1, cur, nxt); cur = nxt
    if "final" in phases:
        phase_final(cur)
    P.barrier()
    stats = P.finalize()
    st.close()
    return nc, stats


def host_consts():
    c = np.arange(256)
    ang1 = 2 * np.pi * ((c[:, None] * c[None, :]) % 256) / 256.0
    cs1 = np.concatenate([np.cos(ang1), np.sin(ang1)], axis=1) / 16.0
    cs1 = cs1.reshape(2, 128, 512).transpose(1, 0, 2)
    s = np.arange(S)
    ang2 = 2 * np.pi * ((s[:, None] * s[None, :]) % S) / float(S)
    C2 = np.cos(ang2) / np.sqrt(S); S2 = -np.sin(ang2) / np.sqrt(S)
    cs2 = np.stack([C2, S2], axis=1)
    cs2 = cs2.reshape(16, 128, 2, 4, 512).transpose(3, 1, 0, 2, 4)
    return (np.ascontiguousarray(cs1).astype(ml_dtypes.bfloat16),
            np.ascontiguousarray(cs2).astype(ml_dtypes.bfloat16))


def pack_vecs(inp):
    vs = {
        "nm0": inp["norm_mix_g"][0], "nm1": inp["norm_mix_g"][1],
        "nf0": inp["norm_ffn_g"][0], "nf1": inp["norm_ffn_g"][1], "nfin": inp["norm_final_g"],
        "w0_0": inp["rwkv_w0"][0, 0], "w0_1": inp["rwkv_w0"][0, 1],
        "a0_0": inp["rwkv_a0"][0, 0], "a0_1": inp["rwkv_a0"][0, 1],
        "k_k": inp["rwkv_k_k"][0], "k_a": inp["rwkv_k_a"][0], "r_k": inp["rwkv_r_k"][0].reshape(-1),
        "ln_w": inp["rwkv_ln_w"][0], "ln_b": inp["rwkv_ln_b"][0],
    }
    for i in range(6):
        vs["mu%d" % i] = inp["rwkv_mu"][0, i]
    out = np.zeros((128, NV * 8), np.float32)
    for n, i in VI.items():
        out[:, i * 8:(i + 1) * 8] = np.asarray(vs[n], np.float32).reshape(8, 128).T
    return out


def make_consts(inp):
    cs1, cs2 = host_consts()
    rows = np.stack([np.asarray(x, np.float32).reshape(-1) for x in (
        inp["rwkv_k_k"][0], inp["rwkv_k_a"][0], inp["rwkv_w0"][0, 0], inp["rwkv_w0"][0, 1], inp["rwkv_a0"][0, 0],
        inp["rwkv_a0"][0, 1], inp["rwkv_r_k"][0], inp["rwkv_ln_w"][0], inp["rwkv_ln_b"][0])])
    rc = rwkv_consts()
    return dict(cmask=rc["cmask"], ctri=rc["ctri"], cind=rc["cind"], cid2=rc["cid2"], cs1=cs1, cs2=cs2, rows=rows,
                ident=np.eye(128, dtype=np.float32).astype(ml_dtypes.bfloat16), vecs=pack_vecs(inp))


def make_inmap(inp, core, c=None):
    if c is None:
        c = make_consts(inp)
    x = inp["x"][2 * core:2 * core + 2]
    return {"xT": np.ascontiguousarray(x.reshape(T, D).T), "vecs": c["vecs"],
            "fno_w": inp["fno_w_out"][0], "wg": inp["ffn_w_gate"], "wu": inp["ffn_w_up"], "wd": inp["ffn_w_down"],
            "cs1": c["cs1"], "cs2": c["cs2"],
            "w_rkv": inp["rwkv_w_rkv"][0], "w_o": inp["rwkv_w_o"][0], "w1": inp["rwkv_w1"][0], "w2": inp["rwkv_w2"][0],
            "a1": inp["rwkv_a1"][0], "a2": inp["rwkv_a2"][0], "g1": inp["rwkv_g1"][0], "g2": inp["rwkv_g2"][0],
            "rows": c["rows"], "ident": c["ident"], "cmask": c["cmask"], "ctri": c["ctri"], "cind": c["cind"], "cid2": c["cid2"]}


def kernel(**inputs):
    inp = {k: np.asarray(v) for k, v in inputs.items()}
    nc, _ = build()
    consts = make_consts(inp)
    in_maps = [make_inmap(inp, c, consts) for c in range(8)]
    res = run_bass_kernel_spmd(nc, in_maps, core_ids=list(range(8)))
    out = np.empty((16, S, D), np.float32)
    for c in range(8):
        out[2 * c:2 * c + 2] = np.asarray(res.results[c]["outT"]).T.reshape(2, S, D)
    return out
```

```python
import numpy as np
import ml_dtypes
from contextlib import ExitStack
import concourse.bass as bass
import concourse.mybir as mybir
from concourse.bass_utils import run_bass_kernel_spmd

ENG_EPOCH = 20000


class Buf:
    __slots__ = ("name", "w", "r")

    def __init__(self, name):
        self.name = name
        self.w = []
        self.r = []


class Op:
    __slots__ = ("eng", "emit", "deps", "sig", "is_dma", "semkey", "ndma", "tok", "idx")


class Prog:
    def __init__(self, nc, stack):
        self.nc = nc
        self.stack = stack
        self.ops = []
        self.engs = {"pe": nc.tensor, "dve": nc.vector, "act": nc.scalar, "pool": nc.gpsimd, "sp": nc.sync}
        self.last_dma = {}

    def _record(self, op, reads, writes):
        deps = []
        raw = set()
        for b in reads:
            deps.extend(b.w)
            for d in b.w:
                raw.add(id(d))
        for b in writes:
            deps.extend(b.w)
            deps.extend(b.r)
        out = []
        seen = set()
        for d in deps:
            if id(d) in seen or d is op:
                continue
            seen.add(id(d))
            if (not d.is_dma) and d.eng == op.eng and not op.is_dma:
                if op.eng == "pe":
                    continue
            out.append(d)
        op.deps = out
        for d in out:
            d.sig = True
        for b in reads:
            b.r.append(op)
        for b in writes:
            if b.r:
                b.w = [op]
            else:
                b.w = b.w + [op]
            b.r = []
        op.idx = len(self.ops)
        self.ops.append(op)
        return op

    def op(self, eng, emit, reads=(), writes=()):
        o = Op()
        o.eng = eng; o.emit = emit; o.sig = False; o.is_dma = False
        o.semkey = None; o.ndma = 0; o.tok = None
        return self._record(o, list(reads), list(writes))

    def dma(self, eng, emits, reads=(), writes=(), semkey=None):
        o = Op()
        o.eng = eng; o.emit = emits; o.sig = True; o.is_dma = True
        o.semkey = semkey; o.ndma = len(emits); o.tok = None
        prev = self.last_dma.get(semkey)
        self._record(o, list(reads), list(writes))
        if prev is not None and prev not in o.deps:
            o.deps.append(prev)
        self.last_dma[semkey] = o
        return o

    def barrier(self, bufs=()):
        last = {}
        for o in self.ops:
            if o.emit is None:
                continue
            key = ("dma", o.semkey) if o.is_dma else ("eng", o.eng)
            last[key] = o
        deps = list(last.values())
        for e in ("pe", "dve", "act", "pool", "sp"):
            o = Op()
            o.eng = e; o.emit = None; o.sig = False; o.is_dma = False
            o.semkey = None; o.ndma = 0; o.tok = None
            o.deps = [d for d in deps]
            for d in o.deps:
                d.sig = True
            o.idx = len(self.ops)
            self.ops.append(o)

    def finalize(self):
        nc = self.nc
        eng_cnt = {}
        eng_sems = {}
        dma_sems = {}
        dma_cnt = {}

        def eng_sem(e, epoch):
            k = (e, epoch)
            if k not in eng_sems:
                eng_sems[k] = self.stack.enter_context(nc.semaphore("s_%s_%d" % (e, epoch)))
            return eng_sems[k]

        for o in self.ops:
            if o.is_dma:
                if o.semkey not in dma_sems:
                    dma_sems[o.semkey] = self.stack.enter_context(nc.semaphore("d_%s" % (o.semkey,)))
                    dma_cnt[o.semkey] = 0
                dma_cnt[o.semkey] += 16 * o.ndma
                o.tok = (dma_sems[o.semkey], dma_cnt[o.semkey], ("d", o.semkey))
            elif o.sig:
                c = eng_cnt.get(o.eng, 0) + 1
                eng_cnt[o.eng] = c
                epoch = (c - 1) // ENG_EPOCH
                o.tok = (eng_sem(o.eng, epoch), c - epoch * ENG_EPOCH, ("e", o.eng, epoch))
        known = {e: {} for e in self.engs}
        nwait = 0
        for o in self.ops:
            E = self.engs[o.eng]
            kn = known[o.eng]
            need = {}
            for d in o.deps:
                sem, val, key = d.tok
                if kn.get(key, 0) >= val:
                    continue
                if key not in need or need[key][1] < val:
                    need[key] = (sem, val)
            for key, (sem, val) in need.items():
                E.wait_ge(sem, val)
                kn[key] = val
                nwait += 1
            if o.emit is None:
                continue
            if o.is_dma:
                sem = o.tok[0]
                for f in o.emit:
                    f(E).then_inc(sem, 16)
            else:
                ins = o.emit(E)
                if o.sig:
                    ins.then_inc(o.tok[0], 1)
        self.stats = dict(nops=len(self.ops), nwait=nwait, nsem=len(eng_sems) + len(dma_sems))
        return self.stats


class Arena:
    def __init__(self, base_ap, words):
        self.base = base_ap
        self.words = words
        self.top = 0

    def mark(self):
        return self.top

    def release(self, m):
        self.top = m

    def alloc(self, nelem, dtype, parts=128):
        bpe = 2 if dtype == mybir.dt.bfloat16 else 4
        nw = (nelem * bpe + 3) // 4
        nw = (nw + 7) // 8 * 8
        assert self.top + nw <= self.words, "SBUF arena overflow: %d + %d > %d" % (self.top, nw, self.words)
        ap = self.base[0:parts, self.top:self.top + nw]
        self.top += nw
        if dtype != mybir.dt.float32:
            ap = ap.bitcast(dtype)
        return ap[:, 0:nelem]


F32 = mybir.dt.float32
BF16 = mybir.dt.bfloat16
ALU = mybir.AluOpType
AF = mybir.ActivationFunctionType
AX = mybir.AxisListType

D = 1024; KD = 8; T = 4096; S = 2048
TP = 128
NBLK = T // 128
CDEC = float(np.exp(-0.5))
DBG_LIMIT = None
DBG_ITERS = None
ROWS = ["k_k", "k_a", "w0_0", "w0_1", "a0_0", "a0_1", "r_k", "ln_w", "ln_b"]
RI = {n: i for i, n in enumerate(ROWS)}
SCR_F32 = ["s_bonus", "s_g", "s_y0", "s_y1", "s_r32", "s_k32", "s_v32"]
SCR_TOK = ["s_v", "s_ka0", "s_ka1", "s_kh0", "s_kh1", "s_bh0", "s_bh1"]
SCR_CH = ["c_kt0", "c_kt1", "c_rt0", "c_rt1", "c_ktl0", "c_ktl1", "c_bt0", "c_bt1"]


def declare(ctx_dr, drh, nc, din, debug):
    din("w_rkv", [3, D, D]); din("w_o", [D, D])
    din("w1", [2, D, 64]); din("w2", [2, 64, D]); din("a1", [2, D, 64]); din("a2", [2, 64, D])
    din("g1", [D, 128]); din("g2", [128, D]); din("rows", [len(ROWS), D]); din("ident", [128, 128], BF16)
    din("cmask", [128, 2, 5, 128]); din("ctri", [128, 2, 3, 128], BF16); din("cind", [128, 2], BF16); din("cid2", [128, 64])
    for n in SCR_F32:
        h = nc.dram_tensor(n, [T, D], F32, kind=("ExternalOutput" if debug else "Internal")); drh[n] = h; ctx_dr[n] = h.ap()
    for n in SCR_TOK:
        h = nc.dram_tensor(n, [T, D], BF16, kind="Internal"); drh[n] = h; ctx_dr[n] = h.ap()
    for n in SCR_CH:
        h = nc.dram_tensor(n, [D, T], BF16, kind="Internal"); drh[n] = h; ctx_dr[n] = h.ap()
    h = nc.dram_tensor("c_lt", [3, 128, T], BF16, kind="Internal"); drh["c_lt"] = h; ctx_dr["c_lt"] = h.ap()


def rwkv_consts():
    import ml_dtypes
    s = np.arange(128)[:, None]; t = np.arange(128)[None, :]
    same = (s // 64) == (t // 64)
    cmask = np.zeros((128, 2, 5, 128), np.float32)
    ctri = np.zeros((128, 2, 3, 128), np.float32)
    for d in range(2):
        rs = (s < t) if d == 0 else (s > t)
        ri = (s <= t) if d == 0 else (s >= t)
        ro = (s > t) if d == 0 else (s < t)
        cmask[:, d, 0, :] = -1.0 * (rs & same); cmask[:, d, 1, :] = (ri & same)
        cmask[:, d, 2, :] = (rs & same); cmask[:, d, 3, :] = (ri & same)
        cmask[:, d, 4, :] = -1.0 * ((rs & same).T)
        ctri[:, d, 0, :] = (ri & same); ctri[:, d, 1, :] = (rs & same); ctri[:, d, 2, :] = (ro & same)
    cind = np.zeros((128, 2), np.float32); cind[:64, 0] = 1; cind[64:, 1] = 1
    cid2 = np.zeros((128, 64), np.float32); cid2[np.arange(128), np.arange(128) % 64] = 1
    return dict(cmask=cmask, ctri=ctri.astype(ml_dtypes.bfloat16), cind=cind.astype(ml_dtypes.bfloat16), cid2=cid2)


def phase_rwkv(c, src, dst, sub=("prep", "scan", "post")):
    P = c.P; A = c.A; dr = c.dr; dbuf = c.dbuf; getbank = c.getbank; vcol = c.vcol
    vecs_b = c.vecs_b; ones_b = c.ones_b; onesD = c.onesD; drh = c.drh
    for n in SCR_F32 + SCR_TOK + SCR_CH + ["c_lt"]:
        if n not in dbuf:
            dbuf[n] = Buf("dram_" + n)
    sv = dr[src].rearrange("(k p) t -> p k t", p=128)
    dv = dr[dst].rearrange("(k p) t -> p k t", p=128)

    def alloc3(k, n, dt):
        return A.alloc(k * n, dt).rearrange("p (k n) -> p k n", k=k)

    def load_rows(names):
        out = {}
        for n in names:
            ap = A.alloc(D, F32); b = Buf("row_" + n)
            P.dma("sp", [lambda e, ap=ap, n=n: e.dma_start(out=ap, in_=dr["rows"][RI[n]].partition_broadcast(128))],
                  writes=[b], semkey="row_" + n)
            out[n] = (ap, b)
        return out

    def h3(ap):
        return ap.rearrange("p (h n) -> p h n", h=16)

    def cload(name, nelem, dt, shape_str=None, **kw):
        ap = A.alloc(nelem, dt); b = Buf("c_" + name)
        src_ap = dr[name]
        P.dma("sp", [lambda e: e.dma_start(out=ap, in_=src_ap.rearrange(shape_str, **kw) if shape_str else src_ap)], writes=[b], semkey="c_" + name)
        return ap, b

    PC = [A.alloc(8 * 64, F32).rearrange("p (j c) -> p j c", j=8) for _ in range(2)]
    PC_b = [Buf("pc0"), Buf("pc1")]
    ident = A.alloc(128, BF16); ident_b = Buf("ident")
    P.dma("sp", [lambda e: e.dma_start(out=ident, in_=dr["ident"])], writes=[ident_b], semkey="ident")

    def prep():
        m = A.mark()
        stages = [(A.alloc(D, F32), Buf("wst%d" % i)) for i in range(2)]
        srr = [0]

        def stage_cast(src_ap, dst_ap, n_, dst_buf, view=None):
            i = srr[0] % 2; srr[0] += 1
            stg, stg_b = stages[i]
            sview = stg[:, 0:n_] if view is None else view(stg[:, 0:n_])
            wb = dst_buf if isinstance(dst_buf, list) else [dst_buf]
            P.dma("sp", [lambda e: e.dma_start(out=sview, in_=src_ap)], writes=[stg_b], semkey="wst%d" % i)
            if (srr[0] // 2) % 2 == 0:
                P.op("dve", lambda e: e.tensor_copy(dst_ap, sview), reads=[stg_b], writes=wb)
            else:
                P.op("act", lambda e: e.copy(dst_ap, sview), reads=[stg_b], writes=wb)

        def load_w(w2d, K, N, tag, npart=128):
            dstw = A.alloc(K * N, BF16).rearrange("p (k n) -> p k n", k=K)
            bufs = [Buf("%s_%d" % (tag, k)) for k in range(K)]
            wv = w2d.rearrange("(k p) n -> p k n", p=npart)
            for k in range(K):
                stage_cast(wv[:, k, :], dstw[:, k, :], N, bufs[k])
            return dstw, bufs
        Wr, Wr_b = load_w(dr["w_rkv"][0], KD, D, "wr")
        Wk, Wk_b = load_w(dr["w_rkv"][1], KD, D, "wk")
        Wv, Wv_b = load_w(dr["w_rkv"][2], KD, D, "wv")
        w1c = A.alloc(KD * 128, BF16).rearrange("p (k n) -> p k n", k=KD); w1c_b = [Buf("w1c%d" % k) for k in range(KD)]
        a1c = A.alloc(KD * 128, BF16).rearrange("p (k n) -> p k n", k=KD); a1c_b = [Buf("a1c%d" % k) for k in range(KD)]
        for (dstw, bufs, name) in ((w1c, w1c_b, "w1"), (a1c, a1c_b, "a1")):
            for j in range(2):
                wv = dr[name][j].rearrange("(k p) n -> p k n", p=128)
                stage_cast(wv, dstw[:, :, j * 64:(j + 1) * 64], 512, bufs, view=lambda ap: ap.rearrange("p (k n) -> p k n", k=KD))
        G1 = A.alloc(KD * 128, BF16).rearrange("p (k n) -> p k n", k=KD); G1_b = [Buf("g1_%d" % k) for k in range(KD)]
        stage_cast(dr["g1"].rearrange("(k p) n -> p k n", p=128), G1, 1024, G1_b, view=lambda ap: ap.rearrange("p (k n) -> p k n", k=KD))
        NH = TP + 2
        FS = []
        for i in range(3):
            FS.append({"xt": (alloc3(KD, NH, F32), Buf("xt%d" % i)), "hf": (alloc3(KD, NH, F32), Buf("hf%d" % i)),
                       "sq": (alloc3(KD, NH, BF16), Buf("sq%d" % i)), "rstd": (A.alloc(NH, F32), Buf("rstd%d" % i)),
                       "xx": (alloc3(KD, TP, F32), Buf("xx%d" % i)), "tmp": (alloc3(KD, TP, F32), Buf("tmp%d" % i)),
                       "xs": [alloc3(KD, TP, BF16) for _ in range(6)], "xs_b": [Buf("xs%d_%d" % (i, c_)) for c_ in range(6)]})
        W2 = [{n: (A.alloc(D, F32), Buf("%s_%d" % (n, i))) for n in ("tr", "tk", "tv")} for i in range(3)]
        LT = [{n: (A.alloc(TP, BF16), Buf("%s_%d" % (n, i))) for n in ("twT", "taT", "sgT")} for i in range(3)]

        def fe(ti):
            t0 = ti * TP
            ss_ = ti % 3
            F_ = FS[ss_]
            xt, xt_b = F_["xt"]; hf, hf_b = F_["hf"]; sq, sq_b = F_["sq"]; rstd, rstd_b = F_["rstd"]
            xx, xx_b = F_["xx"]; tmp, tmp_b = F_["tmp"]; xs = F_["xs"]; xs_b = F_["xs_b"]
            first = (t0 % S == 0); last = ((t0 + TP) % S == 0)
            lo_ = 1 if first else 0; hi_ = NH - 1 if last else NH
            P.dma("sp", [lambda e, t0=t0, lo_=lo_, hi_=hi_: e.dma_start(out=xt[:, :, lo_:hi_], in_=sv[:, :, t0 - 1 + lo_:t0 - 1 + hi_])],
                  reads=[dbuf[src]], writes=[xt_b], semkey="xt%d" % ss_)
            if first:
                P.op("pool", lambda e: e.memset(xt[:, :, 0:1], 0.0), writes=[xt_b])
            if last:
                P.op("pool", lambda e: e.memset(xt[:, :, NH - 1:NH], 0.0), writes=[xt_b])
            P.op("act", lambda e: e.activation(out=sq, in_=xt, func=AF.Square), reads=[xt_b], writes=[sq_b])
            bk, bk_b = getbank()

            def mm(e, bk=bk):
                ins = None
                for k in range(KD):
                    ins = e.matmul(bk[:, 0:NH], onesD[:, :], sq[:, k, :], start=(k == 0), stop=(k == KD - 1))
                return ins
            P.op("pe", mm, reads=[sq_b, ones_b], writes=[bk_b])
            P.op("act", lambda e, bk=bk: e.activation(out=rstd, in_=bk[:, 0:NH], func=AF.Sqrt, bias=1e-6, scale=1.0),
                 reads=[bk_b], writes=[rstd_b])
            P.op("dve", lambda e: e.reciprocal(rstd, rstd), reads=[rstd_b], writes=[rstd_b])
            yield
            for k in range(KD):
                P.op("dve", lambda e, k=k: e.scalar_tensor_tensor(out=hf[:, k, :], in0=xt[:, k, :], scalar=vcol("nm1", k),
                                                                  in1=rstd, op0=ALU.mult, op1=ALU.mult),
                     reads=[xt_b, rstd_b, vecs_b], writes=[hf_b])
            P.op("pool", lambda e: e.tensor_tensor(out=tmp, in0=hf[:, :, 0:TP], in1=hf[:, :, 2:TP + 2], op=ALU.add),
                 reads=[hf_b], writes=[tmp_b])
            P.op("dve", lambda e: e.scalar_tensor_tensor(out=xx, in0=tmp, scalar=0.5, in1=hf[:, :, 1:TP + 1],
                                                         op0=ALU.mult, op1=ALU.subtract),
                 reads=[tmp_b, hf_b], writes=[xx_b])
            yield
            for ci in (3, 4, 5, 0, 1, 2):
                for k in range(KD):
                    P.op("dve", lambda e, ci=ci, k=k: e.scalar_tensor_tensor(
                        out=xs[ci][:, k, :], in0=xx[:, k, :], scalar=vcol("mu%d" % ci, k), in1=hf[:, k, 1:TP + 1],
                        op0=ALU.mult, op1=ALU.add), reads=[xx_b, hf_b, vecs_b], writes=[xs_b[ci]])
                yield
            for (wc, wc_b, xi, oname, func) in ((w1c, w1c_b, 3, "twT", AF.Tanh), (a1c, a1c_b, 4, "taT", None), (G1, G1_b, 5, "sgT", AF.Sigmoid)):
                outT, outT_b = LT[ss_][oname]
                bk, bk_b = getbank()

                def mm(e, bk=bk, wc=wc, xi=xi):
                    ins = None
                    for k in range(KD):
                        ins = e.matmul(bk[:, 0:TP], wc[:, k, :], xs[xi][:, k, :], start=(k == 0), stop=(k == KD - 1))
                    return ins
                P.op("pe", mm, reads=[xs_b[xi]] + wc_b, writes=[bk_b])
                if func is None:
                    P.op("act", lambda e, bk=bk, outT=outT: e.copy(outT, bk[:, 0:TP]), reads=[bk_b], writes=[outT_b])
                else:
                    P.op("act", lambda e, bk=bk, outT=outT, func=func: e.activation(out=outT, in_=bk[:, 0:TP], func=func),
                         reads=[bk_b], writes=[outT_b])
            yield
            for (xi, Wm, Wm_b, tile) in ((0, Wr, Wr_b, "tr"), (1, Wk, Wk_b, "tk"), (2, Wv, Wv_b, "tv")):
                ap, b = W2[ss_][tile]
                for half in range(2):
                    bk, bk_b = getbank()

                    def mm(e, bk=bk, half=half, xi=xi, Wm=Wm):
                        ins = None
                        for k in range(KD):
                            ins = e.matmul(bk[:, :], xs[xi][:, k, :], Wm[:, k, half * 512:(half + 1) * 512],
                                           start=(k == 0), stop=(k == KD - 1))
                        return ins
                    P.op("pe", mm, reads=[xs_b[xi]] + Wm_b, writes=[bk_b])
                    P.op("act", lambda e, bk=bk, half=half, ap=ap: e.copy(ap[:, half * 512:(half + 1) * 512], bk[:, :]),
                         reads=[bk_b], writes=[b])
                yield
            for (tile, scr) in (("tr", "s_r32"), ("tk", "s_k32"), ("tv", "s_v32")):
                ap, b = W2[ss_][tile]
                P.dma("act", [lambda e, ap=ap, scr=scr, t0=t0: e.dma_start(out=dr[scr][t0:t0 + 128, :], in_=ap)],
                      reads=[b], writes=[dbuf[scr]], semkey="fst_%s_%d" % (tile, ss_))
            for li, oname in enumerate(("twT", "taT", "sgT")):
                ap, b = LT[ss_][oname]
                P.dma("act", [lambda e, ap=ap, li=li, t0=t0: e.dma_start(out=dr["c_lt"][li, :, t0:t0 + 128], in_=ap)],
                      reads=[b], writes=[dbuf["c_lt"]], semkey="fst_%s_%d" % (oname, ss_))
            yield


        def drive_window(make_gen, n, width):
            active = []; nxt = 0
            while nxt < n or active:
                while len(active) < width and nxt < n:
                    active.append(make_gen(nxt)); nxt += 1
                for g in list(active):
                    try:
                        next(g)
                    except StopIteration:
                        active.remove(g)

        nb_ = NBLK if DBG_LIMIT is None else DBG_LIMIT
        drive_window(fe, nb_, 3)
        P.barrier()
        A.release(m)

        m = A.mark()
        stages = [(A.alloc(D, F32), Buf("wsu%d" % i)) for i in range(2)]
        srr[0] = 0
        w2c = A.alloc(D, BF16); w2c_b = Buf("w2c")
        a2c = A.alloc(D, BF16); a2c_b = Buf("a2c")
        G2 = A.alloc(D, BF16); G2_b = Buf("g2")
        for (dstw, b, srcap) in ((w2c, w2c_b, dr["w2"].rearrange("j r n -> (j r) n")),
                                 (a2c, a2c_b, dr["a2"].rearrange("j r n -> (j r) n")), (G2, G2_b, dr["g2"])):
            stage_cast(srcap, dstw, D, b)
        rows = load_rows(["k_k", "k_a", "w0_0", "w0_1", "a0_0", "a0_1", "r_k"])
        ctri_ap, ctri_b = cload("ctri", 2 * 3 * 128, BF16, "p d m t -> p (d m t)")
        ctri = ctri_ap.rearrange("p (d m t) -> p d m t", d=2, m=3)
        cind, cind_b = cload("cind", 2, BF16)
        NCH = 2
        CS = []
        for i in range(NCH):
            names = ["tr", "tk", "tv", "tkap", "ta", "tw", "tx", "ty", "te0", "te1"]
            CS.append({"W": {n: (A.alloc(D, F32), Buf("%s_c%d" % (n, i))) for n in names},
                       "twT": (A.alloc(TP, BF16), Buf("twT_c%d" % i)), "taT": (A.alloc(TP, BF16), Buf("taT_c%d" % i)),
                       "sgT": (A.alloc(TP, BF16), Buf("sgT_c%d" % i)),
                       "hi": (A.alloc(D, BF16), Buf("hi_c%d" % i)), "lo": (A.alloc(D, BF16), Buf("lo_c%d" % i)),
                       "ss": (A.alloc(16, F32), Buf("ss_c%d" % i)),
                       "rk": [A.alloc(16, F32) for _ in range(2)], "rk_b": [Buf("rk0_c%d" % i), Buf("rk1_c%d" % i)], "terr": [0]})
        NOB = 8; NOC = 6
        OB = [(A.alloc(D, BF16), Buf("ob%d" % i)) for i in range(NOB)]
        OC = [(A.alloc(D, BF16), Buf("oc%d" % i)) for i in range(NOC)]
        rrc = {"ob": 0, "oc": 0}

        def be(ti):
            tb0 = ti * TP
            ss_ = ti % NCH
            C_ = CS[ss_]
            W = C_["W"]
            tr, tr_b = W["tr"]; tk, tk_b = W["tk"]; tv, tv_b = W["tv"]
            twT, twT_b = C_["twT"]; taT, taT_b = C_["taT"]; sgT, sgT_b = C_["sgT"]
            hi, hi_b = C_["hi"]; lo, lo_b = C_["lo"]; ss, ss_b = C_["ss"]; rk = C_["rk"]; rk_b = C_["rk_b"]
            Wl = W
            for (tile, scr) in (("tr", "s_r32"), ("tk", "s_k32"), ("tv", "s_v32")):
                ap, b = W[tile]
                P.dma("sp", [lambda e, ap=ap, scr=scr, tb0=tb0: e.dma_start(out=ap, in_=dr[scr][tb0:tb0 + 128, :])],
                      reads=[dbuf[scr]], writes=[b], semkey="bld_%s_%d" % (tile, ss_))
            for li, (ap, b) in enumerate(((twT, twT_b), (taT, taT_b), (sgT, sgT_b))):
                P.dma("sp", [lambda e, ap=ap, li=li, tb0=tb0: e.dma_start(out=ap, in_=dr["c_lt"][li, :, tb0:tb0 + 128])],
                      reads=[dbuf["c_lt"]], writes=[b], semkey="bld_lt%d_%d" % (li, ss_))
            yield

            def st(ap, b, scr, key):
                P.dma("act", [lambda e, ap=ap, scr=scr, tb0=tb0: e.dma_start(out=dr[scr][tb0:tb0 + 128, :], in_=ap)],
                      reads=[b], writes=[dbuf[scr]], semkey="st_" + key)

            def getob():
                i = rrc["ob"] % NOB; rrc["ob"] += 1
                return OB[i] + ("ob%d" % i,)

            def emit_tok(srcname, te, te_b, scr):
                ap, b, key = getob()
                s_ap, s_b = Wl[srcname]
                eng = "pool" if rrc["ob"] % 4 == 0 else "dve"
                P.op(eng, lambda e, ap=ap, s_ap=s_ap, te=te: e.tensor_tensor(out=ap, in0=s_ap, in1=te, op=ALU.mult),
                     reads=[s_b, te_b], writes=[b])
                st(ap, b, scr, key)

            def emit_ch(srcname, te, te_b, scr):
                ap, b, key = getob()
                s_ap, s_b = Wl[srcname]
                eng = "pool" if rrc["ob"] % 4 == 0 else "dve"
                P.op(eng, lambda e, ap=ap, s_ap=s_ap, te=te: e.tensor_tensor(out=ap, in0=s_ap, in1=te, op=ALU.mult),
                     reads=[s_b, te_b], writes=[b])
                bk, bk_b = getbank()
                bkb = bk[:, :].bitcast(BF16)

                def trp(e, bkb=bkb, ap=ap):
                    ins = None
                    for q in range(8):
                        ins = e.transpose(bkb[:, q * 128:(q + 1) * 128], ap[:, q * 128:(q + 1) * 128], ident[:, :])
                    return ins
                P.op("pe", trp, reads=[b, ident_b], writes=[bk_b])
                i = rrc["oc"] % NOC; rrc["oc"] += 1
                oc, oc_b = OC[i]
                P.op("act", lambda e, oc=oc, bkb=bkb: e.copy(oc, bkb), reads=[bk_b], writes=[oc_b])
                P.dma("act", [lambda e, oc=oc, scr=scr, tb0=tb0: e.dma_start(
                    out=dr[scr].rearrange("(j p) t -> p j t", p=128)[:, :, tb0:tb0 + 128], in_=oc.rearrange("p (j t) -> p j t", j=8))],
                    reads=[oc_b], writes=[dbuf[scr]], semkey="stc%d" % i)

            tkap, tkap_b = W["tkap"]
            tx, tx_b = W["tx"]; ty, ty_b = W["ty"]; ta, ta_b = W["ta"]; tw, tw_b = W["tw"]
            vb, vb_b, vkey = getob()
            P.op("act", lambda e, vb=vb: e.copy(vb, tv), reads=[tv_b], writes=[vb_b])
            st(vb, vb_b, "s_v", vkey)
            P.op("dve", lambda e: e.tensor_tensor(out=tx, in0=tk, in1=rows["k_k"][0], op=ALU.mult),
                 reads=[tk_b, rows["k_k"][1]], writes=[tx_b])
            P.op("act", lambda e: e.activation(out=ty, in_=tx, func=AF.Square), reads=[tx_b], writes=[ty_b])
            P.op("dve", lambda e: e.tensor_reduce(out=ss, in_=h3(ty), axis=AX.X, op=ALU.add), reads=[ty_b], writes=[ss_b])
            P.op("dve", lambda e: e.tensor_scalar(ss, ss, 1e-24, None, ALU.max), reads=[ss_b], writes=[ss_b])
            P.op("act", lambda e: e.activation(out=ss, in_=ss, func=AF.Sqrt), reads=[ss_b], writes=[ss_b])
            P.op("dve", lambda e: e.reciprocal(ss, ss), reads=[ss_b], writes=[ss_b])
            P.op("dve", lambda e: e.tensor_tensor(out=h3(tkap), in0=h3(tx), in1=ss.unsqueeze(2).to_broadcast([128, 16, 64]), op=ALU.mult),
                 reads=[tx_b, ss_b], writes=[tkap_b])
            yield
            for j in range(2):
                for (cT, cT_b, c2, c2_b, dstt, dstt_b, rown) in ((twT, twT_b, w2c, w2c_b, tw, tw_b, "w0_%d" % j),
                                                                (taT, taT_b, a2c, a2c_b, ta, ta_b, "a0_%d" % j)):
                    for half in range(2):
                        bk, bk_b = getbank()
                        P.op("pe", lambda e, bk=bk, cT=cT, c2=c2, half=half, j=j: e.matmul(
                            bk[:, :], cT[j * 64:(j + 1) * 64, :], c2[j * 64:(j + 1) * 64, half * 512:(half + 1) * 512],
                            start=True, stop=True), reads=[cT_b, c2_b], writes=[bk_b])
                        P.op("dve", lambda e, bk=bk, dstt=dstt, half=half, rown=rown: e.tensor_tensor(
                            out=dstt[:, half * 512:(half + 1) * 512], in0=bk[:, :], in1=rows[rown][0][:, half * 512:(half + 1) * 512],
                            op=ALU.add), reads=[bk_b, rows[rown][1]], writes=[dstt_b])
                P.op("act", lambda e: e.activation(out=tw, in_=tw, func=AF.Sigmoid), reads=[tw_b], writes=[tw_b])
                P.op("act", lambda e: e.activation(out=ta, in_=ta, func=AF.Sigmoid), reads=[ta_b], writes=[ta_b])
                yield
                P.op("pool", lambda e: e.tensor_tensor(out=tx, in0=tkap, in1=ta, op=ALU.mult), reads=[tkap_b, ta_b], writes=[tx_b])
                P.op("dve", lambda e: e.scalar_tensor_tensor(out=ty, in0=ta, scalar=-1.0, in1=rows["k_a"][0], op0=ALU.add, op1=ALU.mult),
                     reads=[ta_b, rows["k_a"][1]], writes=[ty_b])
                P.op("dve", lambda e: e.scalar_tensor_tensor(out=ta, in0=ty, scalar=1.0, in1=tk, op0=ALU.add, op1=ALU.mult),
                     reads=[ty_b, tk_b], writes=[ta_b])
                P.op("pool", lambda e: e.tensor_tensor(out=ty, in0=tr, in1=rows["r_k"][0], op=ALU.mult),
                     reads=[tr_b, rows["r_k"][1]], writes=[ty_b])
                P.op("pool", lambda e: e.tensor_tensor(out=ty, in0=ty, in1=ta, op=ALU.mult), reads=[ty_b, ta_b], writes=[ty_b])
                P.op("dve", lambda e, j=j: e.tensor_reduce(out=rk[j], in_=h3(ty), axis=AX.X, op=ALU.add), reads=[ty_b], writes=[rk_b[j]])
                P.op("act", lambda e: e.copy(hi, tw), reads=[tw_b], writes=[hi_b])
                P.op("dve", lambda e: e.tensor_tensor(out=lo, in0=tw, in1=hi, op=ALU.subtract), reads=[tw_b, hi_b], writes=[lo_b])
                yield
                bk, bk_b = getbank()

                def mmp(e, bk=bk):
                    ins = None
                    for pj in range(8):
                        for (src_, fl) in ((hi, 0), (lo, 1)):
                            ins = e.matmul(bk[:, pj * 2:pj * 2 + 2], src_[:, pj * 128:(pj + 1) * 128], cind[:, :], start=(fl == 0), stop=(fl == 1))
                    return ins
                P.op("pe", mmp, reads=[hi_b, lo_b, cind_b], writes=[bk_b])
                P.op("act", lambda e, bk=bk, j=j, ti=ti: e.activation(out=PC[j][:, :, 2 * ti:2 * ti + 2],
                                                                     in_=bk[:, 0:16].rearrange("p (j c) -> p j c", j=8), func=AF.Exp, scale=-CDEC),
                     reads=[bk_b], writes=[PC_b[j]])
                for mi in range(3):
                    cb = {}
                    for half in range(2):
                        bk, bk_b = getbank()

                        def mmc(e, bk=bk, mi=mi, half=half, j=j):
                            e.matmul(bk[:, :], ctri[:, j, mi, :], hi[:, half * 512:(half + 1) * 512], start=True, stop=False)
                            return e.matmul(bk[:, :], ctri[:, j, mi, :], lo[:, half * 512:(half + 1) * 512], start=False, stop=True)
                        P.op("pe", mmc, reads=[hi_b, lo_b, ctri_b], writes=[bk_b])
                        cb[half] = (bk, bk_b)

                    def expo(scale, cb=cb):
                        i = C_["terr"][0] % 2; C_["terr"][0] += 1
                        te, te_b = W["te%d" % i]
                        for half in range(2):
                            bk, bk_b = cb[half]
                            P.op("act", lambda e, te=te, bk=bk, half=half, scale=scale: e.activation(
                                out=te[:, half * 512:(half + 1) * 512], in_=bk[:, :], func=AF.Exp, scale=scale), reads=[bk_b], writes=[te_b])
                        return te, te_b
                    if mi == 0:
                        te, te_b = expo(-CDEC)
                        te2, te2_b = expo(CDEC)
                        emit_ch("tr", te, te_b, "c_rt%d" % j)
                        yield
                        emit_ch("ta", te2, te2_b, "c_ktl%d" % j)
                        emit_ch("tx", te2, te2_b, "c_bt%d" % j)
                    elif mi == 1:
                        te, te_b = expo(-CDEC)
                        emit_tok("tkap", te, te_b, "s_ka%d" % j)
                        emit_ch("tkap", te, te_b, "c_kt%d" % j)
                    else:
                        te, te_b = expo(-CDEC)
                        emit_tok("ta", te, te_b, "s_kh%d" % j)
                        emit_tok("tx", te, te_b, "s_bh%d" % j)
                    yield
            P.op("dve", lambda e: e.tensor_tensor(out=rk[0], in0=rk[0], in1=rk[1], op=ALU.add), reads=[rk_b[0], rk_b[1]], writes=[rk_b[0]])
            P.op("dve", lambda e: e.tensor_tensor(out=h3(ty), in0=h3(tv), in1=rk[0].unsqueeze(2).to_broadcast([128, 16, 64]), op=ALU.mult),
                 reads=[tv_b, rk_b[0]], writes=[ty_b])
            st(ty, ty_b, "s_bonus", "ty")
            for half in range(2):
                bk, bk_b = getbank()
                P.op("pe", lambda e, bk=bk, half=half: e.matmul(bk[:, :], sgT[:, :], G2[:, half * 512:(half + 1) * 512],
                                                                start=True, stop=True), reads=[sgT_b, G2_b], writes=[bk_b])
                P.op("act", lambda e, bk=bk, half=half: e.copy(tw[:, half * 512:(half + 1) * 512], bk[:, :]),
                     reads=[bk_b], writes=[tw_b])
            st(tw, tw_b, "s_g", "tw")
            yield


        drive_window(be, nb_, NCH)
        P.barrier()
        A.release(m)

    def scan():
        m = A.mark()
        cm_ap, cm_b = cload("cmask", 2 * 5 * 128, F32, "p d m t -> p (d m t)")
        cmask = cm_ap.rearrange("p (d m t) -> p d m t", d=2, m=5)
        id2, id2_b = cload("cid2", 64, F32)

        def a4(n_, dt=BF16):
            return A.alloc(16 * n_, dt).rearrange("p (h n) -> p h n", h=16)
        NSLOT = 2
        IN = []
        for s_ in range(NSLOT):
            d_ = {}
            d_["KR"] = (A.alloc(8 * 2 * 128, BF16).rearrange("p (j c t) -> p j c t", j=8, c=2), Buf("KR%d" % s_))
            d_["KT"] = (alloc3(8, 128, BF16), Buf("KT%d" % s_))
            d_["BT"] = (alloc3(8, 128, BF16), Buf("BT%d" % s_))
            for n in ("V", "KA", "KH", "BH"):
                d_[n] = (A.alloc(D, BF16), Buf("%s%d" % (n, s_)))
            IN.append(d_)
        NR = A.alloc(16 * 2 * 128, BF16).rearrange("p (h c t) -> p h c t", h=16, c=2); NR_b = [Buf("NR%d" % i) for i in range(8)]
        BK = A.alloc(16 * 2 * 128, BF16).rearrange("p (h c t) -> p h c t", h=16, c=2); BK_b = [Buf("BK%d" % i) for i in range(8)]
        Nn = a4(128); Nn_b = [Buf("Nn%d" % i) for i in range(4)]
        SS = [A.alloc(16 * 2 * 128, BF16).rearrange("p (h c t) -> p h c t", h=16, c=2) for _ in range(2)]
        SS_b = [[Buf("SS%d_%d" % (s_, i)) for i in range(8)] for s_ in range(2)]
        QT = [a4(128) for _ in range(2)]; QT_b = [[Buf("QT%d_%d" % (s_, i)) for i in range(4)] for s_ in range(2)]
        Wt = A.alloc(D, BF16); Wt_b = [Buf("Wt0"), Buf("Wt1")]
        BVt = A.alloc(D, BF16); BVt_b = [Buf("BV0"), Buf("BV1")]
        nUt = A.alloc(D, BF16); nUt_b = [Buf("nU0"), Buf("nU1")]
        diagP = [A.alloc(512, F32).rearrange("p (j k) -> p j k", j=8) for _ in range(2)]; diagP_b = [Buf("dP0"), Buf("dP1")]
        OUT = []
        for s_ in range(2):
            d_ = {}
            d_["GT"] = (A.alloc(2 * 512, BF16).rearrange("p (c j k) -> p c j k", c=2, j=8), [Buf("GT%d_0" % s_), Buf("GT%d_1" % s_)])
            d_["H"] = (A.alloc(2 * 512, F32).rearrange("p (c n) -> p c n", c=2), [Buf("H%d_0" % s_), Buf("H%d_1" % s_)])
            d_["RhT"] = (alloc3(8, 128, BF16), [Buf("Rh%d_0" % s_), Buf("Rh%d_1" % s_)])
            d_["Yl"] = (A.alloc(D, F32), [Buf("Yl%d_0" % s_), Buf("Yl%d_1" % s_)])
            OUT.append(d_)
        U = [(A.alloc(512, BF16).rearrange("p (j v) -> p j v", j=8), Buf("U%d" % i)) for i in range(4)]
        yo = [(A.alloc(D, F32), [Buf("yo%d_0" % i), Buf("yo%d_1" % i)]) for i in range(2)]
        urr = [0]

        def nextU():
            u = U[urr[0] % 4]; urr[0] += 1
            return u

        def hp(ix):
            return ix % 8, ix // 8

        def hcol(ix):
            hh_ = 2 * (ix % 8) + ix // 8
            return slice(hh_ * 64, (hh_ + 1) * 64)

        def icol(ix):
            return slice(ix * 64, (ix + 1) * 64)

        def qs(q):
            return slice(q * 64, (q + 1) * 64)

        iters = []
        for b in range(2):
            for dr_ in range(2):
                blocks = list(range(16)) if dr_ == 0 else list(range(15, -1, -1))
                for bidx, bi in enumerate(blocks):
                    iters.append((b, dr_, bi, bidx == 0))
        if DBG_ITERS is not None:
            iters = [iters[i_] for i_ in DBG_ITERS]
        chv = lambda n: dr[n].rearrange("(j p) t -> p j t", p=128)

        def issue_loads(n_):
            b, dr_, bi, _ = iters[n_]
            sl = n_ % NSLOT
            I = IN[sl]
            tb0 = b * S + bi * 128
            KR, KR_b = I["KR"]; KT, KT_b = I["KT"]; BT, BT_b = I["BT"]
            P.dma("sp", [lambda e, KR=KR, tb0=tb0, dr_=dr_: e.dma_start(out=KR[:, :, 0, :], in_=chv("c_kt%d" % dr_)[:, :, tb0:tb0 + 128]),
                         lambda e, KR=KR, tb0=tb0, dr_=dr_: e.dma_start(out=KR[:, :, 1, :], in_=chv("c_rt%d" % dr_)[:, :, tb0:tb0 + 128])],
                  reads=[dbuf["c_kt%d" % dr_], dbuf["c_rt%d" % dr_]], writes=[KR_b], semkey="lKR%d" % sl)
            P.dma("sp", [lambda e, KT=KT, tb0=tb0, dr_=dr_: e.dma_start(out=KT, in_=chv("c_ktl%d" % dr_)[:, :, tb0:tb0 + 128])],
                  reads=[dbuf["c_ktl%d" % dr_]], writes=[KT_b], semkey="lKT%d" % sl)
            P.dma("sp", [lambda e, BT=BT, tb0=tb0, dr_=dr_: e.dma_start(out=BT, in_=chv("c_bt%d" % dr_)[:, :, tb0:tb0 + 128])],
                  reads=[dbuf["c_bt%d" % dr_]], writes=[BT_b], semkey="lBT%d" % sl)
            for (n, scr) in (("V", "s_v"), ("KA", "s_ka%d" % dr_), ("KH", "s_kh%d" % dr_), ("BH", "s_bh%d" % dr_)):
                ap, b_ = I[n]
                P.dma("sp", [lambda e, ap=ap, scr=scr, tb0=tb0: e.dma_start(out=ap, in_=dr[scr][tb0:tb0 + 128, :])],
                      reads=[dbuf[scr]], writes=[b_], semkey="l%s%d" % (n, sl))

        issue_loads(0)
        u_cur = u_cur_b = None
        if True:
            if True:
                def stageA(it_):
                    (b, dr_, bi, isfirst) = iters[it_]
                    if it_ + 1 < len(iters):
                        issue_loads(it_ + 1)
                    corder = (0, 1) if dr_ == 0 else (1, 0)
                    sl = it_ % NSLOT; osl = it_ % 2; it = it_ + 1
                    I = IN[sl]; O = OUT[osl]
                    tb0 = b * S + bi * 128
                    gchunk = (b * 16 + bi) * 2
                    KR, KR_b = I["KR"]; KT, KT_b = I["KT"]; BT, BT_b = I["BT"]
                    Vt, Vt_b = I["V"]; KAt, KAt_b = I["KA"]; KHt, KHt_b = I["KH"]; BHt, BHt_b = I["BH"]
                    MK1 = cmask[:, dr_, 0:2, :]; MK2 = cmask[:, dr_, 2:4, :]; MK3 = cmask[:, dr_, 4, :]

                    def qs(q):
                        return slice(q * 64, (q + 1) * 64)

                    for (lhs, lhs_b, dstt, dstt_b, MK) in ((BT, BT_b, NR, NR_b, MK1), (KT, KT_b, BK, BK_b, MK2)):
                        for g in range(8):
                            bk, bk_b = getbank()

                            def mm(e, bk=bk, g=g, lhs=lhs, KR=KR):
                                ins = None
                                for hh in range(2):
                                    h = 2 * g + hh; j, q = hp(h)
                                    ins = e.matmul(bk[:, hh * 256:(hh + 1) * 256], lhs[qs(q), j, :],
                                                   KR[qs(q), j, :, :].rearrange("p c t -> p (c t)"), start=True, stop=True)
                                return ins
                            P.op("pe", mm, reads=[lhs_b, KR_b], writes=[bk_b])
                            P.op("dve", lambda e, bk=bk, g=g, dstt=dstt, MK=MK: e.tensor_tensor(
                                out=dstt[:, 2 * g:2 * g + 2, :, :], in0=bk[:, :].rearrange("p (h c t) -> p h c t", h=2, c=2),
                                in1=MK.unsqueeze(1).to_broadcast([128, 2, 2, 128]), op=ALU.mult),
                                reads=[bk_b, cm_b], writes=[dstt_b[g]])
                    yield
                    for g in range(4):
                        bk, bk_b = getbank()

                        def mm(e, bk=bk, g=g, KR=KR, BT=BT):
                            ins = None
                            for hh in range(4):
                                h = 4 * g + hh; j, q = hp(h)
                                ins = e.matmul(bk[:, hh * 128:(hh + 1) * 128], KR[qs(q), j, 0, :], BT[qs(q), j, :], start=True, stop=True)
                            return ins
                        P.op("pe", mm, reads=[KR_b, BT_b], writes=[bk_b])
                        P.op("dve", lambda e, bk=bk, g=g, MK3=MK3: e.tensor_tensor(
                            out=Nn[:, 4 * g:4 * g + 4, :], in0=bk[:, :].rearrange("p (h t) -> p h t", h=4),
                            in1=MK3.unsqueeze(1).to_broadcast([128, 4, 128]), op=ALU.mult), reads=[bk_b, cm_b], writes=[Nn_b[g]])
                    yield
                    q0 = 0
                    for g in range(4):
                        P.op("pool", lambda e, g=g: e.tensor_tensor(out=QT[0][:, 4 * g:4 * g + 4, :], in0=NR[:, 4 * g:4 * g + 4, 0, :],
                                                                     in1=ident.unsqueeze(1).to_broadcast([128, 4, 128]), op=ALU.add),
                             reads=[NR_b[2 * g], NR_b[2 * g + 1], ident_b], writes=[QT_b[0][g]])
                    for lev in range(5):
                        sidx = lev % 2
                        SSn = SS[sidx]; SSn_b = SS_b[sidx]
                        if lev == 0:
                            Np = lambda h: Nn[:, h, :]; NTp = lambda h: NR[:, h, 0, :]
                            Np_b = lambda h: [Nn_b[h // 4]]; NTp_b = lambda h: [NR_b[h // 2]]
                        else:
                            SSp = SS[1 - sidx]; SSp_b = SS_b[1 - sidx]
                            Np = lambda h, SSp=SSp: SSp[:, h, 0, :]; NTp = lambda h, SSp=SSp: SSp[:, h, 1, :]
                            Np_b = lambda h, SSp_b=SSp_b: [SSp_b[h // 2]]; NTp_b = Np_b
                        for g in range(8):
                            bk, bk_b = getbank()

                            lastlev = (lev == 4)

                            def mm(e, bk=bk, g=g, Np=Np, NTp=NTp, lastlev=lastlev):
                                ins = None
                                for hh in range(2):
                                    h = 2 * g + hh
                                    ins = e.matmul(bk[:, hh * 256:hh * 256 + 128], NTp(h), Np(h), start=True, stop=True)
                                    if not lastlev:
                                        ins = e.matmul(bk[:, hh * 256 + 128:hh * 256 + 256], Np(h), NTp(h), start=True, stop=True)
                                return ins
                            P.op("pe", mm, reads=Np_b(2 * g) + NTp_b(2 * g) + Np_b(2 * g + 1) + NTp_b(2 * g + 1), writes=[bk_b])
                            if lastlev:
                                P.op("act", lambda e, bk=bk, g=g, SSn=SSn: e.copy(SSn[:, 2 * g:2 * g + 2, 0, :],
                                                                                bk[:, :].rearrange("p (h c t) -> p h c t", h=2, c=2)[:, :, 0, :]),
                                     reads=[bk_b], writes=[SSn_b[g]])
                            else:
                                P.op("act", lambda e, bk=bk, g=g, SSn=SSn: e.copy(SSn[:, 2 * g:2 * g + 2, :, :],
                                                                                bk[:, :].rearrange("p (h c t) -> p h c t", h=2, c=2)),
                                     reads=[bk_b], writes=[SSn_b[g]])
                        yield
                        Qp = QT[q0]; Qn = QT[1 - q0]; Qp_b = QT_b[q0]; Qn_b = QT_b[1 - q0]
                        for g in range(4):
                            bk, bk_b = getbank()

                            def mm(e, bk=bk, g=g, SSn=SSn, Qp=Qp):
                                ins = None
                                for hh in range(4):
                                    h = 4 * g + hh
                                    ins = e.matmul(bk[:, hh * 128:(hh + 1) * 128], SSn[:, h, 0, :], Qp[:, h, :], start=True, stop=True)
                                return ins
                            P.op("pe", mm, reads=[SSn_b[2 * g], SSn_b[2 * g + 1], Qp_b[g]], writes=[bk_b])
                            P.op("dve", lambda e, bk=bk, g=g, Qp=Qp, Qn=Qn: e.tensor_tensor(
                                out=Qn[:, 4 * g:4 * g + 4, :], in0=bk[:, :].rearrange("p (h t) -> p h t", h=4), in1=Qp[:, 4 * g:4 * g + 4, :], op=ALU.add),
                                reads=[bk_b, Qp_b[g]], writes=[Qn_b[g]])
                        q0 = 1 - q0
                    Qf = QT[q0]; Qf_b = QT_b[q0]
                    yield
                    for (kind, dstt, dstt_b) in (("W", Wt, Wt_b), ("BV", BVt, BVt_b), ("nU", nUt, nUt_b)):
                        for g in range(2):
                            bk, bk_b = getbank()

                            def mm(e, bk=bk, g=g, kind=kind, Qf=Qf, KAt=KAt, Vt=Vt):
                                ins = None
                                for hh in range(8):
                                    h = 8 * g + hh
                                    if kind == "W":
                                        ins = e.matmul(bk[:, hh * 64:(hh + 1) * 64], Qf[:, h, :], KAt[:, hcol(h)], start=True, stop=True)
                                    elif kind == "BV":
                                        ins = e.matmul(bk[:, hh * 64:(hh + 1) * 64], BK[:, h, 0, :], Vt[:, hcol(h)], start=True, stop=True)
                                    else:
                                        ins = e.matmul(bk[:, hh * 64:(hh + 1) * 64], Qf[:, h, :], BVt[:, icol(h)], start=True, stop=True)
                                return ins
                            if kind == "W":
                                rd = [Qf_b[2 * g], Qf_b[2 * g + 1], KAt_b]
                            elif kind == "BV":
                                rd = BK_b[4 * g:4 * g + 4] + [Vt_b]
                            else:
                                rd = [Qf_b[2 * g], Qf_b[2 * g + 1], BVt_b[g]]
                            P.op("pe", mm, reads=rd, writes=[bk_b])
                            if kind == "nU":
                                P.op("act", lambda e, bk=bk, g=g, dstt=dstt: e.mul(dstt[:, g * 512:(g + 1) * 512], bk[:, :], -1.0),
                                     reads=[bk_b], writes=[dstt_b[g]])
                            else:
                                P.op("act", lambda e, bk=bk, g=g, dstt=dstt: e.copy(dstt[:, g * 512:(g + 1) * 512], bk[:, :]),
                                     reads=[bk_b], writes=[dstt_b[g]])
                    GT, GT_b = O["GT"]; H, H_b = O["H"]; RhT, RhT_b = O["RhT"]; Yl, Yl_b = O["Yl"]
                    yield
                    for cch in range(2):
                        csl = slice(cch * 64, (cch + 1) * 64)
                        dP = diagP[cch]; dP_b = diagP_b[cch]
                        P.op("pool", lambda e, dP=dP, dr_=dr_, gc=gchunk + cch: e.tensor_tensor(
                            out=dP, in0=id2.unsqueeze(1).to_broadcast([128, 8, 64]),
                            in1=PC[dr_][:, :, gc:gc + 1].to_broadcast([128, 8, 64]), op=ALU.mult),
                            reads=[id2_b, PC_b[dr_]], writes=[dP_b])
                        bk, bk_b = getbank()

                        def mm(e, bk=bk, csl=csl, cch=cch, BHt=BHt):
                            ins = None
                            for h in range(16):
                                j, q = hp(h)
                                ins = e.matmul(bk[qs(q), j * 64:(j + 1) * 64], Wt[csl, icol(h)], BHt[csl, hcol(h)], start=True, stop=True,
                                               tile_position=(cch * 64, q * 64))
                            return ins
                        P.op("pe", mm, reads=Wt_b + [BHt_b], writes=[bk_b])
                        P.op("dve", lambda e, bk=bk, cch=cch, GT=GT, dP=dP: e.tensor_tensor(
                            out=GT[:, cch, :, :], in0=dP, in1=bk[:, :].rearrange("p (j k) -> p j k", j=8), op=ALU.subtract),
                            reads=[bk_b, dP_b], writes=[GT_b[cch]])
                        bk, bk_b = getbank()

                        def mm(e, bk=bk, csl=csl, cch=cch, KHt=KHt, BHt=BHt, Vt=Vt):
                            ins = None
                            for h in range(16):
                                j, q = hp(h)
                                e.matmul(bk[qs(q), j * 64:(j + 1) * 64], KHt[csl, hcol(h)], Vt[csl, hcol(h)], start=True, stop=False,
                                         tile_position=(cch * 64, q * 64))
                                ins = e.matmul(bk[qs(q), j * 64:(j + 1) * 64], BHt[csl, hcol(h)], nUt[csl, icol(h)], start=False, stop=True,
                                               tile_position=(cch * 64, q * 64))
                            return ins
                        P.op("pe", mm, reads=[KHt_b, BHt_b, Vt_b] + nUt_b, writes=[bk_b])
                        P.op("act", lambda e, bk=bk, cch=cch, H=H: e.copy(H[:, cch, :], bk[:, :]), reads=[bk_b], writes=[H_b[cch]])
                    yield
                    for g in range(2):
                        bk, bk_b = getbank()

                        def mm(e, bk=bk, g=g):
                            ins = None
                            for jj in range(4):
                                for q in range(2):
                                    j = 4 * g + jj; h = q * 8 + j
                                    ins = e.matmul(bk[qs(q), jj * 128:(jj + 1) * 128], Wt[:, icol(h)], NR[:, h, 1, :],
                                                   start=True, stop=True, tile_position=(0, q * 64))
                            return ins
                        P.op("pe", mm, reads=Wt_b + NR_b[2 * g:2 * g + 2] + NR_b[4 + 2 * g:4 + 2 * g + 2], writes=[bk_b])
                        P.op("dve", lambda e, bk=bk, g=g, RhT=RhT, KR=KR: e.tensor_tensor(
                            out=RhT[:, 4 * g:4 * g + 4, :], in0=KR[:, 4 * g:4 * g + 4, 1, :], in1=bk[:, :].rearrange("p (j t) -> p j t", j=4),
                            op=ALU.subtract), reads=[bk_b, KR_b], writes=[RhT_b[g]])
                    yield
                    for g in range(2):
                        bk, bk_b = getbank()

                        def mm(e, bk=bk, g=g, Vt=Vt):
                            ins = None
                            for hh in range(8):
                                h = 8 * g + hh
                                e.matmul(bk[:, hh * 64:(hh + 1) * 64], BK[:, h, 1, :], Vt[:, hcol(h)], start=True, stop=False)
                                ins = e.matmul(bk[:, hh * 64:(hh + 1) * 64], NR[:, h, 1, :], nUt[:, icol(h)], start=False, stop=True)
                            return ins
                        P.op("pe", mm, reads=BK_b[4 * g:4 * g + 4] + NR_b[4 * g:4 * g + 4] + [Vt_b, nUt_b[g]], writes=[bk_b])
                        P.op("act", lambda e, bk=bk, g=g, Yl=Yl: e.copy(Yl[:, g * 512:(g + 1) * 512], bk[:, :]), reads=[bk_b], writes=[Yl_b[g]])
                    yield

                ust = {}

                def stageBC(it_):
                    (b, dr_, bi, isfirst) = iters[it_]
                    corder = (0, 1) if dr_ == 0 else (1, 0)
                    osl = it_ % 2; it = it_ + 1
                    O = OUT[osl]
                    tb0 = b * S + bi * 128
                    GT, GT_b = O["GT"]; H, H_b = O["H"]; RhT, RhT_b = O["RhT"]; Yl, Yl_b = O["Yl"]
                    if isfirst:
                        u_cur, u_cur_b = nextU()
                        P.op("pool", lambda e, u_cur=u_cur: e.memset(u_cur, 0.0), writes=[u_cur_b])
                    else:
                        u_cur, u_cur_b = ust["u"]
                    Uc = {}
                    for cch in corder:
                        Uc[cch] = (u_cur, u_cur_b)
                        u_new, u_new_b = nextU()
                        for q in range(2):
                            bk, bk_b = getbank()

                            def mm(e, bk=bk, cch=cch, GT=GT, u_cur=u_cur, q=q):
                                ins = None
                                for j in range(8):
                                    ins = e.matmul(bk[qs(q), j * 64:(j + 1) * 64], GT[qs(q), cch, j, :], u_cur[qs(q), j, :], start=True, stop=True,
                                                   tile_position=(q * 64, q * 64))
                                return ins
                            P.op("pe", mm, reads=[GT_b[cch], u_cur_b], writes=[bk_b])
                            P.op("dve", lambda e, bk=bk, cch=cch, H=H, u_new=u_new, q=q: e.tensor_tensor(
                                out=u_new[qs(q)].rearrange("p j v -> p (j v)"), in0=bk[qs(q), :], in1=H[qs(q), cch, :], op=ALU.add),
                                reads=[bk_b, H_b[cch]], writes=[u_new_b])
                        u_cur, u_cur_b = u_new, u_new_b
                        yield
                    yo_ap, yo_b = yo[it % 2]
                    for g in range(2):
                        bk, bk_b = getbank()

                        def mm(e, bk=bk, g=g, RhT=RhT, Uc=dict(Uc)):
                            ins = None
                            for cch in range(2):
                                uu = Uc[cch][0]
                                for hh in range(8):
                                    j = hh; q = g
                                    ins = e.matmul(bk[cch * 64:(cch + 1) * 64, hh * 64:(hh + 1) * 64], RhT[qs(q), j, cch * 64:(cch + 1) * 64],
                                                   uu[qs(q), j, :], start=True, stop=True, tile_position=(q * 64, cch * 64))
                            return ins
                        P.op("pe", mm, reads=RhT_b + [Uc[0][1], Uc[1][1]], writes=[bk_b])
                        P.op("dve", lambda e, bk=bk, g=g, Yl=Yl, yo_ap=yo_ap: e.tensor_tensor(
                            out=yo_ap.rearrange("p (j q v) -> p j q v", j=8, q=2)[:, :, g, :], in0=bk[:, :].rearrange("p (j v) -> p j v", j=8),
                            in1=Yl[:, g * 512:(g + 1) * 512].rearrange("p (j v) -> p j v", j=8), op=ALU.add),
                            reads=[bk_b, Yl_b[g]], writes=[yo_b[g]])
                    P.dma("pool", [lambda e, yo_ap=yo_ap, tb0=tb0, dr_=dr_: e.dma_start(out=dr["s_y%d" % dr_][tb0:tb0 + 128, :], in_=yo_ap)],
                          reads=yo_b, writes=[dbuf["s_y%d" % dr_]], semkey="yst%d" % (it % 2))
                    ust["u"] = (u_cur, u_cur_b)
                    yield

                def drive2(gens):
                    gens = [g for g in gens if g is not None]
                    while gens:
                        for g in list(gens):
                            try:
                                next(g)
                            except StopIteration:
                                gens.remove(g)

                drive2([stageA(0)])
                for it_ in range(len(iters)):
                    drive2([stageBC(it_), stageA(it_ + 1) if it_ + 1 < len(iters) else None])
        P.barrier()
        A.release(m)

    def post():
        m = A.mark()
        pstage = [(A.alloc(D, F32), Buf("wst%d" % i)) for i in range(2)]
        Wo = A.alloc(KD * D, BF16).rearrange("p (k n) -> p k n", k=KD); Wo_b = [Buf("wo%d" % k) for k in range(KD)]
        wv = dr["w_o"].rearrange("(k p) n -> p k n", p=128)
        for k in range(KD):
            stg, stg_b = pstage[k % 2]
            P.dma("sp", [lambda e, k=k, stg=stg: e.dma_start(out=stg, in_=wv[:, k, :])], writes=[stg_b], semkey="wst%d" % (k % 2))
            P.op("dve" if k % 2 == 0 else "act", (lambda e, k=k, stg=stg: e.tensor_copy(Wo[:, k, :], stg)) if k % 2 == 0 else (lambda e, k=k, stg=stg: e.copy(Wo[:, k, :], stg)),
                 reads=[stg_b], writes=[Wo_b[k]])
        rows = load_rows(["ln_w", "ln_b"])
        SETS = []
        for i in range(2):
            d_ = {}
            for n in ("y0", "y1", "bon", "gg", "tz"):
                d_[n] = (A.alloc(D, F32), Buf("%s_%d" % (n, i)))
            d_["ob"] = (A.alloc(D, BF16), Buf("ob_%d" % i))
            d_["st1"] = (A.alloc(16, F32), Buf("st1_%d" % i))
            d_["st2"] = (A.alloc(16, F32), Buf("st2_%d" % i))
            SETS.append(d_)
        TQ = 512
        oTs = [A.alloc(KD * TQ, BF16).rearrange("p (k n) -> p k n", k=KD) for _ in range(2)]
        oTs_b = [[Buf("oT%d_%d" % (i, b_)) for b_ in range(TQ // 128)] for i in range(2)]
        xts = [A.alloc(KD * TQ, F32).rearrange("p (k n) -> p k n", k=KD) for _ in range(2)]
        xts_b = [Buf("xt0"), Buf("xt1")]

        def blkgen(ti, blk):
            Sx = SETS[blk % 2]
            y0, y0_b = Sx["y0"]; y1, y1_b = Sx["y1"]; bon, bon_b = Sx["bon"]; gg, gg_b = Sx["gg"]; tz, tz_b = Sx["tz"]
            ob, ob_b = Sx["ob"]; st1, st1_b = Sx["st1"]; st2, st2_b = Sx["st2"]
            oT = oTs[ti % 2]; oT_b = oTs_b[ti % 2]
            tb0 = ti * TQ + blk * 128
            for (ap, b_, scr) in ((y0, y0_b, "s_y0"), (y1, y1_b, "s_y1"), (bon, bon_b, "s_bonus"), (gg, gg_b, "s_g")):
                P.dma("sp", [lambda e, ap=ap, scr=scr, tb0=tb0: e.dma_start(out=ap, in_=dr[scr][tb0:tb0 + 128, :])],
                      reads=[dbuf[scr]], writes=[b_], semkey="ld_%s_%d" % (scr, blk % 2))
            yield
            P.op("dve", lambda e: e.tensor_tensor(out=y0, in0=y0, in1=y1, op=ALU.add), reads=[y0_b, y1_b], writes=[y0_b])
            P.op("dve", lambda e: e.tensor_reduce(out=st1, in_=h3(y0), axis=AX.X, op=ALU.add), reads=[y0_b], writes=[st1_b])
            P.op("dve", lambda e: e.tensor_scalar(st1, st1, 1.0 / 64, None, ALU.mult), reads=[st1_b], writes=[st1_b])
            yield
            P.op("dve", lambda e: e.tensor_tensor(out=h3(y0), in0=h3(y0), in1=st1.unsqueeze(2).to_broadcast([128, 16, 64]), op=ALU.subtract),
                 reads=[y0_b, st1_b], writes=[y0_b])
            P.op("act", lambda e: e.activation(out=tz, in_=y0, func=AF.Square), reads=[y0_b], writes=[tz_b])
            P.op("pool", lambda e: e.tensor_tensor(out=bon, in0=bon, in1=rows["ln_b"][0], op=ALU.add), reads=[bon_b, rows["ln_b"][1]], writes=[bon_b])
            yield
            P.op("dve", lambda e: e.tensor_reduce(out=st2, in_=h3(tz), axis=AX.X, op=ALU.add), reads=[tz_b], writes=[st2_b])
            P.op("act", lambda e: e.activation(out=st2, in_=st2, func=AF.Sqrt, bias=64e-5, scale=1.0 / 64), reads=[st2_b], writes=[st2_b])
            P.op("dve", lambda e: e.reciprocal(st2, st2), reads=[st2_b], writes=[st2_b])
            yield
            P.op("dve", lambda e: e.tensor_tensor(out=h3(y0), in0=h3(y0), in1=st2.unsqueeze(2).to_broadcast([128, 16, 64]), op=ALU.mult),
                 reads=[y0_b, st2_b], writes=[y0_b])
            P.op("pool", lambda e: e.tensor_tensor(out=y0, in0=y0, in1=rows["ln_w"][0], op=ALU.mult), reads=[y0_b, rows["ln_w"][1]], writes=[y0_b])
            yield
            P.op("dve", lambda e: e.tensor_tensor(out=y0, in0=y0, in1=bon, op=ALU.add), reads=[y0_b, bon_b], writes=[y0_b])
            P.op("pool", lambda e: e.tensor_tensor(out=ob, in0=y0, in1=gg, op=ALU.mult), reads=[y0_b, gg_b], writes=[ob_b])
            yield
            for half in range(2):
                bk, bk_b = getbank()
                bkb = bk[:, :].bitcast(BF16)

                def tr(e, bkb=bkb, half=half):
                    ins = None
                    for q in range(4):
                        k = half * 4 + q
                        ins = e.transpose(bkb[:, q * 128:(q + 1) * 128], ob[:, k * 128:(k + 1) * 128], ident[:, :])
                    return ins
                P.op("pe", tr, reads=[ob_b, ident_b], writes=[bk_b])
                P.op("act", lambda e, bkb=bkb, half=half: e.copy(
                    oT[:, half * 4:(half + 1) * 4, blk * 128:(blk + 1) * 128], bkb[:, 0:512].rearrange("p (q n) -> p q n", q=4)),
                    reads=[bk_b], writes=[oT_b[blk]])
            yield

        def fingen(ti):
            t0 = ti * TQ
            oT = oTs[ti % 2]; oT_b = oTs_b[ti % 2]
            xt = xts[ti % 2]; xt_b = xts_b[ti % 2]
            P.dma("sp", [lambda e, t0=t0: e.dma_start(out=xt, in_=sv[:, :, t0:t0 + TQ])], reads=[dbuf[src]], writes=[xt_b], semkey="xt%d" % (ti % 2))
            yield
            for do in range(KD):
                bk, bk_b = getbank()

                def mm(e, bk=bk, do=do):
                    ins = None
                    for k in range(KD):
                        ins = e.matmul(bk[:, :], Wo[:, k, do * 128:(do + 1) * 128], oT[:, k, :], start=(k == 0), stop=(k == KD - 1))
                    return ins
                P.op("pe", mm, reads=oT_b + Wo_b, writes=[bk_b])
                P.op("dve", lambda e, bk=bk, do=do: e.tensor_tensor(out=xt[:, do, :], in0=xt[:, do, :], in1=bk[:, :], op=ALU.add),
                     reads=[bk_b, xt_b], writes=[xt_b])
                if do % 2 == 1:
                    yield
            P.dma("pool", [lambda e, t0=t0: e.dma_start(out=dv[:, :, t0:t0 + TQ], in_=xt)], reads=[xt_b], writes=[dbuf[dst]], semkey="xo%d" % (ti % 2))
            yield

        def drive3(gens):
            gens = [g for g in gens if g is not None]
            while gens:
                for g in list(gens):
                    try:
                        next(g)
                    except StopIteration:
                        gens.remove(g)

        prev_fin = None
        for ti in range(T // TQ):
            drive3([blkgen(ti, 0), blkgen(ti, 1), prev_fin])
            drive3([blkgen(ti, 2), blkgen(ti, 3)])
            prev_fin = fingen(ti)
        drive3([prev_fin])
        P.barrier()
        A.release(m)

    if "prep" in sub:
        prep()
    if "scan" in sub:
        scan()
    if "post" in sub:
        post()


F32 = mybir.dt.float32
BF16 = mybir.dt.bfloat16
ALU = mybir.AluOpType
AF = mybir.ActivationFunctionType

D = 1024; KD = 8; FF = 2816; KF = 22; T = 4096; S = 2048; TT = 512; NTT = T // TT
ARENA_WORDS = 52800
VEC_NAMES = ["nm0", "nm1", "nf0", "nf1", "nfin", "mu0", "mu1", "mu2", "mu3", "mu4", "mu5",
             "w0_0", "w0_1", "a0_0", "a0_1", "k_k", "k_a", "r_k", "ln_w", "ln_b"]
VI = {n: i for i, n in enumerate(VEC_NAMES)}
NV = len(VEC_NAMES)


class Ctx:
    pass


def build(phases=("l0mix", "ffn0", "l1mix", "ffn1", "final"), debug=False, rwkv_sub=("prep", "scan", "post"), dbg_scr=False):
    nc = bass.Bass("TRN2", target_bir_lowering=False)
    st = ExitStack()
    P = Prog(nc, st)
    dr = {}

    drh = {}

    def din(name, shape, dt=F32):
        drh[name] = nc.dram_tensor(name, list(shape), dt, kind="ExternalInput")
        dr[name] = drh[name].ap()

    def dscr(name, shape, dt=F32):
        kind = "ExternalOutput" if debug else "Internal"
        dr[name] = nc.dram_tensor(name, list(shape), dt, kind=kind).ap()

    din("xT", [D, T]); din("vecs", [128, NV * 8])
    din("fno_w", [D, D])
    din("wg", [2, D, FF]); din("wu", [2, D, FF]); din("wd", [2, FF, D])
    din("cs1", [128, 2, 512], BF16); din("cs2", [4, 128, 16, 2, 512], BF16)
    dr["outT"] = nc.dram_tensor("outT", [D, T], F32, kind="ExternalOutput").ap()
    dscr("xa", [D, T]); dscr("xb", [D, T])
    if "l1mix" in phases:
        declare(dr, drh, nc, din, dbg_scr)

    arena_t = st.enter_context(nc.sbuf_tensor("arena", [128, ARENA_WORDS], F32))
    A = Arena(arena_t, ARENA_WORDS)
    banks = []
    for i in range(8):
        pt = st.enter_context(nc.psum_tensor("bank%d" % i, [128, 512], F32))
        banks.append((pt, Buf("bank%d" % i)))
    bank_rr = [0]

    def getbank():
        b = banks[bank_rr[0] % 8]
        bank_rr[0] += 1
        return b

    vecs = A.alloc(NV * 8, F32); vecs_b = Buf("vecs")
    P.dma("sp", [lambda e: e.dma_start(out=vecs, in_=dr["vecs"])], writes=[vecs_b], semkey="vecs")
    onesD = A.alloc(128, BF16); ones_b = Buf("ones")
    P.op("pool", lambda e: e.memset(onesD, 1.0 / D), writes=[ones_b])
    persist_mark = A.mark()

    def vcol(name, k):
        i = VI[name] * 8 + k
        return vecs[:, i:i + 1]

    def dview(name):
        return dr[name].rearrange("(k p) t -> p k t", p=128)

    rr = {"cast": 0}

    def load_w_gen(w2d, K, N, tag, stage, stage_bufs, out):
        dst = A.alloc(K * N, BF16).rearrange("p (k n) -> p k n", k=K)
        bufs = [Buf("%s_k%d" % (tag, k)) for k in range(K)]
        out.append((dst, bufs))
        wv = w2d.rearrange("(k p) n -> p k n", p=128)
        for k in range(K):
            s = rr["cast"] % len(stage); rr["cast"] += 1
            sap = stage[s][:, 0:N]
            P.dma("sp", [lambda e, sap=sap, k=k: e.dma_start(out=sap, in_=wv[:, k, :])],
                  writes=stage_bufs[s], semkey="wst%d" % s)
            eng = "dve" if (k % 2 == 0) else "act"
            if eng == "dve":
                P.op("dve", lambda e, sap=sap, k=k: e.tensor_copy(dst[:, k, :], sap),
                     reads=stage_bufs[s], writes=[bufs[k]])
            else:
                P.op("act", lambda e, sap=sap, k=k: e.copy(dst[:, k, :], sap),
                     reads=stage_bufs[s], writes=[bufs[k]])
            yield

    def load_w_bf16(w2d, K, N, tag, stage, stage_bufs):
        out = []
        sb = [b if isinstance(b, list) else [b] for b in stage_bufs]
        for _ in load_w_gen(w2d, K, N, tag, stage, sb, out):
            pass
        return out[0]

    def drive(gens):
        gens = [g for g in gens if g is not None]
        while gens:
            for g in list(gens):
                try:
                    next(g)
                except StopIteration:
                    gens.remove(g)

    def rmsnorm(xt, xt_b, gname, hT, hT_b, sq, sq_b, rstd, rstd_b, n=TT):
        P.op("pool", lambda e: e.tensor_tensor(out=sq, in0=xt, in1=xt, op=ALU.mult), reads=[xt_b], writes=sq_b)
        bk, bk_b = getbank()

        def mm(e):
            ins = None
            for k in range(KD):
                ins = e.matmul(bk[:, 0:n], onesD[:, :], sq[:, k, :], start=(k == 0), stop=(k == KD - 1))
            return ins
        P.op("pe", mm, reads=sq_b + [ones_b], writes=[bk_b])
        P.op("act", lambda e: e.activation(out=rstd, in_=bk[:, 0:n], func=AF.Sqrt, bias=1e-6, scale=1.0),
             reads=[bk_b], writes=[rstd_b])
        P.op("dve", lambda e: e.reciprocal(rstd, rstd), reads=[rstd_b], writes=[rstd_b])
        for k in range(KD):
            eng = "dve"
            P.op(eng, lambda e, k=k: e.scalar_tensor_tensor(out=hT[:, k, :], in0=xt[:, k, :], scalar=vcol(gname, k),
                                                          in1=rstd, op0=ALU.mult, op1=ALU.mult),
                 reads=[xt_b, rstd_b, vecs_b], writes=[hT_b[k]])

    def phase_fourier(src, dst):
        m = A.mark()
        stage = [A.alloc(D, F32) for _ in range(2)]
        stage_bufs = [Buf("wst0"), Buf("wst1")]
        Wf, Wf_b = load_w_bf16(dr["fno_w"], KD, D, "wf", stage, stage_bufs)
        cs1 = A.alloc(2 * 512, BF16).rearrange("p (k n) -> p k n", k=2); cs1_b = Buf("cs1")
        P.dma("sp", [lambda e: e.dma_start(out=cs1, in_=dr["cs1"])], writes=[cs1_b], semkey="cs1")
        cs2 = [A.alloc(16 * 2 * 512, BF16).rearrange("p (s c n) -> p s c n", s=16, c=2) for _ in range(2)]
        cs2_b = [Buf("cs2_0"), Buf("cs2_1")]
        AB = A.alloc(2 * 16 * D, BF16).rearrange("p (c s d) -> p c s d", c=2, s=16)
        AB_b = [[Buf("AB%d_%d" % (i, j)) for j in range(4)] for i in range(16)]
        AB_all = [b_ for l_ in AB_b for b_ in l_]
        xt = [A.alloc(KD * TT, F32).rearrange("p (k n) -> p k n", k=KD) for _ in range(2)]
        xt_b = [Buf("xt0"), Buf("xt1")]
        hT = A.alloc(KD * TT, BF16).rearrange("p (k n) -> p k n", k=KD); hT_b = [Buf("hT%d" % i) for i in range(KD)]
        sq = A.alloc(KD * TT, BF16).rearrange("p (k n) -> p k n", k=KD); sq_b = [Buf("sq%d" % i) for i in range(KD)]
        rstd = A.alloc(TT, F32); rstd_b = Buf("rstd")
        fT = sq; fT_b = sq_b
        sv = dview(src); dv = dview(dst)
        ev = [0]
        for b in range(2):
            for tt in range(4):
                t0 = b * S + tt * TT
                x_ = xt[tt % 2]; x_b = xt_b[tt % 2]
                P.dma("sp", [lambda e, x_=x_, t0=t0: e.dma_start(out=x_, in_=sv[:, :, t0:t0 + TT])],
                      reads=[dbuf[src]], writes=[x_b], semkey="xt%d" % (tt % 2))
                rmsnorm(x_, x_b, "nm0", hT, hT_b, sq, sq_b, rstd, rstd_b)
                for blk in range(4):
                    sc = tt * 4 + blk
                    for g in range(4):
                        bk, bk_b = getbank()

                        def mm(e, bk=bk, blk=blk, g=g):
                            ins = None
                            for kk in range(2):
                                ins = e.matmul(bk[:, :], hT[:, 2 * g + kk, blk * 128:(blk + 1) * 128], cs1[:, kk, :],
                                               start=(kk == 0), stop=(kk == 1))
                            return ins
                        P.op("pe", mm, reads=hT_b + [cs1_b], writes=[bk_b])
                        eng = "act" if ev[0] % 2 == 0 else "dve"; ev[0] += 1
                        outv = AB[:, :, sc, g * 256:(g + 1) * 256]
                        inv = bk[:, :].rearrange("p (c n) -> p c n", c=2)
                        if eng == "act":
                            P.op("act", lambda e, outv=outv, inv=inv: e.copy(outv, inv), reads=[bk_b], writes=[AB_b[sc][g]])
                        else:
                            P.op("dve", lambda e, outv=outv, inv=inv: e.tensor_copy(outv, inv), reads=[bk_b], writes=[AB_b[sc][g]])
            for stl in range(4):
                c2 = cs2[stl % 2]; c2_b = cs2_b[stl % 2]
                P.dma("sp", [lambda e, c2=c2, stl=stl: e.dma_start(out=c2, in_=dr["cs2"][stl])],
                      writes=[c2_b], semkey="cs2_%d" % (stl % 2))
                t0 = b * S + stl * TT
                x_ = xt[stl % 2]; x_b = xt_b[stl % 2]
                P.dma("sp", [lambda e, x_=x_, t0=t0: e.dma_start(out=x_, in_=sv[:, :, t0:t0 + TT])],
                      reads=[dbuf[src]], writes=[x_b], semkey="xt%d" % (stl % 2))
                for dc in range(KD):
                    bk, bk_b = getbank()

                    def mm(e, bk=bk, dc=dc, c2=c2):
                        ins = None
                        for sc in range(16):
                            for c in range(2):
                                ins = e.matmul(bk[:, :], AB[:, c, sc, dc * 128:(dc + 1) * 128], c2[:, sc, c, :],
                                               start=(sc == 0 and c == 0), stop=(sc == 15 and c == 1))
                        return ins
                    P.op("pe", mm, reads=AB_all + [c2_b], writes=[bk_b])
                    if dc % 2 == 0:
                        P.op("act", lambda e, bk=bk, dc=dc: e.copy(fT[:, dc, :], bk[:, :]), reads=[bk_b], writes=[fT_b[dc]])
                    else:
                        P.op("dve", lambda e, bk=bk, dc=dc: e.tensor_copy(fT[:, dc, :], bk[:, :]), reads=[bk_b], writes=[fT_b[dc]])
                for do in range(KD):
                    bk, bk_b = getbank()

                    def mm(e, bk=bk, do=do):
                        ins = None
                        for k in range(KD):
                            ins = e.matmul(bk[:, :], Wf[:, k, do * 128:(do + 1) * 128], fT[:, k, :],
                                           start=(k == 0), stop=(k == KD - 1))
                        return ins
                    P.op("pe", mm, reads=fT_b + Wf_b, writes=[bk_b])
                    P.op("dve", lambda e, bk=bk, do=do, x_=x_: e.tensor_tensor(out=x_[:, do, :], in0=x_[:, do, :], in1=bk[:, :], op=ALU.add),
                         reads=[bk_b, x_b], writes=[x_b])
                P.dma("act", [lambda e, x_=x_, t0=t0: e.dma_start(out=dv[:, :, t0:t0 + TT], in_=x_)],
                      reads=[x_b], writes=[dbuf[dst]], semkey="xo%d" % (stl % 2))
        P.barrier()
        A.release(m)

    def phase_ffn(layer, src, dst):
        m = A.mark()
        xt_flat = A.alloc(KD * TT, F32)
        xt = [xt_flat.rearrange("p (k n) -> p k n", k=KD)]
        xt_b = [Buf("xt0")]
        hT = A.alloc(KD * TT, BF16).rearrange("p (k n) -> p k n", k=KD); hT_b = [Buf("hT%d" % i) for i in range(KD)]
        rstd = A.alloc(TT, F32); rstd_b = Buf("rstd")
        actT_w = A.alloc(KF * TT // 2, F32)
        actT = actT_w.bitcast(BF16).rearrange("p (k n) -> p k n", k=KF)
        act_b = [Buf("act%d" % i) for i in range(KF)]
        sq = actT[:, 0:KD, :]; sq_b = act_b[0:KD]
        sg = [A.alloc(TT, F32) for _ in range(2)]; sg_b = [Buf("sg0"), Buf("sg1")]
        st0 = A.alloc(FF, F32); st0_b = [Buf("wst0a"), Buf("wst0b")]
        HW_ = FF // 2
        stage4 = [st0, xt_flat[:, 0:FF], actT_w[:, 0:FF], actT_w[:, FF:2 * FF]]
        stage4_b = [st0_b, [xt_b[0]], act_b[0:11], act_b[11:22]]
        Wg, Wg_b = load_w_bf16(dr["wg"][layer], KD, FF, "wg", stage4, stage4_b)
        Wu, Wu_b = load_w_bf16(dr["wu"][layer], KD, FF, "wu", stage4, stage4_b)
        stageD = [st0[:, 0:D], st0[:, HW_:HW_ + D]]
        stageD_b = [[st0_b[0]], [st0_b[1]]]
        wd_out = []
        wd_gen = load_w_gen(dr["wd"][layer], KF, D, "wd", stageD, stageD_b, wd_out)
        next(wd_gen)
        Wd, Wd_b = wd_out[0]
        sv = dview(src); dv = dview(dst)
        gname = "nf%d" % layer

        def tile_gen(tt):
            t0 = tt * TT
            x_ = xt[0]; x_b = xt_b[0]
            P.dma("sp", [lambda e, x_=x_, t0=t0: e.dma_start(out=x_, in_=sv[:, :, t0:t0 + TT])],
                  reads=[dbuf[src]], writes=[x_b], semkey="xt0")
            rmsnorm(x_, x_b, gname, hT, hT_b, sq, sq_b, rstd, rstd_b)
            yield
            for fc in range(KF):
                bg, bg_b = getbank()
                bu, bu_b = getbank()

                def mmg(e, bk=bg, fc=fc):
                    ins = None
                    for k in range(KD):
                        ins = e.matmul(bk[:, :], Wg[:, k, fc * 128:(fc + 1) * 128], hT[:, k, :], start=(k == 0), stop=(k == KD - 1))
                    return ins

                def mmu(e, bk=bu, fc=fc):
                    ins = None
                    for k in range(KD):
                        ins = e.matmul(bk[:, :], Wu[:, k, fc * 128:(fc + 1) * 128], hT[:, k, :], start=(k == 0), stop=(k == KD - 1))
                    return ins
                P.op("pe", mmg, reads=hT_b + Wg_b, writes=[bg_b])
                P.op("pe", mmu, reads=hT_b + Wu_b, writes=[bu_b])
                s_ = sg[fc % 2]; s_b = sg_b[fc % 2]
                P.op("act", lambda e, s_=s_, bg=bg: e.activation(out=s_, in_=bg[:, :], func=AF.Silu), reads=[bg_b], writes=[s_b])
                P.op("dve", lambda e, s_=s_, bu=bu, fc=fc: e.tensor_tensor(out=actT[:, fc, :], in0=s_, in1=bu[:, :], op=ALU.mult),
                     reads=[s_b, bu_b], writes=[act_b[fc]])
                yield
            for do in range(KD):
                bk, bk_b = getbank()

                def mm(e, bk=bk, do=do):
                    ins = None
                    for fc in range(KF):
                        ins = e.matmul(bk[:, :], Wd[:, fc, do * 128:(do + 1) * 128], actT[:, fc, :], start=(fc == 0), stop=(fc == KF - 1))
                    return ins
                P.op("pe", mm, reads=act_b + Wd_b, writes=[bk_b])
                P.op("dve", lambda e, bk=bk, do=do, x_=x_: e.tensor_tensor(out=x_[:, do, :], in0=x_[:, do, :], in1=bk[:, :], op=ALU.add),
                     reads=[bk_b, x_b], writes=[x_b])
            P.dma("act", [lambda e, x_=x_, t0=t0: e.dma_start(out=dv[:, :, t0:t0 + TT], in_=x_)],
                  reads=[x_b], writes=[dbuf[dst]], semkey="xo0")
            yield

        drive([wd_gen, tile_gen(0)])
        for tt in range(1, NTT):
            drive([tile_gen(tt)])
        P.barrier()
        A.release(m)

    def phase_final(src):
        m = A.mark()
        xt = [A.alloc(KD * TT, F32).rearrange("p (k n) -> p k n", k=KD) for _ in range(2)]
        xt_b = [Buf("xt0"), Buf("xt1")]
        ot = [A.alloc(KD * TT, F32).rearrange("p (k n) -> p k n", k=KD) for _ in range(2)]
        ot_b = [[Buf("ot0_%d" % i) for i in range(KD)], [Buf("ot1_%d" % i) for i in range(KD)]]
        sq = A.alloc(KD * TT, BF16).rearrange("p (k n) -> p k n", k=KD); sq_b = [Buf("sq%d" % i) for i in range(KD)]
        rstd = A.alloc(TT, F32); rstd_b = Buf("rstd")
        sv = dview(src); dv = dview("outT")
        for tt in range(NTT):
            t0 = tt * TT
            x_ = xt[tt % 2]; x_b = xt_b[tt % 2]
            P.dma("sp", [lambda e, x_=x_, t0=t0: e.dma_start(out=x_, in_=sv[:, :, t0:t0 + TT])],
                  reads=[dbuf[src]], writes=[x_b], semkey="xt%d" % (tt % 2))
            rmsnorm(x_, x_b, "nfin", ot[tt % 2], ot_b[tt % 2], sq, sq_b, rstd, rstd_b)
            P.dma("act", [lambda e, o_=ot[tt % 2], t0=t0: e.dma_start(out=dv[:, :, t0:t0 + TT], in_=o_)],
                  reads=ot_b[tt % 2], writes=[dbuf["outT"]], semkey="xo%d" % (tt % 2))
        P.barrier()
        A.release(m)

    dbuf = {n: Buf("dram_" + n) for n in ("xT", "xa", "xb", "outT")}
    cur = "xT"
    ctx = Ctx()
    ctx.__dict__.update(locals())
    if "l0mix" in phases:
        phase_fourier(cur, "xa"); cur = "xa"
    if "ffn0" in phases:
        phase_ffn(0, cur, "xb"); cur = "xb"
    if "l1mix" in phases:
        nxt = "xb" if cur == "xa" else "xa"
        phase_rwkv(ctx, cur, nxt, sub=rwkv_sub); cur = nxt
    if "ffn1" in phases:
        nxt = "xb" if cur == "xa" else "xa"
        phase_ffn(1, cur, nxt); cur = nxt
    if "final" in phases:
        phase_final(cur)
    P.barrier()
    stats = P.finalize()
    st.close()
    return nc, stats


def host_consts():
    c = np.arange(256)
    ang1 = 2 * np.pi * ((c[:, None] * c[None, :]) % 256) / 256.0
    cs1 = np.concatenate([np.cos(ang1), np.sin(ang1)], axis=1) / 16.0
    cs1 = cs1.reshape(2, 128, 512).transpose(1, 0, 2)
    s = np.arange(S)
    ang2 = 2 * np.pi * ((s[:, None] * s[None, :]) % S) / float(S)
    C2 = np.cos(ang2) / np.sqrt(S); S2 = -np.sin(ang2) / np.sqrt(S)
    cs2 = np.stack([C2, S2], axis=1)
    cs2 = cs2.reshape(16, 128, 2, 4, 512).transpose(3, 1, 0, 2, 4)
    return (np.ascontiguousarray(cs1).astype(ml_dtypes.bfloat16),
            np.ascontiguousarray(cs2).astype(ml_dtypes.bfloat16))


def pack_vecs(inp):
    vs = {
        "nm0": inp["norm_mix_g"][0], "nm1": inp["norm_mix_g"][1],
        "nf0": inp["norm_ffn_g"][0], "nf1": inp["norm_ffn_g"][1], "nfin": inp["norm_final_g"],
        "w0_0": inp["rwkv_w0"][0, 0], "w0_1": inp["rwkv_w0"][0, 1],
        "a0_0": inp["rwkv_a0"][0, 0], "a0_1": inp["rwkv_a0"][0, 1],
        "k_k": inp["rwkv_k_k"][0], "k_a": inp["rwkv_k_a"][0], "r_k": inp["rwkv_r_k"][0].reshape(-1),
        "ln_w": inp["rwkv_ln_w"][0], "ln_b": inp["rwkv_ln_b"][0],
    }
    for i in range(6):
        vs["mu%d" % i] = inp["rwkv_mu"][0, i]
    out = np.zeros((128, NV * 8), np.float32)
    for n, i in VI.items():
        out[:, i * 8:(i + 1) * 8] = np.asarray(vs[n], np.float32).reshape(8, 128).T
    return out


def make_consts(inp):
    cs1, cs2 = host_consts()
    rows = np.stack([np.asarray(x, np.float32).reshape(-1) for x in (
        inp["rwkv_k_k"][0], inp["rwkv_k_a"][0], inp["rwkv_w0"][0, 0], inp["rwkv_w0"][0, 1], inp["rwkv_a0"][0, 0],
        inp["rwkv_a0"][0, 1], inp["rwkv_r_k"][0], inp["rwkv_ln_w"][0], inp["rwkv_ln_b"][0])])
    rc = rwkv_consts()
    return dict(cmask=rc["cmask"], ctri=rc["ctri"], cind=rc["cind"], cid2=rc["cid2"], cs1=cs1, cs2=cs2, rows=rows,
                ident=np.eye(128, dtype=np.float32).astype(ml_dtypes.bfloat16), vecs=pack_vecs(inp))


def make_inmap(inp, core, c=None):
    if c is None:
        c = make_consts(inp)
    x = inp["x"][2 * core:2 * core + 2]
    return {"xT": np.ascontiguousarray(x.reshape(T, D).T), "vecs": c["vecs"],
            "fno_w": inp["fno_w_out"][0], "wg": inp["ffn_w_gate"], "wu": inp["ffn_w_up"], "wd": inp["ffn_w_down"],
            "cs1": c["cs1"], "cs2": c["cs2"],
            "w_rkv": inp["rwkv_w_rkv"][0], "w_o": inp["rwkv_w_o"][0], "w1": inp["rwkv_w1"][0], "w2": inp["rwkv_w2"][0],
            "a1": inp["rwkv_a1"][0], "a2": inp["rwkv_a2"][0], "g1": inp["rwkv_g1"][0], "g2": inp["rwkv_g2"][0],
            "rows": c["rows"], "ident": c["ident"], "cmask": c["cmask"], "ctri": c["ctri"], "cind": c["cind"], "cid2": c["cid2"]}


def kernel(**inputs):
    inp = {k: np.asarray(v) for k, v in inputs.items()}
    nc, _ = build()
    consts = make_consts(inp)
    in_maps = [make_inmap(inp, c, consts) for c in range(8)]
    res = run_bass_kernel_spmd(nc, in_maps, core_ids=list(range(8)))
    out = np.empty((16, S, D), np.float32)
    for c in range(8):
        out[2 * c:2 * c + 2] = np.asarray(res.results[c]["outT"]).T.reshape(2, S, D)
    return out
```
